# Optimizing a Trainium2 kernel written in Bass

```python
import jax, jax.numpy as jnp
from jax import lax
import numpy as np

D_MODEL = 2048
BATCH = 4
SEQ = 2048
DEPTH = 4
DEC_BATCH = 8
DEC_SEQ = 1
PAST_LEN = 16384
PAGE_SIZE = 128

D_MIX = D_MODEL
C_CONV = D_MIX // 2
N_HEADS = 16
HEAD_DIM = (D_MIX - C_CONV) // N_HEADS
N_KV = 4
GROUP = N_HEADS // N_KV
HD = N_HEADS * HEAD_DIM
KVD = N_KV * HEAD_DIM
CONV_W = 31
CMP_BLOCK = 64
SEL_BLOCK = 64
TOP_N = 16
WINDOW = 512
CMP_HIDDEN = 2 * HEAD_DIM
ROPE_DIM = HEAD_DIM // 4
ROPE_THETA = 500000.0
Q_BLOCK = 64
EPS = 1e-6
BIG = 1e9
NEG = -1e30
N_IN = 3 * C_CONV + 2 * HD + 6 * KVD + 3 * N_HEADS

kernel_name = "hymba_conformer_nsa_decode_step"


def _rmsnorm(x, g):
    xf = x.astype(jnp.float32)
    y = xf * lax.rsqrt(jnp.mean(xf * xf, axis=-1, keepdims=True) + EPS)
    return y.astype(x.dtype) * g


def _layernorm(x, g, b):
    xf = x.astype(jnp.float32)
    mu = jnp.mean(xf, axis=-1, keepdims=True)
    var = jnp.mean(jnp.square(xf - mu), axis=-1, keepdims=True)
    return ((xf - mu) * lax.rsqrt(var + EPS)).astype(x.dtype) * g + b


def _rope(x, pos):
    half = ROPE_DIM // 2
    inv = ROPE_THETA ** (-jnp.arange(half, dtype=jnp.float32) / half)
    ang = pos.astype(jnp.float32)[:, None] * inv[None, :]
    cos = jnp.cos(ang)[None, :, None, :]
    sin = jnp.sin(ang)[None, :, None, :]
    xr = x[..., :ROPE_DIM].astype(jnp.float32)
    x1, x2 = xr[..., :half], xr[..., half:]
    rot = jnp.concatenate([x1 * cos - x2 * sin, x2 * cos + x1 * sin], axis=-1)
    return jnp.concatenate([rot.astype(x.dtype), x[..., ROPE_DIM:]], axis=-1)


def _masked_softmax(s, mask):
    s = jnp.where(mask, s.astype(jnp.float32), NEG)
    m = jnp.max(s, axis=-1, keepdims=True)
    e = jnp.where(mask, jnp.exp(s - m), 0.0)
    return e / jnp.maximum(jnp.sum(e, axis=-1, keepdims=True), 1e-30)


def _nsa(q, q_rot, gates, kv_all, win_ext, q0, cmp_pos, cmp_w1, cmp_b1, cmp_w2, cmp_b2):
    B, T = q.shape[0], q.shape[1]
    L = kv_all.shape[1]
    scale = HEAD_DIM ** -0.5
    nbc = L // CMP_BLOCK
    blocks = kv_all[:, :nbc * CMP_BLOCK, :2].reshape(B, nbc, CMP_BLOCK, 2, N_KV, HEAD_DIM)
    blocks = blocks + cmp_pos.transpose(1, 0, 2)[None, None, :, :, None, :]
    flat = blocks.transpose(0, 1, 3, 4, 2, 5).reshape(B, nbc, 2, N_KV, CMP_BLOCK * HEAD_DIM)
    hid = jax.nn.silu(jnp.einsum('bnckf,cfe->bncke', flat, cmp_w1) + cmp_b1[:, None, :])
    comp = jnp.einsum('bncke,ced->bnckd', hid, cmp_w2) + cmp_b2[:, None, :]
    k_cmp, v_cmp = comp[:, :, 0], comp[:, :, 1]
    nbs = -(-L // SEL_BLOCK)
    sel = jnp.pad(kv_all[:, :, 2:], ((0, 0), (0, nbs * SEL_BLOCK - L), (0, 0), (0, 0), (0, 0)))
    sel = sel.reshape(B, nbs, SEL_BLOCK, 2, N_KV, HEAD_DIM).transpose(3, 0, 4, 1, 2, 5)
    k_sel_b, v_sel_b = sel[0], sel[1]
    k_top = min(TOP_N, nbs)
    qb = Q_BLOCK if T % Q_BLOCK == 0 else T
    nqb = T // qb
    qg = q.reshape(B, T, N_KV, GROUP, HEAD_DIM)
    qrg = q_rot.reshape(B, T, N_KV, GROUP, HEAD_DIM)
    gg = gates.reshape(B, T, N_KV, GROUP, 3)
    blk_ids = jnp.arange(nbs, dtype=jnp.int32)
    cmp_end = (jnp.arange(nbc, dtype=jnp.int32) + 1) * CMP_BLOCK - 1
    gather = jax.vmap(jax.vmap(lambda kb, i: kb[i]))

    def one_block(i):
        qs = i * qb
        qc = lax.dynamic_slice_in_dim(qg, qs, qb, axis=1)
        qr = lax.dynamic_slice_in_dim(qrg, qs, qb, axis=1)
        g = lax.dynamic_slice_in_dim(gg, qs, qb, axis=1)
        tpos = q0 + qs + jnp.arange(qb, dtype=jnp.int32)
        s_c = jnp.einsum('bqkgd,bnkd->bkgqn', qc, k_cmp) * scale
        p_c = _masked_softmax(s_c, cmp_end[None, :] <= tpos[:, None])
        o_c = jnp.einsum('bkgqn,bnkd->bqkgd', p_c.astype(v_cmp.dtype), v_cmp)
        imp = jnp.pad(jnp.sum(p_c, axis=2), ((0, 0), (0, 0), (0, 0), (0, nbs - nbc)))
        cur = (tpos // SEL_BLOCK)[:, None]
        forced = (blk_ids == 0) | (blk_ids == cur) | (blk_ids == cur - 1)
        score = jnp.where(blk_ids <= cur, jnp.where(forced, BIG, imp), -BIG)
        top_val, top_idx = lax.top_k(score, k_top)
        k_sel = gather(k_sel_b, top_idx).reshape(B, N_KV, qb, k_top * SEL_BLOCK, HEAD_DIM)
        v_sel = gather(v_sel_b, top_idx).reshape(B, N_KV, qb, k_top * SEL_BLOCK, HEAD_DIM)
        kpos = top_idx[..., None] * SEL_BLOCK + jnp.arange(SEL_BLOCK, dtype=jnp.int32)
        m_s = (top_val[..., None] > -0.5 * BIG) & (kpos <= tpos[None, None, :, None, None])
        m_s = m_s.reshape(B, N_KV, qb, k_top * SEL_BLOCK)
        s_s = jnp.einsum('bqkgd,bkqmd->bkgqm', qr, k_sel) * scale
        p_s = _masked_softmax(s_s, m_s[:, :, None])
        o_s = jnp.einsum('bkgqm,bkqmd->bqkgd', p_s.astype(v_sel.dtype), v_sel)
        wk = lax.dynamic_slice_in_dim(win_ext, qs, WINDOW + qb, axis=1)
        wpos = q0 - WINDOW + qs + jnp.arange(WINDOW + qb, dtype=jnp.int32)
        dlt = tpos[:, None] - wpos[None, :]
        m_w = (dlt >= 0) & (dlt < WINDOW) & (wpos[None, :] >= 0)
        s_w = jnp.einsum('bqkgd,bmkd->bkgqm', qr, wk[:, :, 0]) * scale
        p_w = _masked_softmax(s_w, m_w)
        o_w = jnp.einsum('bkgqm,bmkd->bqkgd', p_w.astype(wk.dtype), wk[:, :, 1])
        return g[..., 0:1] * o_c + g[..., 1:2] * o_s + g[..., 2:3] * o_w

    out = lax.map(one_block, jnp.arange(nqb, dtype=jnp.int32))
    return out.transpose(1, 0, 2, 3, 4, 5).reshape(B, T, HD)


def _layer(x, q0, kv_past, win_buf, conv_buf, n_keep, w_in, w_out, g_pre, g_post,
           conv_w, conv_b, ln_g, ln_b, cmp_pos, cmp_w1, cmp_b1, cmp_w2, cmp_b2, gate_b):
    B, T, _ = x.shape
    pos = q0 + jnp.arange(T, dtype=jnp.int32)
    u = _rmsnorm(x, g_pre)
    proj = u @ w_in
    cuts = [C_CONV, 2 * C_CONV, 3 * C_CONV, 3 * C_CONV + HD, 3 * C_CONV + 2 * HD,
            3 * C_CONV + 2 * HD + 6 * KVD]
    a_val, a_gate, z_conv, q, z_attn, kv6, gl = jnp.split(proj, cuts, axis=-1)
    glu = a_val * jax.nn.sigmoid(a_gate)
    ap = jnp.concatenate([conv_buf, glu], axis=1)
    c = lax.conv_general_dilated(ap, conv_w[:, None, :], (1,), 'VALID',
                                 dimension_numbers=('NWC', 'WIO', 'NWC'),
                                 feature_group_count=C_CONV) + conv_b
    y_conv = jax.nn.silu(_layernorm(c, ln_g, ln_b)) * jax.nn.silu(z_conv)
    conv_new = ap[:, ap.shape[1] - (CONV_W - 1):]
    q = q.reshape(B, T, N_HEADS, HEAD_DIM)
    kc, vc, ks, vs, kw, vw = [t.reshape(B, T, N_KV, HEAD_DIM) for t in jnp.split(kv6, 6, axis=-1)]
    q_rot = _rope(q, pos)
    ks = _rope(ks, pos)
    kw = _rope(kw, pos)
    gates = jax.nn.sigmoid(gl + gate_b).reshape(B, T, N_HEADS, 3)
    kv_new = jnp.stack([kc, vc, ks, vs], axis=2)
    kv_all = jnp.concatenate([kv_past, kv_new], axis=1)
    win_rows = jnp.stack([kw, vw], axis=2)
    pad_rows = jnp.zeros((B, WINDOW - win_buf.shape[1], 2, N_KV, HEAD_DIM), x.dtype)
    win_ext = jnp.concatenate([pad_rows, win_buf, win_rows], axis=1)
    win_new = win_ext[:, win_ext.shape[1] - n_keep:]
    o = _nsa(q, q_rot, gates, kv_all, win_ext, q0, cmp_pos, cmp_w1, cmp_b1, cmp_w2, cmp_b2)
    mix = jnp.concatenate([y_conv, o * jax.nn.silu(z_attn)], axis=-1)
    y = x + _rmsnorm(mix @ w_out, g_post)
    return y, kv_new, win_new, conv_new


def setup_inputs(seed: int = 0) -> dict:
    key = jax.random.key(seed)
    ks = jax.random.split(key, 20)
    n_pages = PAST_LEN // PAGE_SIZE
    n_used = DEC_BATCH * n_pages
    n_pool = n_used + max(1, n_used // 4)
    w_buf = min(WINDOW, PAST_LEN)
    f = jnp.float32
    nrm = lambda k, s, sc: jax.random.normal(k, s, f) * sc
    page_table = jax.random.permutation(ks[5], n_pool)[:n_used].reshape(DEC_BATCH, n_pages).astype(jnp.int32)
    return {
        "x_prompt": nrm(ks[0], (BATCH, SEQ, D_MODEL), 1.0),
        "x_sample": nrm(ks[1], (DEC_BATCH, DEC_SEQ, D_MODEL), 1.0),
        "cache_kv_pages": nrm(ks[2], (DEPTH, n_pool, PAGE_SIZE, 4, N_KV, HEAD_DIM), 1.0),
        "cache_win": nrm(ks[3], (DEPTH, DEC_BATCH, w_buf, 2, N_KV, HEAD_DIM), 1.0),
        "state_conv": nrm(ks[4], (DEPTH, DEC_BATCH, CONV_W - 1, C_CONV), 0.5),
        "page_table": page_table,
        "w_in": nrm(ks[6], (DEPTH, D_MODEL, N_IN), D_MODEL ** -0.5),
        "w_out": nrm(ks[7], (DEPTH, D_MIX, D_MODEL), D_MIX ** -0.5),
        "norm_pre": 1.0 + nrm(ks[8], (DEPTH, D_MODEL), 0.05),
        "norm_post": 1.0 + nrm(ks[9], (DEPTH, D_MODEL), 0.05),
        "conv_w": nrm(ks[10], (DEPTH, CONV_W, C_CONV), CONV_W ** -0.5),
        "conv_b": nrm(ks[11], (DEPTH, C_CONV), 0.02),
        "conv_ln_g": 1.0 + nrm(ks[12], (DEPTH, C_CONV), 0.05),
        "conv_ln_b": nrm(ks[13], (DEPTH, C_CONV), 0.02),
        "cmp_pos": nrm(ks[14], (DEPTH, 2, CMP_BLOCK, HEAD_DIM), 0.1),
        "cmp_w1": nrm(ks[15], (DEPTH, 2, CMP_BLOCK * HEAD_DIM, CMP_HIDDEN), (CMP_BLOCK * HEAD_DIM) ** -0.5),
        "cmp_b1": nrm(ks[16], (DEPTH, 2, CMP_HIDDEN), 0.02),
        "cmp_w2": nrm(ks[17], (DEPTH, 2, CMP_HIDDEN, HEAD_DIM), CMP_HIDDEN ** -0.5),
        "cmp_b2": nrm(ks[18], (DEPTH, 2, HEAD_DIM), 0.02),
        "gate_b": nrm(ks[19], (DEPTH, 3 * N_HEADS), 0.1),
    }


def reference(x_prompt, x_sample, cache_kv_pages, cache_win, state_conv, page_table,
              w_in, w_out, norm_pre, norm_post, conv_w, conv_b, conv_ln_g, conv_ln_b,
              cmp_pos, cmp_w1, cmp_b1, cmp_w2, cmp_b2, gate_b):
    B, S = x_prompt.shape[0], x_prompt.shape[1]
    DB = x_sample.shape[0]
    past_len = page_table.shape[1] * cache_kv_pages.shape[2]
    dt = x_prompt.dtype
    yp, ys = x_prompt, x_sample
    kvp_l, kvs_l, winp_l, wins_l, convp_l, convs_l = [], [], [], [], [], []
    for l in range(DEPTH):
        w = (w_in[l], w_out[l], norm_pre[l], norm_post[l], conv_w[l], conv_b[l],
             conv_ln_g[l], conv_ln_b[l], cmp_pos[l], cmp_w1[l], cmp_b1[l], cmp_w2[l],
             cmp_b2[l], gate_b[l])
        kv0 = jnp.zeros((B, 0, 4, N_KV, HEAD_DIM), dt)
        win0 = jnp.zeros((B, 0, 2, N_KV, HEAD_DIM), dt)
        conv0 = jnp.zeros((B, CONV_W - 1, C_CONV), dt)
        yp, kvp, winp, convp = _layer(yp, 0, kv0, win0, conv0, min(WINDOW, S), *w)
        kv_past = cache_kv_pages[l][page_table].reshape(DB, past_len, 4, N_KV, HEAD_DIM)
        ys, kvs, wins, convs = _layer(ys, past_len, kv_past, cache_win[l], state_conv[l],
                                      cache_win.shape[2], *w)
        kvp_l.append(kvp); kvs_l.append(kvs); winp_l.append(winp)
        wins_l.append(wins); convp_l.append(convp); convs_l.append(convs)
    kv_prompt = jnp.stack(kvp_l)
    kv_sample = jnp.stack(kvs_l)
    win_prompt = jnp.stack(winp_l)
    win_sample = jnp.stack(wins_l)
    conv_prompt = jnp.stack(convp_l)
    conv_sample = jnp.stack(convs_l)
    return (yp, ys, kv_prompt, kv_sample, win_prompt, win_sample, conv_prompt, conv_sample)
```

```python
import math
import types
from contextlib import ExitStack
import numpy as np
import concourse.bass as bass
import concourse.mybir as mybir
from concourse.bass_utils import run_bass_kernel_spmd

F32 = mybir.dt.float32
BF16 = mybir.dt.bfloat16
I32 = mybir.dt.int32
ALU = mybir.AluOpType
AF = mybir.ActivationFunctionType
AX = mybir.AxisListType

D = 2048
KC = 16
H = 16
DH = 64
NKV = 4
CC = 1024
NIN = 6704
HALF = 1024
WINDOW = 512
TOPN = 16
CONVW = 31
EPS = 1e-6
BIG = 1e9
MASK_NEG = 30000.0
ROPE_DIM = 16
ROPE_THETA = 500000.0
NDMA_SEM = 8

KV_OFF = 0
GL_OFF = 1536
Q_OFF = 1584
Z_OFF = Q_OFF + 1024
AG_OFF = Z_OFF + 1024
ZC_OFF = AG_OFF + 2048
SC_KV = 0
SC_GL = 12
SC_Q = 13
SC_Z = 21
SC_AG = 29
SC_ZC = 45
NSC = 53


def win_perm():
    perm = np.zeros(NIN, dtype=np.int64)
    perm[0:1536] = 5120 + np.arange(1536)
    perm[1536:1584] = 6656 + np.arange(48)
    for gp in range(2):
        for i in range(4):
            for e in range(2):
                h = (2 * gp + e) * 4 + i
                dst = gp * 512 + i * 128 + e * 64
                perm[Q_OFF + dst:Q_OFF + dst + 64] = 3072 + h * 64 + np.arange(64)
                perm[Z_OFF + dst:Z_OFF + dst + 64] = 4096 + h * 64 + np.arange(64)
    for c8 in range(8):
        perm[AG_OFF + c8 * 256:AG_OFF + c8 * 256 + 128] = c8 * 128 + np.arange(128)
        perm[AG_OFF + c8 * 256 + 128:AG_OFF + c8 * 256 + 256] = 1024 + c8 * 128 + np.arange(128)
    perm[ZC_OFF:ZC_OFF + 1024] = 2048 + np.arange(1024)
    return perm


def wout_perm():
    perm = np.zeros(D, dtype=np.int64)
    perm[0:1024] = np.arange(1024)
    for gp in range(2):
        for i in range(4):
            for e in range(2):
                h = (2 * gp + e) * 4 + i
                dst = 1024 + (gp * 4 + i) * 128 + e * 64
                perm[dst:dst + 64] = 1024 + h * 64 + np.arange(64)
    return perm


def _freeze(fn):
    if fn.__closure__ is None:
        return fn
    cells = []
    for c in fn.__closure__:
        try:
            cells.append(types.CellType(c.cell_contents))
        except ValueError:
            cells.append(c)
    g = types.FunctionType(fn.__code__, fn.__globals__, fn.__name__, fn.__defaults__, tuple(cells))
    g.__kwdefaults__ = fn.__kwdefaults__
    return g


class Em:
    COMPUTE = ("pe", "act", "dve", "pool")

    def __init__(self, nc, stack):
        self.nc = nc
        self.eng_names = ("pe", "act", "dve", "pool", "sp")
        self.prog = {e: [] for e in self.eng_names}
        self.tick = {e: 0 for e in self.COMPUTE}
        self.sem = {e: stack.enter_context(nc.semaphore("tk_" + e)) for e in self.COMPUTE}
        self.dq = ("sp", "pool")
        self.dsem = {q: [stack.enter_context(nc.semaphore(f"d_{q}{i}")) for i in range(NDMA_SEM)]
                     for q in self.dq}
        self.dcount = {q: 0 for q in self.dq}
        self.seen = {e: {} for e in self.eng_names}
        self.buf = {}
        self.n_inst = 0

    def _need(self, eng, tok, waits):
        if tok is None:
            return
        if tok[0] == "c":
            key, val = ("c", tok[1]), tok[2]
        else:
            key, val = ("d", tok[1], tok[2]), tok[3]
        if self.seen[eng].get(key, 0) >= val:
            return
        waits[key] = max(waits.get(key, 0), val)

    PSUM_KEYS = frozenset(("pA", "pB", "pC", "pD", "pO0", "pO1", "pO2", "pT"))

    def _split(self, reads, writes):
        r2 = [k for k in reads if k not in self.PSUM_KEYS]
        w2 = list(writes) + [k for k in reads if k in self.PSUM_KEYS]
        return r2, w2

    def _deps(self, eng, reads, writes):
        waits = {}
        for k in reads:
            st = self.buf.get(k)
            if st is not None:
                self._need(eng, st["w"], waits)
        for k in writes:
            st = self.buf.get(k)
            if st is not None:
                self._need(eng, st["w"], waits)
                for t in st["r"]:
                    self._need(eng, t, waits)
        for key, val in waits.items():
            self.seen[eng][key] = val
        return waits

    def _commit(self, tok, reads, writes):
        src = tok[:2] if tok[0] == "c" else tok[:3]
        for k in reads:
            st = self.buf.setdefault(k, {"w": None, "r": []})
            st["r"] = [t for t in st["r"] if (t[:2] if t[0] == "c" else t[:3]) != src] + [tok]
        for k in writes:
            self.buf[k] = {"w": tok, "r": []}

    def op(self, eng, fns, reads=(), writes=()):
        if callable(fns):
            fns = [fns]
        fns = [_freeze(f_) for f_ in fns]
        reads, writes = self._split(reads, writes)
        waits = self._deps(eng, reads, writes)
        self.tick[eng] += 1
        t = self.tick[eng]
        tok = ("c", eng, t)
        self.prog[eng].append((waits, fns, tok))
        self._commit(tok, reads, writes)
        self.n_inst += len(fns)

    def dma(self, q, fn, reads=(), writes=()):
        reads, writes = self._split(reads, writes)
        waits = self._deps(q, reads, writes)
        i = self.dcount[q]
        self.dcount[q] += 1
        j = i % NDMA_SEM
        val = 16 * (i // NDMA_SEM + 1)
        if i >= NDMA_SEM:
            key = ("d", q, j)
            prev = val - 16
            if self.seen[q].get(key, 0) < prev:
                waits[key] = max(waits.get(key, 0), prev)
                self.seen[q][key] = prev
        tok = ("d", q, j, val)
        self.prog[q].append((waits, [_freeze(fn)], tok))
        self._commit(tok, reads, writes)
        self.n_inst += 1
        return tok

    def _all_waits(self):
        waits = {}
        for q in self.dq:
            n = self.dcount[q]
            for j in range(min(n, NDMA_SEM)):
                waits[("d", q, j)] = 16 * ((n - 1 - j) // NDMA_SEM + 1)
        for e in self.COMPUTE:
            if self.tick[e] > 0:
                waits[("c", e)] = self.tick[e]
        return waits

    def barrier(self):
        allw = self._all_waits()
        for e in self.eng_names:
            w = {k: v for k, v in allw.items() if self.seen[e].get(k, 0) < v}
            for k, v in w.items():
                self.seen[e][k] = v
            if w:
                self.prog[e].append((w, [], None))
        self.buf = {}

    def _sem_of(self, key):
        if key[0] == "c":
            return self.sem[key[1]]
        return self.dsem[key[1]][key[2]]

    def finish(self, final_eng="sp"):
        self.prog[final_eng].append((self._all_waits(), [], None))
        nc = self.nc
        em = self

        def replay(name, engine):
            for waits, fns, tok in em.prog[name]:
                for key, val in waits.items():
                    engine.wait_ge(em._sem_of(key), val)
                n = len(fns)
                for i, fn in enumerate(fns):
                    ins = fn(engine)
                    if i == n - 1 and tok is not None:
                        if tok[0] == "c":
                            ins.then_inc(em.sem[tok[1]], 1)
                        else:
                            ins.then_inc(em.dsem[tok[1]][tok[2]], 16)

        with nc.Block() as block:
            @block.tensor
            def _(e):
                replay("pe", e)

            @block.scalar
            def _(e):
                replay("act", e)

            @block.vector
            def _(e):
                replay("dve", e)

            @block.gpsimd
            def _(e):
                replay("pool", e)

            @block.sync
            def _(e):
                replay("sp", e)


def rope_inv():
    half = ROPE_DIM // 2
    return (ROPE_THETA ** (-np.arange(half, dtype=np.float32) / half)).astype(np.float32)


def host_consts(SEQ, PAST):
    NT = SEQ // 128
    NBLK_S = PAST // 64
    c = {}
    a = np.arange(128)[:, None]
    b = np.arange(512)[None, :]
    masks = np.zeros((128, 8, 512), np.float32)
    for m in range(4):
        masks[:, m, :] = (b - a >= 128 * m)
    for m in range(4, 8):
        Dd = m - 3
        masks[:, m, :] = (b - a < 512 - 128 * Dd)
    c["masks"] = (masks - 1.0) * MASK_NEG
    cm = np.zeros((16, 2, 512), np.float32)
    i16 = np.arange(16)[:, None]
    for r in range(2):
        cm[:, r, :] = (64 * i16 + 63 <= 512 * r + b)
    c["cmaskc"] = cm
    inv = rope_inv()
    pos = np.arange(SEQ, dtype=np.float32)
    ang = pos[:, None] * inv[None, :]
    cosv, sinv = np.cos(ang).astype(np.float32), np.sin(ang).astype(np.float32)
    c["cos_tm"] = np.ascontiguousarray(cosv.reshape(NT, 128, 8).transpose(1, 0, 2))
    c["sin_tm"] = np.ascontiguousarray(sinv.reshape(NT, 128, 8).transpose(1, 0, 2))
    Cf = np.ones((128, SEQ), np.float32)
    Sf = np.zeros((128, SEQ), np.float32)
    for e in range(2):
        for i in range(8):
            Cf[e * 64 + i] = cosv[:, i]
            Cf[e * 64 + 8 + i] = cosv[:, i]
            Sf[e * 64 + i] = -sinv[:, i]
            Sf[e * 64 + 8 + i] = sinv[:, i]
    c["rope_c"] = Cf
    c["rope_s"] = Sf
    angs = np.float32(PAST) * inv
    cs, ss = np.cos(angs).astype(np.float32), np.sin(angs).astype(np.float32)
    Cs = np.ones((128, 1), np.float32)
    Ss = np.zeros((128, 1), np.float32)
    for e in range(2):
        for i in range(8):
            Cs[e * 64 + i] = cs[i]
            Cs[e * 64 + 8 + i] = cs[i]
            Ss[e * 64 + i] = -ss[i]
            Ss[e * 64 + 8 + i] = ss[i]
    c["rope_cs"] = np.concatenate([Cs, Ss], axis=1)
    perm = np.zeros((128, 128), np.float32)
    for m in range(128):
        dd = m % 64
        if dd < 8:
            perm[m + 8, m] = 1.0
        elif dd < 16:
            perm[m - 8, m] = 1.0
    c["perm"] = perm
    blk = np.zeros((128, 128), np.float32)
    blk[0:64, 0:64] = 1.0
    blk[64:128, 64:128] = 1.0
    c["blkones"] = blk
    NBP = SEQ // 64
    ex = np.zeros((128, NT, 128), np.float32)
    for kt in range(NT):
        for key in range(128):
            n = 2 * kt + key // 64
            if n < 32:
                ex[n, kt, key] = 1.0
    c["expand"] = ex
    cm_tm = np.zeros((128, 8, 32), np.float32)
    keep = np.zeros((128, 8, 32), np.float32)
    add = np.zeros((128, 8, 32), np.float32)
    n = np.arange(32)[None, :]
    for t in range(8):
        tp = ((NT - 8 + t) * 128 + np.arange(128))[:, None]
        cur = tp // 64
        cm_tm[:, t, :] = ((n + 1) * 64 - 1 <= tp)
        forced = (n == 0) | (n == cur) | (n == cur - 1)
        valid = n <= cur
        keep[:, t, :] = valid & ~forced
        add[:, t, :] = np.where(valid, np.where(forced, BIG, 0.0), -BIG)
    c["cm_tm"], c["keep_tm"], c["add_tm"] = cm_tm, keep, add
    NS = NBLK_S + 8
    ks_ = np.zeros((4, NS), np.float32)
    as_ = np.full((4, NS), -BIG, np.float32)
    for nn in range(NBLK_S + 1):
        forced = nn in (0, NBLK_S, NBLK_S - 1)
        ks_[:, nn] = 0.0 if forced else 1.0
        as_[:, nn] = BIG if forced else 0.0
    c["keep_s"], c["add_s"] = ks_, as_
    ind4 = np.zeros((4, 128), np.float32)
    for p in range(128):
        plo, r = p // 64, p % 64
        ind4[plo * 2 + r // 32, p] = 1.0
    c["ind4"] = ind4
    c["iota64"] = (np.arange(128) % 64).astype(np.float32).reshape(128, 1)
    return c


CONST_SHAPES = None


def build(DEPTH, SEQ, PAST, NPOOL):
    NT = SEQ // 128
    NH = SEQ // HALF
    NPAGES = PAST // 128
    NDP = NPAGES // 2
    NBLK_S = PAST // 64
    NBT = (NBLK_S + 127) // 128
    NS = NBLK_S + 8
    NBP = SEQ // 64
    assert NDP % 8 == 0 and SEQ % HALF == 0 and NBP <= 32
    nc = bass.Bass("TRN2", target_bir_lowering=False)
    st = ExitStack()
    consts = host_consts(SEQ, PAST)
    with st:
        def din(name, shape, dt=F32):
            return nc.dram_tensor(name, list(shape), dt, kind="ExternalInput").ap()

        def dout(name, shape, dt=F32):
            return nc.dram_tensor(name, list(shape), dt, kind="ExternalOutput").ap()

        def dtmp(name, shape, dt=F32):
            return nc.dram_tensor(name, list(shape), dt).ap()

        xp = din("xp", [SEQ, D])
        xs = din("xs", [1, D])
        pool = din("pool", [DEPTH * NPOOL * 64, 2048])
        winc = din("winc", [DEPTH, 512, 512])
        convT = din("convT", [DEPTH, 128, 8, 30])
        convN = din("convN", [DEPTH, 30, 1024])
        ptab = din("ptab", [1, NPAGES], I32)
        w_in = din("w_in", [DEPTH, D, NIN])
        w_out = din("w_out", [DEPTH, D, D])
        gpre = din("gpre", [DEPTH, D])
        gpost = din("gpost", [DEPTH, D])
        gcols = din("gcols", [DEPTH, 128, 32])
        cwT = din("cwT", [DEPTH, 128, 8, CONVW])
        ccols = din("ccols", [DEPTH, 128, 24])
        pos_tm_d = din("pos_tm", [DEPTH, 128, 2, 64])
        pos_dp_d = din("pos_dp", [DEPTH, 128, 2, 2, 64])
        w1 = din("w1", [DEPTH, 2, 4096, 128])
        w2 = din("w2", [DEPTH, 2, 128, 64])
        mcols = din("mcols", [DEPTH, 128, 4])
        b2row = din("b2row", [DEPTH, 1, 128])
        gateb = din("gateb", [DEPTH, 1, 48])
        gatebc = din("gatebc", [DEPTH, 48, 1])
        cd = {k: din("c_" + k, v.shape) for k, v in consts.items()}

        yp = dout("yp", [SEQ, D])
        ys = dout("ys", [1, D])
        kvp = dout("kvp", [DEPTH, SEQ, 1024])
        kvs = dout("kvs", [DEPTH, 1, 1024])
        winp = dout("winp", [DEPTH, 512, 512])
        wins = dout("wins", [DEPTH, 512, 512])
        convp = dout("convp", [DEPTH, 30, 1024])
        convs = dout("convs", [DEPTH, 30, 1024])
        resid = dtmp("resid", [SEQ, D])
        yraw = dtmp("yraw", [HALF, D])
        selscr = dtmp("selscr", [4, NBLK_S])

        em = Em(nc, st)

        def sb(name, shape, dt):
            return st.enter_context(nc.sbuf_tensor(name, list(shape), dt))

        def ps(name, shape, dt=F32):
            return st.enter_context(nc.psum_tensor(name, list(shape), dt))

        uT = sb("uT", [128, KC, HALF], BF16)
        mixT = sb("mixT", [128, KC, HALF], BF16)
        wb = [sb(f"wb{i}", [128, KC, 256], BF16) for i in range(2)]
        ksT = sb("ksT", [128, 2, SEQ], BF16)
        kwT = sb("kwT", [128, 2, SEQ], BF16)
        vsA = sb("vsA", [128, NT, 4, 65], BF16)
        vwA = sb("vwA", [128, NT, 4, 65], BF16)
        masks = sb("masks", [128, 8, 512], BF16)
        cmaskc = sb("cmaskc", [16, 2, 512], BF16)
        kcmpT = sb("kcmpT", [128, 2, 32], BF16)
        vcmp = [sb(f"vcmp{i}", [16, 4, 65], BF16) for i in range(2)]
        ident_b = sb("ident_b", [128, 128], BF16)
        ident_f = sb("ident_f", [128, 128], F32)
        ones_f = sb("ones_f", [128, 128], F32)
        ones_b = sb("ones_b", [128, 128], BF16)
        perm_b = sb("perm_b", [128, 128], BF16)
        perm_f = sb("perm_f", [128, 128], F32)
        blk_f = sb("blk_f", [128, 128], F32)
        expand = sb("expand", [128, NT, 128], BF16)
        cos_tm = sb("cos_tm", [128, NT, 8], F32)
        sin_tm = sb("sin_tm", [128, NT, 8], F32)
        cm_tm = sb("cm_tm", [128, 8, 32], F32)
        keep_tm = sb("keep_tm", [128, 8, 32], F32)
        add_tm = sb("add_tm", [128, 8, 32], F32)
        keep_s = sb("keep_s", [4, NS], F32)
        add_s = sb("add_s", [4, NS], F32)
        ind4 = sb("ind4", [4, 128], F32)
        iota64 = sb("iota64", [128, 1], F32)
        rope_cs = sb("rope_cs", [128, 2], F32)
        gates = sb("gates", [128, 8, 48], F32)
        gtail = sb("gtail", [128, 8, 30], BF16)
        gcol_t = sb("gcol_t", [128, 32], F32)
        cw_t = sb("cw_t", [128, 8, CONVW], F32)
        ccol_t = sb("ccol_t", [128, 24], F32)
        mcol_t = sb("mcol_t", [128, 4], F32)
        w2k_lo = sb("w2k_lo", [128, 128], BF16)
        w2k_hi = sb("w2k_hi", [128, 128], BF16)
        w2kn = sb("w2kn", [128, 64], BF16)
        w2v = sb("w2v", [128, 64], BF16)
        b2bc = sb("b2bc", [32, 128], F32)
        gateb_bc = sb("gateb_bc", [128, 48], F32)
        gateb_c = sb("gateb_c", [48, 1], F32)
        xsT = sb("xsT", [128, KC], F32)
        uTs = sb("uTs", [128, KC], BF16)
        projs = sb("projs", [128, NSC], F32)
        mixTs = sb("mixTs", [128, KC], BF16)
        ysraw = sb("ysraw", [128, KC], F32)
        idxb = sb("idxb", [128, NDP], F32)
        idxq = [sb(f"idxq{q_}", [128, NDP], I32) for q_ in range(4)]
        small = sb("small", [128, 64], F32)
        eps_t = sb("eps_t", [128, 1], F32)
        arena = sb("arena", [128, 32768], BF16)

        pA = ps("pA", [128, 512])
        pB = ps("pB", [128, 512])
        pC = ps("pC", [128, 512])
        pD = ps("pD", [128, 512])
        pO = [ps(f"pO{i}", [128, 512]) for i in range(3)]
        pT = ps("pT", [128, 1024], BF16)

        def carve(off, shape, dt):
            esz = 2 if dt == BF16 else 4
            n = int(np.prod(shape[1:]))
            nb = n * esz
            assert off % 4 == 0 and off + nb <= 65536, (off, nb)
            v = arena[0:shape[0], off // 2: off // 2 + nb // 2]
            if dt != BF16:
                v = v.bitcast(dt)
            if len(shape) == 3:
                v = v.rearrange("p (a b) -> p a b", b=shape[2])
            elif len(shape) == 4:
                v = v.rearrange("p (a b c) -> p a b c", b=shape[2], c=shape[3])
            elif len(shape) == 5:
                v = v.rearrange("p (a b c d) -> p a b c d", b=shape[2], c=shape[3], d=shape[4])
            return v, off + ((nb + 31) // 32) * 32

        class Carver:
            def __init__(self):
                self.off = 0

            def __call__(self, shape, dt):
                v, self.off = carve(self.off, shape, dt)
                return v

        def load_f32(tile_ap, src, key):
            em.dma("sp", lambda e: e.dma_start(out=tile_ap, in_=src), writes=[key])

        def load_cast(tile_ap, src, key):
            em.dma("pool", lambda e: e.dma_start(out=tile_ap, in_=src), writes=[key])

        load_cast(masks[:], cd["masks"], "masks")
        load_cast(cmaskc[:], cd["cmaskc"], "cmaskc")
        load_cast(expand[:], cd["expand"], "expand")
        load_cast(perm_b[:], cd["perm"], "perm_b")
        load_f32(perm_f[:], cd["perm"], "perm_f")
        load_f32(blk_f[:], cd["blkones"], "blk_f")
        load_f32(cos_tm[:], cd["cos_tm"], "cos_tm")
        load_f32(sin_tm[:], cd["sin_tm"], "sin_tm")
        load_f32(cm_tm[:], cd["cm_tm"], "cm_tm")
        load_f32(keep_tm[:], cd["keep_tm"], "keep_tm")
        load_f32(add_tm[:], cd["add_tm"], "add_tm")
        load_f32(keep_s[:], cd["keep_s"], "keep_s")
        load_f32(add_s[:], cd["add_s"], "add_s")
        load_f32(ind4[:], cd["ind4"], "ind4")
        load_f32(iota64[:], cd["iota64"], "iota64")
        load_f32(rope_cs[:], cd["rope_cs"], "rope_cs")
        em.op("pool", lambda e: e.memset(ident_f[:], 0.0), writes=["ident_f"])
        em.op("pool", lambda e: e.affine_select(out=ident_f[:], in_=ident_f[:], pattern=[[-1, 128]],
                                                 compare_op=ALU.not_equal, fill=1.0, base=0,
                                                 channel_multiplier=1),
              reads=["ident_f"], writes=["ident_f"])
        em.op("dve", lambda e: e.tensor_copy(out=ident_b[:], in_=ident_f[:]), reads=["ident_f"], writes=["ident_b"])
        em.op("dve", lambda e: e.memset(ones_f[:], 1.0), writes=["ones_f"])
        em.op("dve", lambda e: e.memset(eps_t[:], EPS), writes=["eps_t"])
        em.op("dve", lambda e: e.memset(ones_b[:], 1.0), writes=["ones_b"])
        em.op("pool", lambda e: e.memset(vsA[:], 1.0), writes=["vsA"])
        em.op("pool", lambda e: e.memset(vwA[:], 1.0), writes=["vwA"])
        for i in range(2):
            em.op("pool", lambda e, i=i: e.memset(vcmp[i][:], 1.0), writes=[f"vcmp{i}"])
        em.op("pool", lambda e: e.memset(w2k_lo[:], 0.0), writes=["w2k_lo"])
        em.op("pool", lambda e: e.memset(w2k_hi[:], 0.0), writes=["w2k_hi"])
        em.dma("sp", lambda e: e.dma_start(out=xsT[:], in_=xs.rearrange("o (k p) -> p (o k)", p=128),
                                           allow_slow_non_contiguous=True), writes=["xsT"])
        with_c = Carver()
        ptb_i = with_c([128, NPAGES], I32)
        ptb_f = with_c([128, NPAGES], F32)
        em.dma("sp", lambda e: e.dma_start(out=ptb_i, in_=ptab[0].partition_broadcast(128)), writes=["ptb_i"])
        em.op("dve", lambda e: e.tensor_copy(out=ptb_f, in_=ptb_i), reads=["ptb_i"], writes=["ptb_f"])
        pv = ptb_f.rearrange("p (u two) -> p u two", two=2)
        em.op("dve", lambda e: e.tensor_scalar(out=idxb[0:64, :], in0=pv[0:64, :, 0], scalar1=64.0,
                                                scalar2=iota64[0:64, 0:1], op0=ALU.mult, op1=ALU.add),
              reads=["ptb_f", "iota64"], writes=["idxb_lo"])
        em.op("dve", lambda e: e.tensor_scalar(out=idxb[64:128, :], in0=pv[64:128, :, 1], scalar1=64.0,
                                                scalar2=iota64[64:128, 0:1], op0=ALU.mult, op1=ALU.add),
              reads=["ptb_f", "iota64"], writes=["idxb_hi"])
        em.barrier()

        wcount = [0]

        def load_w(src_ap, ncols):
            i = wcount[0] % 2
            wcount[0] += 1
            key = f"wb{i}"
            em.dma("pool", lambda e: e.dma_start(out=wb[i][:, :, 0:ncols],
                                                  in_=src_ap.rearrange("(k p) n -> p k n", p=128)),
                   writes=[key])
            return wb[i], key

        psw = [0]

        def next_ps():
            i = psw[0] % 2
            psw[0] += 1
            return (pA, "pA") if i == 0 else (pB, "pB")

        pcd = [0]

        def next_pcd():
            i = pcd[0] % 2
            pcd[0] += 1
            return (pC, "pC") if i == 0 else (pD, "pD")

        alt = [0]

        def evac_eng():
            alt[0] += 1
            return "act" if alt[0] % 2 == 0 else "dve"

        def copy_op(eng, out, in_, reads, writes):
            if eng == "act":
                em.op("act", lambda e: e.copy(out=out, in_=in_), reads=reads, writes=writes)
            else:
                em.op(eng, lambda e: e.tensor_copy(out=out, in_=in_), reads=reads, writes=writes)

        def mm_group(out_ap, pairs, reads, writes):
            n = len(pairs)
            fns = []
            for i, (l_, r_) in enumerate(pairs):
                fns.append(lambda e, l_=l_, r_=r_, i=i: e.matmul(out_ap, lhsT=l_, rhs=r_,
                                                                start=(i == 0), stop=(i == n - 1)))
            em.op("pe", fns, reads=reads, writes=writes)

        def sample_cols(wbuf, wkey, ncols, chunk0):
            nsub = (ncols + 127) // 128
            for s_ in range(nsub):
                m = min(128, ncols - s_ * 128)
                mm_group(pD[0:m, s_:s_ + 1],
                         [(wbuf[:, k, s_ * 128:s_ * 128 + m], uTs[:, k:k + 1]) for k in range(KC)],
                         reads=[wkey, "uTs"], writes=["pD"])
            m_all = min(128, ncols)
            em.op("dve", lambda e: e.tensor_copy(out=projs[0:m_all, chunk0:chunk0 + nsub], in_=pD[0:m_all, 0:nsub]),
                  reads=["pD"], writes=["projs"])

        def mlp_feed(Xp_ap, xpk, XT_ap, slot):
            em.op("pe", [lambda e, ck=ck: e.transpose(
                out=pT[:, ck * 128:(ck + 1) * 128],
                in_=Xp_ap[:, ck // 4, ck % 4, :],
                identity=ident_b[:]) for ck in range(8)], reads=[xpk, "ident_b"], writes=["pT"])
            copy_op(evac_eng(), XT_ap[:, :, slot, :], pT[:].rearrange("p (a b) -> p a b", b=128), ["pT"], [("XT", slot)])

        def mlp_stage1(XT_ap, nslots, hid_ap, W1_ap):
            nb = nslots * 4
            for ck in range(8):
                rhs_all = XT_ap[:, ck, 0:nslots, :].rearrange("p s (n j) -> p (s n) j", j=32)
                mm_group(pC[:, ck * 32:ck * 32 + nb],
                         [(W1_ap[:, ck // 4, jh, :], rhs_all[:, :, jh]) for jh in range(32)],
                         reads=["W1s"] + [("XT", s_) for s_ in range(nslots)], writes=["pC"])
            for c_ in range(2):
                em.op("act", lambda e, c_=c_: e.activation(
                    out=hid_ap[:, c_ * 4:(c_ + 1) * 4, 0:nb],
                    in_=pC[:, c_ * 128:(c_ + 1) * 128].rearrange("p (k n) -> p k n", n=32)[:, :, 0:nb],
                    func=AF.Silu, bias=mcol_t[:, c_:c_ + 1]), reads=["pC", "mcol_t"], writes=["hid"])
            return nb

        pool4 = pool.rearrange("n (q x) -> (n q) x", q=4)

        def col_op(out, in0, in1, op, reads, writes, eng="dve"):
            em.op(eng, lambda e: e.tensor_tensor(out=out, in0=in0, in1=in1, op=op), reads=reads, writes=writes)

        def sample_layer(l):
            em.barrier()
            cv = Carver()
            cmp_s = cv([128, NBT, 8, 64], F32)
            cmpV = cv([128, NBT, 256], BF16)
            D32 = [cv([128, 2, 512], F32) for _ in range(3)]
            Xs = [cv([128, 2, 4, 128], BF16) for _ in range(2)]
            pos_dp = cv([128, 2, 2, 64], F32)
            XT8 = cv([128, 8, 8, 128], BF16)
            W1s = cv([128, 2, 32, 128], BF16)
            hid = cv([128, 8, 32], BF16)
            stg = cv([32, 512], F32)
            load_f32(pos_dp, pos_dp_d[l], "pos_dp")
            em.dma("pool", lambda e: e.dma_start(out=W1s, in_=w1[l].rearrange("c (jh p) e -> p c jh e", p=128)),
                   writes=["W1s"])
            for u in range(NDP):
                b_ = u % 3
                for jl in range(2):
                    em.dma("pool", lambda e, u=u, b_=b_, jl=jl: e.indirect_dma_start(
                        out=D32[b_][:, jl, :], out_offset=None, in_=pool4,
                        in_offset=bass.IndirectOffsetOnAxis(ap=idxq[2 * jl][:, u:u + 1], axis=0)),
                           reads=[("idxq", 2 * jl)], writes=[f"D32{b_}"])
                em.op("dve", lambda e, b_=b_, u=u: e.tensor_tensor(
                    out=Xs[u % 2].rearrange("p c k (j d) -> p j c k d", d=64),
                    in0=D32[b_].rearrange("p j (c k d) -> p j c k d", c=2, d=64),
                    in1=pos_dp.unsqueeze(3).to_broadcast([128, 2, 2, 4, 64]), op=ALU.add),
                      reads=[f"D32{b_}", "pos_dp"], writes=[f"Xs{u % 2}"])
                mlp_feed(Xs[u % 2], f"Xs{u % 2}", XT8, u % 8)
                if u % 8 == 7:
                    bt = u // 8
                    nb = mlp_stage1(XT8, 8, hid, W1s)
                    pp, pk = next_ps()
                    for ck in range(8):
                        mm_group(pp[0:32, ck * 64:(ck + 1) * 64], [(hid[:, ck, 0:32], (w2kn if ck < 4 else w2v)[:])],
                                 reads=["hid", "w2kn", "w2v"], writes=[pk])
                    em.op("dve", lambda e, pp=pp: e.tensor_tensor(
                        out=stg.rearrange("p (c k d) -> p c k d", c=2, d=64),
                        in0=pp[0:32, :].rearrange("p (c k d) -> p c k d", c=2, d=64),
                        in1=b2bc[0:32, :].rearrange("p (c d) -> p c d", d=64).unsqueeze(2).to_broadcast([32, 2, 4, 64]),
                        op=ALU.add), reads=[pk, "b2bc"], writes=["stg"])
                    tl, po = (bt * 32) // 128, (bt * 32) % 128
                    em.dma("sp", lambda e, tl=tl, po=po: e.dma_start(
                        out=cmp_s[po:po + 32, tl, :, :].rearrange("p a b -> p (a b)"), in_=stg), reads=["stg"], writes=[("cmp_s", tl)])
            for tl in range(NBT):
                P = min(128, NBLK_S - tl * 128)
                em.op("dve", lambda e, tl=tl, P=P: e.tensor_copy(out=cmpV[0:P, tl, :].rearrange("p (k d) -> p k d", d=64),
                                                                 in_=cmp_s[0:P, tl, 4:8, :]),
                      reads=[("cmp_s", tl)], writes=[("cmpV", tl)])
            em.barrier()
            cv = Carver()
            cmp_s = cv([128, NBT, 8, 64], F32)
            cmpV = cv([128, NBT, 256], BF16)
            qbc_r = cv([128, 1024], F32)
            qbc_c = cv([128, 1024], F32)
            prod = cv([128, 2048], F32)
            D32b = [cv([128, 2, 512], F32) for _ in range(3)]
            KVb = [cv([128, 2, 512], BF16) for _ in range(2)]
            KT4 = cv([128, 4, 128], BF16)
            Qblk = cv([128, 2, 8], BF16)
            wtile = cv([128, 4, 512], F32)
            wV = cv([128, 4, 256], BF16)
            selexp = cv([128, NDP, 4], F32)
            sE = cv([128, 2, 16], F32)
            sEb = cv([128, 2, 16], BF16)
            Ec_all = cv([128, NBT, 16], F32)
            Qd = cv([128, 128], F32)
            cols = cv([128, 64], F32)
            apT = cv([128, 8, CONVW], F32)
            tmp31 = cv([128, 8, CONVW], F32)
            rin = cv([128, 12], F32)
            rr = cv([128, 12], F32)
            kvrow = cv([128, 8], F32)
            wrow = cv([128, 4], F32)
            srow = cv([4, NS], F32)
            srow2 = cv([4, NS], F32)
            selrow = cv([4, NS], F32)
            m16 = cv([4, 16], F32)
            selR = cv([4, NDP, 4], F32)
            imp_n = cv([128, NBT, 4], F32)
            denc = cv([128, 3, 16], F32)
            numc = cv([128, 3, 8], F32)
            dcol = cv([128, 3, 8], F32)
            gcol = cv([128, 8, 3], F32)
            gbc = cv([128, 48], F32)
            Gd = cv([48, 48], F32)
            gsc = cv([48, 1], F32)
            ocol = cv([128, 8], F32)
            agv = projs[:, SC_AG:SC_AG + 16].rearrange("p (c two) -> p c two", two=2)
            em.op("act", lambda e: e.activation(out=cols[:, 0:8], in_=agv[:, :, 1], func=AF.Sigmoid), reads=["projs"], writes=["c_sg"])
            col_op(cols[:, 8:16], agv[:, :, 0], cols[:, 0:8], ALU.mult, ["projs", "c_sg"], ["c_glu"])
            em.dma("sp", lambda e: e.dma_start(out=apT[:, :, 0:30], in_=convT[l]), writes=["apT_h"])
            em.op("dve", lambda e: e.tensor_copy(out=apT[:, :, 30], in_=cols[:, 8:16]), reads=["c_glu"], writes=["apT_n"])
            col_op(tmp31, apT, cw_t[:], ALU.mult, ["apT_h", "apT_n", "cw_t"], ["tmp31"])
            em.op("dve", lambda e: e.tensor_reduce(out=cols[:, 16:24], in_=tmp31, axis=AX.X, op=ALU.add), reads=["tmp31"], writes=["c_cs"])
            col_op(cols[:, 16:24], cols[:, 16:24], ccol_t[:, 0:8], ALU.add, ["c_cs", "ccol_t"], ["c_cs"])
            em.dma("sp", lambda e: e.dma_start(out=convs[l, 0:29, :], in_=convN[l, 1:30, :]))
            em.dma("sp", lambda e: e.dma_start(out=convs[l, 29, :].rearrange("(c p) -> p c", p=128), in_=cols[:, 8:16],
                                               allow_slow_non_contiguous=True), reads=["c_glu"])
            em.op("dve", lambda e: e.tensor_reduce(out=cols[:, 24:25], in_=cols[:, 16:24], axis=AX.X, op=ALU.add), reads=["c_cs"], writes=["c_p1"])
            col_op(cols[:, 32:40], cols[:, 16:24], cols[:, 16:24], ALU.mult, ["c_cs"], ["c_sq"])
            em.op("dve", lambda e: e.tensor_reduce(out=cols[:, 25:26], in_=cols[:, 32:40], axis=AX.X, op=ALU.add), reads=["c_sq"], writes=["c_p2"])
            mm_group(pD[:, 0:2], [(ones_f[:], cols[:, 24:26])], reads=["ones_f", "c_p1", "c_p2"], writes=["pD"])
            em.op("dve", lambda e: e.tensor_scalar(out=cols[:, 26:28], in0=pD[:, 0:2], scalar1=1.0 / CC, scalar2=None, op0=ALU.mult),
                  reads=["pD"], writes=["c_mv"])
            col_op(cols[:, 28:29], cols[:, 26:27], cols[:, 26:27], ALU.mult, ["c_mv"], ["c_m2"])
            col_op(cols[:, 29:30], cols[:, 27:28], cols[:, 28:29], ALU.subtract, ["c_mv", "c_m2"], ["c_var"])
            em.op("act", lambda e: e.activation(out=cols[:, 29:30], in_=cols[:, 29:30], func=AF.Sqrt, bias=eps_t[:, 0:1]),
                  reads=["c_var", "eps_t"], writes=["c_var"])
            em.op("dve", lambda e: e.reciprocal(out=cols[:, 29:30], in_=cols[:, 29:30]), reads=["c_var"], writes=["c_var"])
            em.op("dve", lambda e: e.tensor_scalar(out=cols[:, 32:40], in0=cols[:, 16:24], scalar1=cols[:, 26:27], scalar2=cols[:, 29:30],
                                                    op0=ALU.subtract, op1=ALU.mult), reads=["c_cs", "c_mv", "c_var", "c_sq"], writes=["c_t"])
            col_op(cols[:, 32:40], cols[:, 32:40], ccol_t[:, 8:16], ALU.mult, ["c_t", "ccol_t"], ["c_t"])
            col_op(cols[:, 32:40], cols[:, 32:40], ccol_t[:, 16:24], ALU.add, ["c_t"], ["c_t"])
            em.op("act", lambda e: e.activation(out=cols[:, 32:40], in_=cols[:, 32:40], func=AF.Silu), reads=["c_t"], writes=["c_t"])
            em.op("act", lambda e: e.activation(out=cols[:, 40:48], in_=projs[:, SC_ZC:SC_ZC + 8], func=AF.Silu), reads=["projs"], writes=["c_zs"])
            col_op(mixTs[:, 0:8], cols[:, 32:40], cols[:, 40:48], ALU.mult, ["c_t", "c_zs"], ["mixTs_c"])
            em.op("dve", lambda e: e.tensor_copy(out=rin[:, 0:8], in_=projs[:, SC_Q:SC_Q + 8]), reads=["projs"], writes=["rin_a"])
            em.op("dve", lambda e: e.tensor_copy(out=rin[:, 8:10], in_=projs[:, SC_KV + 4:SC_KV + 6]), reads=["projs"], writes=["rin_b"])
            em.op("dve", lambda e: e.tensor_copy(out=rin[:, 10:12], in_=projs[:, SC_KV + 8:SC_KV + 10]), reads=["projs"], writes=["rin_c"])
            mm_group(pD[:, 8:20], [(perm_f[:], rin)], reads=["perm_f", "rin_a", "rin_b", "rin_c"], writes=["pD"])
            em.op("dve", lambda e: e.tensor_scalar(out=rr, in0=rin, scalar1=rope_cs[:, 0:1], scalar2=None, op0=ALU.mult),
                  reads=["rin_a", "rin_b", "rin_c", "rope_cs"], writes=["rr"])
            em.op("dve", lambda e: e.scalar_tensor_tensor(out=rr, in0=pD[:, 8:20], scalar=rope_cs[:, 1:2], in1=rr, op0=ALU.mult, op1=ALU.add),
                  reads=["pD", "rr", "rope_cs"], writes=["rr"])
            qr, ksr, kwr = rr[:, 0:8], rr[:, 8:10], rr[:, 10:12]
            em.op("dve", lambda e: e.tensor_copy(out=kvrow[:, 0:4], in_=projs[:, SC_KV:SC_KV + 4]), reads=["projs"], writes=["kvrow_a"])
            em.op("dve", lambda e: e.tensor_copy(out=kvrow[:, 4:6], in_=ksr), reads=["rr"], writes=["kvrow_b"])
            em.op("dve", lambda e: e.tensor_copy(out=kvrow[:, 6:8], in_=projs[:, SC_KV + 6:SC_KV + 8]), reads=["projs"], writes=["kvrow_c"])
            em.dma("sp", lambda e: e.dma_start(out=kvs[l, 0, :].rearrange("(c p) -> p c", p=128), in_=kvrow, allow_slow_non_contiguous=True),
                   reads=["kvrow_a", "kvrow_b", "kvrow_c"])
            em.op("dve", lambda e: e.tensor_copy(out=wrow[:, 0:2], in_=kwr), reads=["rr"], writes=["wrow_a"])
            em.op("dve", lambda e: e.tensor_copy(out=wrow[:, 2:4], in_=projs[:, SC_KV + 10:SC_KV + 12]), reads=["projs"], writes=["wrow_b"])
            em.dma("sp", lambda e: e.dma_start(out=wins[l, 511, :].rearrange("(c p) -> p c", p=128), in_=wrow, allow_slow_non_contiguous=True),
                   reads=["wrow_a", "wrow_b"], writes=["wins_row"])
            em.dma("sp", lambda e: e.dma_start(out=wins[l, 0:511, :], in_=winc[l, 1:512, :]))
            em.dma("sp", lambda e: e.dma_start(out=wtile, in_=winc[l].rearrange("(t p) x -> p t x", p=128)), writes=["wtile"])
            em.dma("sp", lambda e: e.dma_start(out=wtile[0:1, 0, :], in_=wins[l, 511:512, :]), reads=["wins_row", "wtile"], writes=["wtile"])
            em.op("act", lambda e: e.copy(out=wV, in_=wtile[:, :, 256:512]), reads=["wtile"], writes=["wV"])
            for which, (qsrc, qk, qdst, qdk) in enumerate(((qr, "rr", qbc_r, "qbc_r"), (projs[:, SC_Q:SC_Q + 8], "projs", qbc_c, "qbc_c"))):
                for gp_ in range(2):
                    pp, pk = next_ps()
                    for i_ in range(4):
                        j_ = gp_ * 4 + i_
                        em.op("dve", lambda e, j_=j_, qsrc=qsrc: e.tensor_scalar(out=Qd, in0=ident_f[:], scalar1=qsrc[:, j_:j_ + 1], scalar2=None,
                                                                               op0=ALU.mult), reads=["ident_f", qk], writes=["Qd"])
                        mm_group(pp[:, i_ * 128:(i_ + 1) * 128], [(ones_f[:], Qd)], reads=["ones_f", "Qd"], writes=[pk])
                    em.op("dve", lambda e, pp=pp, gp_=gp_, qdst=qdst: e.tensor_copy(
                        out=qdst.rearrange("p (g e i d) -> p g i e d", g=2, e=2, i=4)[:, gp_],
                        in_=pp[:].rearrange("p (i e d) -> p i e d", e=2, d=64)), reads=[pk], writes=[(qdk, gp_)])

            def s_branch(Kap, Vap, P, nrow, qbc, qbk, kkeys, vkeys, br, first, last, mask=None, mkeys=(), keepE=None, front=None):
                pv_ = prod[0:P, 0:nrow * 1024].rearrange("p (r k i d) -> p r k i d", k=4, i=4, d=64)
                qv_ = qbc[0:P, :].rearrange("p (k i d) -> p k i d", k=4, i=4)
                if front is not None:
                    front()
                for r_ in (range(nrow) if front is None else ()):
                    em.op("dve", lambda e, r_=r_: e.tensor_tensor(out=pv_[:, r_], in0=Kap[:, r_].unsqueeze(2).to_broadcast([P, 4, 4, 64]),
                                                                  in1=qv_, op=ALU.mult),
                          reads=list(kkeys) + [(qbk, 0), (qbk, 1)], writes=[("prod", r_)])
                if front is None:
                    em.op("dve", lambda e: e.tensor_reduce(out=sE[0:P, 0:nrow, :], in_=prod[0:P, 0:nrow * 1024].rearrange("p (r h d) -> p r h d", h=16, d=64),
                                                           axis=AX.X, op=ALU.add), reads=[("prod", r_) for r_ in range(nrow)], writes=["sE"])
                    em.op("act", lambda e: e.activation(out=sE[0:P, 0:nrow, :], in_=sE[0:P, 0:nrow, :], func=AF.Exp, scale=0.125),
                          reads=["sE"], writes=["sE"])
                if mask is not None:
                    em.op("dve", lambda e: e.tensor_tensor(out=sE[0:P, 0:nrow, :].rearrange("p r (k i) -> p r k i", i=4),
                                                           in0=sE[0:P, 0:nrow, :].rearrange("p r (k i) -> p r k i", i=4),
                                                           in1=mask, op=ALU.mult), reads=["sE"] + list(mkeys), writes=["sE"])
                if keepE is not None:
                    em.op("dve", lambda e: e.tensor_copy(out=keepE, in_=sE[0:P, 0, :]), reads=["sE"], writes=["Ec_all"])
                em.op("dve", lambda e: e.tensor_copy(out=sEb[0:P, 0:nrow, :], in_=sE[0:P, 0:nrow, :]), reads=["sE"], writes=["sEb"])
                fns = []
                for r_ in range(nrow):
                    st_ = first and r_ == 0
                    fns.append(lambda e, r_=r_, st_=st_: e.matmul(pO[br][:, 16:32], lhsT=ones_b[0:P, :], rhs=sEb[0:P, r_, :], start=st_, stop=False,
                                                                  skip_group_check=True))
                    ev = sEb[0:P, r_, :].rearrange("p (k i) -> p k i", i=4)
                    for gp_ in range(2):
                        for i_ in range(4):
                            j_ = gp_ * 4 + i_
                            fns.append(lambda e, r_=r_, gp_=gp_, i_=i_, j_=j_, ev=ev: e.matmul(
                                pO[br][:, 2 * j_:2 * j_ + 2], lhsT=Vap[:, r_, gp_ * 128:(gp_ + 1) * 128], rhs=ev[:, 2 * gp_:2 * gp_ + 2, i_],
                                start=False, stop=False, skip_group_check=True))
                em.op("pe", fns, reads=["sEb", "ones_b"] + list(vkeys), writes=[f"pO{br}"])

            for tl in range(NBT):
                P = min(128, NBLK_S - tl * 128)
                s_branch(cmp_s[0:P, tl:tl + 1, 0:4, :], cmpV[0:P, tl:tl + 1, :], P, 1, qbc_c, "qbc_c", [("cmp_s", tl)], [("cmpV", tl)], 0,
                         tl == 0, tl == NBT - 1, keepE=Ec_all[0:P, tl, :])
            em.op("dve", lambda e: e.tensor_scalar(out=denc[:, 0, :], in0=pO[0][:, 16:32], scalar1=1e-30, scalar2=None, op0=ALU.max),
                  reads=["pO0"], writes=[("denc", 0)])
            em.op("dve", lambda e: e.reciprocal(out=cols[:, 48:64], in_=denc[:, 0, :]), reads=[("denc", 0)], writes=["c_rd"])
            Pl = min(128, NBLK_S)
            em.op("dve", lambda e: e.tensor_tensor(out=Ec_all[0:Pl], in0=Ec_all[0:Pl], in1=cols[0:Pl, 48:64].unsqueeze(1).to_broadcast([Pl, NBT, 16]),
                                                   op=ALU.mult), reads=["Ec_all", "c_rd"], writes=["Ec_all"])
            em.op("dve", lambda e: e.tensor_reduce(out=imp_n[0:Pl], in_=Ec_all[0:Pl].rearrange("p t (k i) -> p t k i", i=4), axis=AX.X, op=ALU.add),
                  reads=["Ec_all"], writes=["imp_n"])
            pp, pk = next_ps()
            for tl in range(NBT):
                P = min(128, NBLK_S - tl * 128)
                em.op("pe", lambda e, tl=tl, P=P, pp=pp: e.transpose(out=pp[0:4, tl * 128:tl * 128 + P], in_=imp_n[0:P, tl, :], identity=ident_f[0:P, 0:P]),
                      reads=["imp_n", "ident_f"], writes=[pk])
            em.op("dve", lambda e: e.memset(srow, 0.0), writes=["srow"])
            em.op("dve", lambda e, pp=pp: e.tensor_copy(out=srow[:, 0:NBLK_S], in_=pp[0:4, 0:NBLK_S]), reads=[pk, "srow"], writes=["srow"])
            col_op(srow, srow, keep_s[:], ALU.mult, ["srow", "keep_s"], ["srow"])
            col_op(srow, srow, add_s[:], ALU.add, ["srow", "add_s"], ["srow"])
            em.op("dve", lambda e: e.max(out=m16[:, 0:8], in_=srow), reads=["srow"], writes=["m16a"])
            em.op("dve", lambda e: e.match_replace(out=srow2, in_to_replace=m16[:, 0:8], in_values=srow, imm_value=-3e9),
                  reads=["srow", "m16a"], writes=["srow2"])
            em.op("dve", lambda e: e.max(out=m16[:, 8:16], in_=srow2), reads=["srow2"], writes=["m16b"])
            em.op("dve", lambda e: e.tensor_scalar(out=selrow, in0=srow, scalar1=m16[:, 15:16], scalar2=None, op0=ALU.is_ge),
                  reads=["srow", "m16b"], writes=["selrow"])
            em.dma("sp", lambda e: e.dma_start(out=selscr, in_=selrow[:, 0:NBLK_S]), reads=["selrow"], writes=["selscr"])
            for k_ in range(4):
                em.dma("sp", lambda e, k_=k_: e.dma_start(out=selR[:, :, k_], in_=selscr[k_].rearrange("(u n) -> n u", n=4),
                                                          allow_slow_non_contiguous=True),
                       reads=["selscr"], writes=[("selR", k_)])
            pp, pk = next_ps()
            mm_group(pp[:, 0:NDP * 4], [(ind4[:], selR.rearrange("p u k -> p (u k)"))], reads=["ind4"] + [("selR", k_) for k_ in range(4)], writes=[pk])
            em.op("dve", lambda e, pp=pp: e.tensor_copy(out=selexp.rearrange("p u k -> p (u k)"), in_=pp[:, 0:NDP * 4]), reads=[pk], writes=["selexp"])
            for tl in range(4):
                s_branch(wtile[:, tl:tl + 1, 0:256].rearrange("p r (k d) -> p r k d", d=64), wV[:, tl:tl + 1, :], 128, 1, qbc_r, "qbc_r",
                         ["wtile"], ["wV"], 2, tl == 0, tl == 3)
            em.op("dve", lambda e: e.memset(Qblk, 0.0), writes=["Qblk"])
            em.op("dve", lambda e: e.tensor_copy(out=Qblk[0:64, :, 0:4], in_=qr[0:64, :].rearrange("p (g i) -> p g i", i=4)),
                  reads=["rr", "Qblk"], writes=["Qblk"])
            em.op("dve", lambda e: e.tensor_copy(out=Qblk[64:128, :, 4:8], in_=qr[64:128, :].rearrange("p (g i) -> p g i", i=4)),
                  reads=["rr", "Qblk"], writes=["Qblk"])
            for u in range(NDP):
                b_ = u % 3
                v_ = u % 2
                for jl in range(2):
                    em.dma("pool", lambda e, u=u, b_=b_, jl=jl: e.indirect_dma_start(
                        out=D32b[b_][:, jl, :], out_offset=None, in_=pool4,
                        in_offset=bass.IndirectOffsetOnAxis(ap=idxq[2 * jl + 1][:, u:u + 1], axis=0)),
                           reads=[("idxq", 2 * jl + 1)], writes=[f"D32b{b_}"])
                em.op("act", lambda e, b_=b_, v_=v_: e.copy(out=KVb[v_], in_=D32b[b_]), reads=[f"D32b{b_}"], writes=[f"KVb{v_}"])

                def front(v_=v_):
                    em.op("pe", [lambda e, jl=jl, g_=g_: e.transpose(out=pT[:, (jl * 2 + g_) * 128:(jl * 2 + g_ + 1) * 128],
                                                                    in_=KVb[v_][:, jl, g_ * 128:(g_ + 1) * 128], identity=ident_b[:])
                                 for jl in range(2) for g_ in range(2)], reads=[f"KVb{v_}", "ident_b"], writes=["pT"])
                    em.op("dve", lambda e: e.tensor_copy(out=KT4, in_=pT[:, 0:512].rearrange("p (a b) -> p a b", b=128)), reads=["pT"], writes=["KT4"])
                    pq, pqk = next_pcd()
                    for jl in range(2):
                        for g_ in range(2):
                            mm_group(pq[:, jl * 16 + g_ * 8: jl * 16 + g_ * 8 + 8], [(KT4[:, jl * 2 + g_, :], Qblk[:, g_, :])],
                                     reads=["KT4", "Qblk"], writes=[pqk])
                    em.op("act", lambda e, pq=pq: e.activation(out=sE[:, 0:2, :], in_=pq[:, 0:32].rearrange("p (r h) -> p r h", h=16),
                                                               func=AF.Exp, scale=0.125), reads=[pqk], writes=["sE"])
                s_branch(None, KVb[v_][:, :, 256:512], 128, 2, qbc_r, "qbc_r",
                         [f"KVb{v_}"], [f"KVb{v_}"], 1, u == 0, u == NDP - 1,
                         mask=selexp[:, u, :].unsqueeze(1).unsqueeze(3).to_broadcast([128, 2, 4, 4]), mkeys=["selexp"], front=front)
            em.op("dve", lambda e: e.tensor_tensor(out=cols[:, 0:8].rearrange("p (g i) -> p g i", i=4), in0=qr.rearrange("p (g i) -> p g i", i=4),
                                                   in1=ksr.unsqueeze(2).to_broadcast([128, 2, 4]), op=ALU.mult), reads=["rr"], writes=["c_sn"])
            mm_group(pD[:, 32:40], [(blk_f[:], cols[:, 0:8])], reads=["blk_f", "c_sn"], writes=["pD"])
            em.op("act", lambda e: e.activation(out=cols[:, 8:16], in_=pD[:, 32:40], func=AF.Exp, scale=0.125), reads=["pD"], writes=["c_en"])
            for br in range(3):
                n3 = pO[br][:, 0:16].rearrange("p (j two) -> p j two", two=2)
                em.op("dve", lambda e, br=br, n3=n3: e.tensor_copy(out=numc[0:64, br, :], in_=n3[0:64, :, 0]), reads=[f"pO{br}"], writes=[("numc", br, 0)])
                em.op("dve", lambda e, br=br, n3=n3: e.tensor_copy(out=numc[64:128, br, :], in_=n3[64:128, :, 1]), reads=[f"pO{br}"], writes=[("numc", br, 1)])
                dv = pO[br][:, 16:32].rearrange("p (g e i) -> p g e i", g=2, e=2)
                em.op("dve", lambda e, br=br, dv=dv: e.tensor_copy(out=dcol[0:64, br, :].rearrange("p (g i) -> p g i", i=4), in_=dv[0:64, :, 0, :]),
                      reads=[f"pO{br}"], writes=[("dcol", br, 0)])
                em.op("dve", lambda e, br=br, dv=dv: e.tensor_copy(out=dcol[64:128, br, :].rearrange("p (g i) -> p g i", i=4), in_=dv[64:128, :, 1, :]),
                      reads=[f"pO{br}"], writes=[("dcol", br, 1)])
            nk = [("numc", br, e_) for br in range(3) for e_ in range(2)]
            dk = [("dcol", br, e_) for br in range(3) for e_ in range(2)]
            em.op("dve", lambda e: e.tensor_tensor(out=cols[:, 16:24].rearrange("p (g i) -> p g i", i=4), in0=cols[:, 8:16].rearrange("p (g i) -> p g i", i=4),
                                                   in1=projs[:, SC_KV + 6:SC_KV + 8].unsqueeze(2).to_broadcast([128, 2, 4]), op=ALU.mult),
                  reads=["c_en", "projs"], writes=["c_ev"])
            col_op(numc[:, 1, :], numc[:, 1, :], cols[:, 16:24], ALU.add, nk + ["c_ev"], nk)
            col_op(dcol[:, 1, :], dcol[:, 1, :], cols[:, 8:16], ALU.add, dk + ["c_en"], dk)
            em.op("dve", lambda e: e.tensor_scalar(out=dcol, in0=dcol, scalar1=1e-30, scalar2=None, op0=ALU.max), reads=dk, writes=dk)
            em.op("dve", lambda e: e.reciprocal(out=dcol, in_=dcol), reads=dk, writes=dk)
            col_op(numc, numc, dcol, ALU.mult, nk + dk, nk)
            col_op(gsc, projs[0:48, SC_GL:SC_GL + 1], gateb_c[:], ALU.add, ["projs", "gateb_c"], ["gsc"])
            em.op("act", lambda e: e.activation(out=gsc, in_=gsc, func=AF.Sigmoid), reads=["gsc"], writes=["gsc"])
            em.op("dve", lambda e: e.tensor_scalar(out=Gd, in0=ident_f[0:48, 0:48], scalar1=gsc[:, 0:1], scalar2=None, op0=ALU.mult),
                  reads=["ident_f", "gsc"], writes=["Gd"])
            pp, pk = next_ps()
            mm_group(pp[:, 0:48], [(ones_f[0:48, :], Gd)], reads=["ones_f", "Gd"], writes=[pk])
            em.op("dve", lambda e, pp=pp: e.tensor_copy(out=gbc, in_=pp[:, 0:48]), reads=[pk], writes=["gbc"])
            gv = gbc.rearrange("p (g e i b) -> p g e i b", g=2, e=2, i=4)
            for e_ in range(2):
                rs_ = slice(e_ * 64, (e_ + 1) * 64)
                em.op("dve", lambda e, e_=e_, rs_=rs_: e.tensor_copy(out=gcol[rs_].rearrange("p (g i) b -> p g i b", i=4), in_=gv[rs_, :, e_, :, :]),
                      reads=["gbc"], writes=[("gcol", e_)])
            gk = [("gcol", 0), ("gcol", 1)]
            em.op("dve", lambda e: e.tensor_tensor(out=numc, in0=numc, in1=gcol.rearrange("p j b -> p b j"), op=ALU.mult), reads=nk + gk, writes=nk)
            col_op(ocol, numc[:, 0, :], numc[:, 1, :], ALU.add, nk, ["ocol"])
            col_op(ocol, ocol, numc[:, 2, :], ALU.add, nk + ["ocol"], ["ocol"])
            em.op("act", lambda e: e.activation(out=cols[:, 24:32], in_=projs[:, SC_Z:SC_Z + 8], func=AF.Silu), reads=["projs"], writes=["c_za"])
            col_op(mixTs[:, 8:16], ocol, cols[:, 24:32], ALU.mult, ["ocol", "c_za"], ["mixTs_a"])
            em.op("dve", lambda e: e.memset(small[:, 32:33], 0.0), reads=["mixTs_a", "mixTs_c"], writes=["mixTs"])

        def sample_post(l, last):
            sq = small[:, 0:16]
            em.op("dve", lambda e: e.tensor_tensor(out=sq, in0=ysraw[:], in1=ysraw[:], op=ALU.mult), reads=["ysraw"], writes=["sm_sq"])
            em.op("dve", lambda e: e.tensor_reduce(out=small[:, 16:17], in_=sq, axis=AX.X, op=ALU.add), reads=["sm_sq"], writes=["sm_part"])
            mm_group(pD[:, 0:1], [(ones_f[:], small[:, 16:17])], reads=["ones_f", "sm_part"], writes=["pD"])
            em.op("act", lambda e: e.activation(out=small[:, 17:18], in_=pD[:, 0:1], func=AF.Sqrt, bias=eps_t[:, 0:1], scale=1.0 / D),
                  reads=["pD", "eps_t"], writes=["sm_rs"])
            em.op("dve", lambda e: e.reciprocal(out=small[:, 17:18], in_=small[:, 17:18]), reads=["sm_rs"], writes=["sm_rs"])
            em.op("dve", lambda e: e.scalar_tensor_tensor(out=sq, in0=ysraw[:], scalar=small[:, 17:18], in1=gcol_t[:, 16:32], op0=ALU.mult, op1=ALU.mult),
                  reads=["ysraw", "sm_rs", "gcol_t", "sm_sq"], writes=["sm_sq"])
            em.op("dve", lambda e: e.tensor_tensor(out=xsT[:], in0=xsT[:], in1=sq, op=ALU.add), reads=["xsT", "sm_sq"], writes=["xsT"])
            if last:
                em.dma("sp", lambda e: e.dma_start(out=ys.rearrange("o (k p) -> p (o k)", p=128), in_=xsT[:], allow_slow_non_contiguous=True),
                       reads=["xsT"])

        for l in range(DEPTH):
            last = (l == DEPTH - 1)
            load_f32(gcol_t[:], gcols[l], "gcol_t")
            load_f32(cw_t[:], cwT[l], "cw_t")
            load_f32(ccol_t[:], ccols[l], "ccol_t")
            load_f32(mcol_t[:], mcols[l], "mcol_t")
            load_cast(w2k_lo[:, 0:64], w2[l, 0], "w2k_lo")
            load_cast(w2k_hi[:, 64:128], w2[l, 0], "w2k_hi")
            load_cast(w2kn[:], w2[l, 0], "w2kn")
            load_cast(w2v[:], w2[l, 1], "w2v")
            em.dma("sp", lambda e, l=l: e.dma_start(out=b2bc[:], in_=b2row[l, 0].partition_broadcast(32)), writes=["b2bc"])
            em.dma("sp", lambda e, l=l: e.dma_start(out=gateb_bc[:], in_=gateb[l, 0].partition_broadcast(128)), writes=["gateb_bc"])
            load_f32(gateb_c[:], gatebc[l], "gateb_c")
            for q_ in range(4):
                em.op("dve", lambda e, l=l, q_=q_: e.tensor_scalar(out=idxq[q_][:], in0=idxb[:], scalar1=4.0,
                                                                   scalar2=float(l * NPOOL * 64 * 4 + q_), op0=ALU.mult, op1=ALU.add),
                      reads=["idxb_lo", "idxb_hi"], writes=[("idxq", q_)])

            sq = small[:, 0:16]
            em.op("dve", lambda e: e.tensor_tensor(out=sq, in0=xsT[:], in1=xsT[:], op=ALU.mult),
                  reads=["xsT"], writes=["sm_sq"])
            em.op("dve", lambda e: e.tensor_reduce(out=small[:, 16:17], in_=sq, axis=AX.X, op=ALU.add),
                  reads=["sm_sq"], writes=["sm_part"])
            mm_group(pD[:, 0:1], [(ones_f[:], small[:, 16:17])], reads=["ones_f", "sm_part"], writes=["pD"])
            em.op("act", lambda e: e.activation(out=small[:, 17:18], in_=pD[:, 0:1], func=AF.Sqrt, bias=eps_t[:, 0:1], scale=1.0 / D),
                  reads=["pD", "eps_t"], writes=["sm_rs"])
            em.op("dve", lambda e: e.reciprocal(out=small[:, 17:18], in_=small[:, 17:18]), reads=["sm_rs"], writes=["sm_rs"])
            em.op("dve", lambda e: e.scalar_tensor_tensor(out=uTs[:], in0=xsT[:], scalar=small[:, 17:18],
                                                           in1=gcol_t[:, 0:16], op0=ALU.mult, op1=ALU.mult),
                  reads=["xsT", "sm_rs", "gcol_t"], writes=["uTs"])

            for half in range(NH):
                T0 = half * 8
                do_s = (half == 0)
                em.barrier()
                cv = Carver()
                xt = [cv([128, D], F32) for _ in range(2)]
                ub = [cv([128, D], BF16) for _ in range(2)]
                gbc = cv([128, D], F32)
                rsb = cv([128, 16], F32)
                em.dma("sp", lambda e, l=l: e.dma_start(out=gbc, in_=gpre[l].partition_broadcast(128)),
                       writes=["gbc"])
                src = xp if l == 0 else resid
                for t in range(8):
                    T = T0 + t
                    b_ = t % 2
                    em.dma("sp", lambda e, T=T, b_=b_: e.dma_start(out=xt[b_], in_=src[T * 128:(T + 1) * 128, :]),
                           reads=[("resid", T)], writes=[f"xt{b_}"])
                    em.op("act", lambda e, b_=b_, t=t: e.activation(out=ub[b_], in_=xt[b_], func=AF.Square,
                                                                    accum_out=rsb[:, t:t + 1]),
                          reads=[f"xt{b_}"], writes=[f"ub{b_}", ("rsb", t)])
                    em.op("act", lambda e, t=t: e.activation(out=rsb[:, t:t + 1], in_=rsb[:, t:t + 1], func=AF.Sqrt,
                                                             bias=eps_t[:, 0:1], scale=1.0 / D),
                          reads=[("rsb", t), "eps_t"], writes=[("rsb", t)])
                    em.op("dve", lambda e, t=t: e.reciprocal(out=rsb[:, t:t + 1], in_=rsb[:, t:t + 1]),
                          reads=[("rsb", t)], writes=[("rsb", t)])
                    em.op("dve", lambda e, b_=b_, t=t: e.scalar_tensor_tensor(out=ub[b_], in0=xt[b_], scalar=rsb[:, t:t + 1],
                                                                              in1=gbc, op0=ALU.mult, op1=ALU.mult),
                          reads=[f"xt{b_}", ("rsb", t), "gbc", f"ub{b_}"], writes=[f"ub{b_}"])
                    for kh in range(2):
                        pk = f"pT{kh}"
                        em.op("pe", [lambda e, b_=b_, k=kh * 8 + kk, kk=kk: e.transpose(
                            out=pT[:, kk * 128:(kk + 1) * 128],
                            in_=ub[b_][:, k * 128:(k + 1) * 128], identity=ident_b[:]) for kk in range(8)],
                              reads=[f"ub{b_}", "ident_b"], writes=["pT"])
                        copy_op(evac_eng(), uT[:, kh * 8:(kh + 1) * 8, t * 128:(t + 1) * 128],
                                pT[:].rearrange("p (a b) -> p a b", b=128), ["pT"], [("uT", t)])

                em.barrier()
                cv = Carver()
                stage = [cv([128, 256], F32) for _ in range(2)]
                kb = [cv([128, 256], BF16) for _ in range(2)]
                rtmp = cv([128, 4, 4, 8], F32)
                Xb_all = cv([128, 8, 2, 256], BF16)
                Xp = [cv([128, 2, 4, 128], BF16) for _ in range(2)]
                XT = cv([128, 8, 4, 128], BF16)
                W1s = cv([128, 2, 32, 128], BF16)
                hid = cv([128, 8, 32], BF16)
                pos_t = cv([128, 2, 64], F32)
                vst = cv([32, 256], F32)
                load_f32(pos_t, pos_tm_d[l], "pos_t")
                em.dma("pool", lambda e, l=l: e.dma_start(out=W1s, in_=w1[l].rearrange("c (jh p) e -> p c jh e", p=128)),
                       writes=["W1s"])
                uT_all = [("uT", t) for t in range(8)]
                for j in range(7):
                    ncols = 256 if j < 6 else 48
                    off = KV_OFF + j * 256 if j < 6 else GL_OFF
                    wbuf, wkey = load_w(w_in[l][:, off:off + ncols], ncols)
                    if do_s:
                        sample_cols(wbuf, wkey, ncols, SC_KV + 2 * j if j < 6 else SC_GL)
                    for t in range(8):
                        T = T0 + t
                        pp, pk = next_ps()
                        mm_group(pp[:, 0:ncols], [(uT[:, k, t * 128:(t + 1) * 128], wbuf[:, k, 0:ncols]) for k in range(KC)],
                                 reads=[("uT", t), wkey], writes=[pk])
                        if j == 6:
                            em.op("dve", lambda e, pp=pp, t=t: e.tensor_tensor(out=gates[:, t, :], in0=pp[:, 0:48], in1=gateb_bc[:],
                                                                              op=ALU.add), reads=[pk, "gateb_bc"], writes=[("gates", t)])
                            em.op("act", lambda e, t=t: e.activation(out=gates[:, t, :], in_=gates[:, t, :], func=AF.Sigmoid),
                                  reads=[("gates", t)], writes=[("gates", t)])
                            continue
                        sb_ = (j * 8 + t) % 2
                        sg, sgk = stage[sb_], f"stage{sb_}"
                        em.op("act", lambda e, sg=sg, pp=pp: e.copy(out=sg, in_=pp[:, 0:256]), reads=[pk], writes=[sgk])
                        if j in (2, 4):
                            s4 = sg.rearrange("p (k d) -> p k d", d=64)
                            x1, x2 = s4[:, :, 0:8], s4[:, :, 8:16]
                            cb_ = cos_tm[:, T, :].unsqueeze(1).to_broadcast([128, 4, 8])
                            sn_ = sin_tm[:, T, :].unsqueeze(1).to_broadcast([128, 4, 8])
                            rk = "rtmp"
                            em.op("dve", lambda e, x1=x1, cb_=cb_: e.tensor_tensor(out=rtmp[:, 0], in0=x1, in1=cb_, op=ALU.mult),
                                  reads=[sgk, "cos_tm"], writes=[rk])
                            em.op("dve", lambda e, x2=x2, sn_=sn_: e.tensor_tensor(out=rtmp[:, 1], in0=x2, in1=sn_, op=ALU.mult),
                                  reads=[sgk, "sin_tm", rk], writes=[rk])
                            em.op("dve", lambda e, x2=x2, cb_=cb_: e.tensor_tensor(out=rtmp[:, 2], in0=x2, in1=cb_, op=ALU.mult),
                                  reads=[sgk, rk], writes=[rk])
                            em.op("dve", lambda e, x1=x1, sn_=sn_: e.tensor_tensor(out=rtmp[:, 3], in0=x1, in1=sn_, op=ALU.mult),
                                  reads=[sgk, rk], writes=[rk])
                            em.op("dve", lambda e, x1=x1: e.tensor_tensor(out=x1, in0=rtmp[:, 0], in1=rtmp[:, 1], op=ALU.subtract),
                                  reads=[rk, sgk], writes=[sgk])
                            em.op("dve", lambda e, x2=x2: e.tensor_tensor(out=x2, in0=rtmp[:, 2], in1=rtmp[:, 3], op=ALU.add),
                                  reads=[rk, sgk], writes=[sgk])
                        if j < 4:
                            em.dma("sp", lambda e, sg=sg, T=T, j=j, l=l: e.dma_start(
                                out=kvp[l, T * 128:(T + 1) * 128, j * 256:(j + 1) * 256], in_=sg), reads=[sgk])
                        elif T >= NT - 4:
                            em.dma("sp", lambda e, sg=sg, T=T, j=j, l=l: e.dma_start(
                                out=winp[l, (T - (NT - 4)) * 128:(T - (NT - 4) + 1) * 128, (j - 4) * 256:(j - 3) * 256], in_=sg),
                                   reads=[sgk])
                        if j in (0, 1):
                            em.op("dve", lambda e, sg=sg, t=t, j=j: e.tensor_tensor(
                                out=Xb_all[:, t, j, :].rearrange("p (k d) -> p k d", d=64),
                                in0=sg.rearrange("p (k d) -> p k d", d=64),
                                in1=pos_t[:, j, :].unsqueeze(1).to_broadcast([128, 4, 64]), op=ALU.add),
                                  reads=[sgk, "pos_t"], writes=[("Xb", t)])
                        elif j in (2, 4):
                            kbb, kbk = kb[sb_], f"kb{sb_}"
                            em.op("dve", lambda e, kbb=kbb, sg=sg: e.tensor_copy(out=kbb, in_=sg), reads=[sgk], writes=[kbk])
                            em.op("pe", [lambda e, kbb=kbb, g_=g_: e.transpose(out=pT[:, g_ * 128:(g_ + 1) * 128],
                                                                              in_=kbb[:, g_ * 128:(g_ + 1) * 128], identity=ident_b[:])
                                         for g_ in range(2)], reads=[kbk, "ident_b"], writes=["pT"])
                            dst = ksT if j == 2 else kwT
                            dk = ("ksT", T) if j == 2 else ("kwT", T)
                            copy_op("act", dst[:, :, T * 128:(T + 1) * 128], pT[:, 0:256].rearrange("p (a b) -> p a b", b=128),
                                    ["pT"], [dk])
                        else:
                            dst = vsA if j == 3 else vwA
                            dk = ("vsA", T) if j == 3 else ("vwA", T)
                            em.op("dve", lambda e, dst=dst, sg=sg, T=T: e.tensor_copy(
                                out=dst[:, T, :, 0:64], in_=sg.rearrange("p (k d) -> p k d", d=64)),
                                  reads=[sgk], writes=[dk])

                for v in range(4):
                    xpp, xpk = Xp[v % 2], f"Xp{v % 2}"
                    for tp in range(2):
                        for jl in range(2):
                            em.dma("sp", lambda e, xpp=xpp, tp=tp, jl=jl, v=v: e.dma_start(
                                out=xpp[tp * 64:(tp + 1) * 64, :, :, jl * 64:(jl + 1) * 64],
                                in_=Xb_all[jl:128:2, 2 * v + tp, :, :].rearrange("p c (k d) -> p c k d", d=64)),
                                   reads=[("Xb", 2 * v + tp)], writes=[xpk])
                    mlp_feed(xpp, xpk, XT, v)
                nb = mlp_stage1(XT, 4, hid, W1s)
                for gp in range(2):
                    pp, pk = next_ps()
                    mm_group(pp[:, 0:nb], [(w2k_lo[:], hid[:, 2 * gp, 0:nb]), (w2k_hi[:], hid[:, 2 * gp + 1, 0:nb])],
                             reads=["hid", "w2k_lo", "w2k_hi"], writes=[pk])
                    em.op("act", lambda e, pp=pp, gp=gp: e.activation(out=kcmpT[:, gp, half * 16:half * 16 + nb], in_=pp[:, 0:nb],
                                                                     func=AF.Identity, bias=mcol_t[:, 2:3]),
                          reads=[pk, "mcol_t"], writes=[("kcmpT", half)])
                pp, pk = next_ps()
                for kv_ in range(4):
                    mm_group(pp[0:nb, kv_ * 64:(kv_ + 1) * 64], [(hid[:, 4 + kv_, 0:nb], w2v[:])], reads=["hid", "w2v"], writes=[pk])
                em.op("dve", lambda e, pp=pp: e.tensor_tensor(
                    out=vcmp[half][0:nb, :, 0:64], in0=pp[0:nb, 0:256].rearrange("p (k d) -> p k d", d=64),
                    in1=b2bc[0:nb, 64:128].unsqueeze(1).to_broadcast([nb, 4, 64]), op=ALU.add),
                      reads=[pk, "b2bc"], writes=[f"vcmp{half}"])

                Qc0 = half * 2
                for gp in range(2):
                    em.barrier()
                    cv = Carver()
                    QT = cv([128, 4, HALF], BF16)
                    QrP = [cv([128, 4, HALF], BF16) for _ in range(2)]
                    zT = cv([128, 4, HALF], BF16)
                    Ct = cv([128, HALF], F32)
                    osb = Ct[:, 0:780].rearrange("p (b c) -> p b c", c=260)
                    St = cv([128, HALF], F32)
                    rt = [cv([128, 512], F32) for _ in range(2)]
                    Et = [cv([128, 512], BF16) for _ in range(3)]
                    E16 = [cv([16, 512], BF16) for _ in range(2)]
                    o_tm = cv([128, 8, 512], BF16)
                    selT = [cv([128, HALF], BF16) for _ in range(2)]
                    ec = cv([128, 4, 32], F32)
                    tk = cv([128, 128], F32)
                    selb = cv([128, 32], BF16)
                    d3 = cv([128, 4, 3], F32)
                    w3 = cv([128, 4, 3], F32)
                    ot = [rt[b_][:, 0:256].rearrange("p (s d) -> p s d", d=64) for b_ in range(2)]
                    em.op("pool", lambda e: e.memset(QrP[0][64:128], 0.0), writes=[("QrT", i_, t_) for i_ in range(4) for t_ in range(2)])
                    em.op("pool", lambda e: e.memset(QrP[1][0:64], 0.0), writes=[("QrT", i_, t_) for i_ in range(4) for t_ in range(2)])
                    if half >= 1:
                        for e2 in range(2):
                            em.op("pool", lambda e, e2=e2: e.memset(selT[e2], 0.0), writes=[("selT", e2, 0), ("selT", e2, 1)])
                    load_f32(Ct, cd["rope_c"][:, half * HALF:(half + 1) * HALF], "Ct")
                    load_f32(St, cd["rope_s"][:, half * HALF:(half + 1) * HALF], "St")
                    for which in range(2):
                        for wl in range(2):
                            off = (Q_OFF if which == 0 else Z_OFF) + gp * 512 + wl * 256
                            wbuf, wkey = load_w(w_in[l][:, off:off + 256], 256)
                            if do_s:
                                sample_cols(wbuf, wkey, 256, (SC_Q if which == 0 else SC_Z) + gp * 4 + wl * 2)
                            for s_ in range(2):
                                i = wl * 2 + s_
                                for tq in range(2):
                                    pp, pk = next_ps()
                                    mm_group(pp[:], [(wbuf[:, k, s_ * 128:(s_ + 1) * 128], uT[:, k, tq * 512:(tq + 1) * 512])
                                                     for k in range(KC)], reads=uT_all + [wkey], writes=[pk])
                                    sl = slice(tq * 512, (tq + 1) * 512)
                                    if which == 1:
                                        em.op("act", lambda e, pp=pp, i=i, sl=sl: e.activation(out=zT[:, i, sl], in_=pp[:], func=AF.Silu),
                                              reads=[pk], writes=[("zT", i, tq)])
                                        continue
                                    em.op("act", lambda e, pp=pp, i=i, sl=sl: e.copy(out=QT[:, i, sl], in_=pp[:]),
                                          reads=[pk], writes=[("QT", i, tq)])
                                    pq, pqk = next_pcd()
                                    mm_group(pq[:], [(perm_b[:], QT[:, i, sl])], reads=["perm_b", ("QT", i, tq)], writes=[pqk])
                                    em.op("dve", lambda e, i=i, sl=sl: e.tensor_tensor(out=rt[0], in0=QT[:, i, sl], in1=Ct[:, sl], op=ALU.mult),
                                          reads=[("QT", i, tq), "Ct"], writes=["rt0"])
                                    em.op("dve", lambda e, pq=pq, sl=sl: e.tensor_tensor(out=rt[1], in0=pq[:], in1=St[:, sl], op=ALU.mult),
                                          reads=[pqk, "St"], writes=["rt1"])
                                    for e2 in range(2):
                                        hs = slice(e2 * 64, (e2 + 1) * 64)
                                        em.op("dve", lambda e, i=i, sl=sl, e2=e2, hs=hs: e.tensor_tensor(out=QrP[e2][hs, i, sl], in0=rt[0][hs, :], in1=rt[1][hs, :], op=ALU.add),
                                              reads=["rt0", "rt1"], writes=[("QrT", i, tq)])
                    QT_all = [("QT", i, tq) for i in range(4) for tq in range(2)]
                    def prelude_tile(e_, t):
                        base = e_ * 64
                        T = T0 + t
                        pp, pk = next_ps()
                        for i in range(4):
                            mm_group(pp[:, i * 32:i * 32 + NBP],
                                     [(QT[base:base + 64, i, t * 128:(t + 1) * 128], kcmpT[base:base + 64, gp, 0:NBP])],
                                     reads=[("QT", i, t // 4), ("kcmpT", 0), ("kcmpT", 1)], writes=[pk])
                        em.op("act", lambda e, pp=pp: e.activation(out=ec[:, :, 0:NBP],
                                                                   in_=pp[:, 0:128].rearrange("p (h n) -> p h n", n=32)[:, :, 0:NBP],
                                                                   func=AF.Exp, scale=0.125), reads=[pk], writes=["ec"])
                        em.op("dve", lambda e, T=T: e.tensor_tensor(out=ec[:, :, 0:NBP], in0=ec[:, :, 0:NBP],
                                                                    in1=cm_tm[:, T - (NT - 8), 0:NBP].unsqueeze(1).to_broadcast([128, 4, NBP]),
                                                                    op=ALU.mult), reads=["ec", "cm_tm"], writes=["ec"])
                        em.op("dve", lambda e: e.tensor_reduce(out=tk[:, 0:4], in_=ec[:, :, 0:NBP], axis=AX.X, op=ALU.add),
                              reads=["ec"], writes=["tk_den"])
                        em.op("dve", lambda e: e.tensor_scalar(out=tk[:, 0:4], in0=tk[:, 0:4], scalar1=1e-30, scalar2=None, op0=ALU.max),
                              reads=["tk_den"], writes=["tk_den"])
                        em.op("dve", lambda e: e.reciprocal(out=tk[:, 4:8], in_=tk[:, 0:4]), reads=["tk_den"], writes=["tk_r"])
                        em.op("dve", lambda e: e.tensor_tensor(out=ec[:, :, 0:NBP], in0=ec[:, :, 0:NBP],
                                                               in1=tk[:, 4:8].unsqueeze(2).to_broadcast([128, 4, NBP]), op=ALU.mult),
                              reads=["ec", "tk_r"], writes=["ec"])
                        em.op("dve", lambda e: e.tensor_reduce(out=tk[:, 8:8 + NBP], in_=ec[:, :, 0:NBP].rearrange("p h n -> p n h"),
                                                               axis=AX.X, op=ALU.add), reads=["ec"], writes=["tk_imp"])
                        em.op("dve", lambda e, T=T: e.tensor_tensor(out=tk[:, 8:8 + NBP], in0=tk[:, 8:8 + NBP], in1=keep_tm[:, T - (NT - 8), 0:NBP],
                                                                    op=ALU.mult), reads=["tk_imp", "keep_tm"], writes=["tk_imp"])
                        em.op("dve", lambda e, T=T: e.tensor_tensor(out=tk[:, 8:8 + NBP], in0=tk[:, 8:8 + NBP], in1=add_tm[:, T - (NT - 8), 0:NBP],
                                                                    op=ALU.add), reads=["tk_imp", "add_tm"], writes=["tk_imp"])
                        em.op("dve", lambda e: e.max(out=tk[:, 48:56], in_=tk[:, 8:8 + NBP]), reads=["tk_imp"], writes=["tk_m1"])
                        em.op("dve", lambda e: e.match_replace(out=tk[:, 64:64 + NBP], in_to_replace=tk[:, 48:56],
                                                               in_values=tk[:, 8:8 + NBP], imm_value=-3e9),
                              reads=["tk_imp", "tk_m1"], writes=["tk_s2"])
                        em.op("dve", lambda e: e.max(out=tk[:, 56:64], in_=tk[:, 64:64 + NBP]), reads=["tk_s2"], writes=["tk_m2"])
                        em.op("dve", lambda e: e.tensor_scalar(out=tk[:, 64:64 + NBP], in0=tk[:, 8:8 + NBP], scalar1=tk[:, 63:64], scalar2=None,
                                                               op0=ALU.is_ge), reads=["tk_imp", "tk_m2", "tk_s2"], writes=["tk_s2"])
                        em.op("dve", lambda e: e.tensor_scalar(out=selb[:, 0:NBP], in0=tk[:, 64:64 + NBP], scalar1=-1.0, scalar2=MASK_NEG,
                                                               op0=ALU.add, op1=ALU.mult), reads=["tk_s2"], writes=["selb"])
                        em.op("pe", lambda e: e.transpose(out=pT[0:NBP, 0:128], in_=selb[:, 0:NBP], identity=ident_b[:]),
                              reads=["selb", "ident_b"], writes=["pT"])
                        copy_op("act", selT[e_][0:NBP, t * 128:(t + 1) * 128], pT[0:NBP, 0:128], ["pT"], [("selT", e_, t // 4)])
                    prelude_todo = []
                    if half >= 1:
                        for t in range(8):
                            prelude_tile(0, t)
                        prelude_todo = [(1, t) for t in range(8)]
                    LOOK = 2
                    jobs = []
                    for e_ in range(2):
                        for i in range(4):
                            for tq in range(2):
                                Qc = Qc0 + tq
                                grp = dict(e_=e_, i=i, tq=tq, Qc=Qc)
                                sets = [(0, tq)] if half == 0 else [(0, None), (1, tq)]
                                for si, (set_, r_) in enumerate(sets):
                                    jobs.append(dict(g=grp, br=0, first=(si == 0), set_=set_, r_=r_))
                                for br in (1, 2):
                                    kts = list(range(0, 4 * Qc + 4)) if br == 1 else list(range(max(0, 4 * Qc - 4), 4 * Qc + 4))
                                    for ki, kt in enumerate(kts):
                                        jobs.append(dict(g=grp, br=br, first=(ki == 0), kt=kt))
                                jobs[-1]["last"] = True
                    ecnt = [0, 0]

                    def emitA(jb):
                        g = jb["g"]
                        e_, i, tq, Qc = g["e_"], g["i"], g["tq"], g["Qc"]
                        base = e_ * 64
                        sl = slice(tq * 512, (tq + 1) * 512)
                        pq, pqk = next_pcd()
                        if jb["br"] == 0:
                            set_, r_ = jb["set_"], jb["r_"]
                            mm_group(pq[0:16, :], [(kcmpT[base:base + 64, gp, set_ * 16:(set_ + 1) * 16], QT[base:base + 64, i, sl])],
                                     reads=[("kcmpT", set_), ("QT", i, tq)], writes=[pqk])
                            E, Ek = E16[ecnt[0] % 2], f"E16{ecnt[0] % 2}"
                            ecnt[0] += 1
                            em.op("act", lambda e: e.activation(out=E, in_=pq[0:16, :], func=AF.Exp, scale=0.125), reads=[pqk], writes=[Ek])
                            if r_ is not None:
                                em.op("dve", lambda e: e.tensor_tensor(out=E, in0=E, in1=cmaskc[:, r_, :], op=ALU.mult),
                                      reads=[Ek, "cmaskc"], writes=[Ek])
                        else:
                            br, kt = jb["br"], jb["kt"]
                            KT, kname = (ksT, "ksT") if br == 1 else (kwT, "kwT")
                            Dd = 4 * Qc - kt
                            midx = -Dd if Dd <= 0 else (3 + Dd if br == 2 else None)
                            pairs = [(KT[:, gp, kt * 128:(kt + 1) * 128], QrP[e_][:, i, sl])]
                            rds = [(kname, kt), ("QrT", i, tq)]
                            if midx is not None:
                                pairs.append((ident_b[:], masks[:, midx, :]))
                                rds += ["ident_b", "masks"]
                            if br == 1 and Qc >= 2:
                                pairs.append((expand[:, kt, :], selT[e_][:, sl]))
                                rds += ["expand", ("selT", e_, tq)]
                            mm_group(pq[:], pairs, reads=rds, writes=[pqk])
                            E, Ek = Et[ecnt[1] % 3], f"Et{ecnt[1] % 3}"
                            ecnt[1] += 1
                            em.op("act", lambda e: e.activation(out=E, in_=pq[:], func=AF.Exp, scale=0.125), reads=[pqk], writes=[Ek])
                        jb["E"], jb["Ek"] = E, Ek

                    def emitB(jb):
                        g = jb["g"]
                        e_, i, tq = g["e_"], g["i"], g["tq"]
                        kvh = 2 * gp + e_
                        h = kvh * 4 + i
                        base = e_ * 64
                        E, Ek, br, first = jb["E"], jb["Ek"], jb["br"], jb["first"]
                        if br == 0:
                            set_ = jb["set_"]
                            em.op("pe", [lambda e, s_=s_: e.matmul(
                                pO[0][:, s_ * 65:(s_ + 1) * 65], lhsT=E[:, s_ * 128:(s_ + 1) * 128], rhs=vcmp[set_][:, kvh, :],
                                start=(first and s_ == 0), stop=False, skip_group_check=True) for s_ in range(4)],
                                  reads=[Ek, f"vcmp{set_}"], writes=["pO0"])
                        else:
                            kt = jb["kt"]
                            Vv, vname = (vsA, "vsA") if br == 1 else (vwA, "vwA")
                            em.op("pe", [lambda e, s_=s_: e.matmul(
                                pO[br][:, s_ * 65:(s_ + 1) * 65], lhsT=E[:, s_ * 128:(s_ + 1) * 128], rhs=Vv[:, kt, kvh, :],
                                start=(first and s_ == 0), stop=False, skip_group_check=True) for s_ in range(4)],
                                  reads=[Ek, (vname, kt)], writes=[f"pO{br}"])
                        if not jb.get("last"):
                            return
                        for br_ in range(3):
                            ceng = "dve" if br_ == 2 else "act"
                            copy_op(ceng, osb[:, br_, :], pO[br_][:, 0:260], [f"pO{br_}"], [("osb", br_), "Ct"])
                        for br_ in range(3):
                            em.op("dve", lambda e, br_=br_: e.tensor_scalar(
                                out=d3[:, :, br_], in0=osb[:, br_, :].rearrange("p (s c) -> p s c", c=65)[:, :, 64],
                                scalar1=1e-30, scalar2=None, op0=ALU.max), reads=[("osb", br_)], writes=[("d3", br_)])
                        d3k = [("d3", b_) for b_ in range(3)]
                        em.op("dve", lambda e: e.reciprocal(out=w3[:], in_=d3[:]), reads=d3k, writes=["w3"])
                        em.op("dve", lambda e: e.tensor_tensor(out=w3[:], in0=w3[:], in1=gates[:, tq * 4:(tq + 1) * 4, h * 3:(h + 1) * 3], op=ALU.mult),
                              reads=["w3"] + [("gates", tq * 4 + s_) for s_ in range(4)], writes=["w3"])

                        def oview(br_):
                            return osb[:, br_, :].rearrange("p (s c) -> p s c", c=65)[:, :, 0:64]

                        def wv(br_):
                            return w3[:, :, br_:br_ + 1].to_broadcast([128, 4, 64])
                        em.op("dve", lambda e: e.tensor_tensor(out=ot[0], in0=oview(0), in1=wv(0), op=ALU.mult),
                              reads=[("osb", 0), "w3"], writes=["rt0"])
                        em.op("dve", lambda e: e.tensor_tensor(out=ot[1], in0=oview(1), in1=wv(1), op=ALU.mult),
                              reads=[("osb", 1), "w3"], writes=["rt1"])
                        em.op("dve", lambda e: e.tensor_tensor(out=ot[0], in0=ot[0], in1=ot[1], op=ALU.add),
                              reads=["rt0", "rt1"], writes=["rt0"])
                        em.op("dve", lambda e: e.tensor_tensor(out=ot[1], in0=oview(2), in1=wv(2), op=ALU.mult),
                              reads=[("osb", 2), "w3", "rt0"], writes=["rt1"])
                        em.op("dve", lambda e: e.tensor_tensor(
                            out=o_tm[:, tq * 4:(tq + 1) * 4, i * 128 + base:i * 128 + base + 64], in0=ot[0], in1=ot[1], op=ALU.add),
                              reads=["rt0", "rt1"], writes=[("o_tm", tq, i, e_)])

                    pend = []
                    for ji, jb in enumerate(jobs):
                        if prelude_todo and ji % 4 == 3:
                            prelude_tile(*prelude_todo.pop(0))
                        emitA(jb)
                        pend.append(jb)
                        if len(pend) > LOOK:
                            emitB(pend.pop(0))
                    while pend:
                        emitB(pend.pop(0))
                    for i in range(4):
                        em.op("pe", [lambda e, t=t, i=i: e.transpose(out=pT[:, t * 128:(t + 1) * 128], in_=o_tm[:, t, i * 128:(i + 1) * 128],
                                                                      identity=ident_b[:]) for t in range(8)],
                              reads=[("o_tm", tq, i, e_) for tq in range(2) for e_ in range(2)] + ["ident_b"], writes=["pT"])
                        em.op("dve", lambda e, i=i, gp=gp: e.tensor_tensor(out=mixT[:, 8 + gp * 4 + i, :], in0=pT[:], in1=zT[:, i, :], op=ALU.mult),
                              reads=["pT", ("zT", i, 0), ("zT", i, 1)], writes=[("mixT", 8 + gp * 4 + i)])

                em.barrier()
                cv = Carver()
                cT = cv([128, 8, HALF], F32)
                gluT = cv([128, 30 + HALF], BF16)
                diag = cv([128, CONVW, 128], BF16)
                f5 = [cv([128, 512], F32) for _ in range(3)]
                b5 = [cv([128, 512], BF16) for _ in range(2)]
                mean_t = cv([128, 2, 512], F32)
                rstd_t = cv([128, 2, 512], F32)
                glu_tm = cv([128, 128], F32)
                for c8 in range(8):
                    wbuf, wkey = load_w(w_in[l][:, AG_OFF + c8 * 256:AG_OFF + (c8 + 1) * 256], 256)
                    if do_s:
                        sample_cols(wbuf, wkey, 256, SC_AG + 2 * c8)
                    if half == 0:
                        em.op("pool", lambda e: e.memset(gluT[:, 0:30], 0.0), writes=["gluT_h"])
                    else:
                        em.op("pool", lambda e, c8=c8: e.tensor_copy(out=gluT[:, 0:30], in_=gtail[:, c8, :]),
                              reads=[("gtail", c8)], writes=["gluT_h"])
                    for w_ in range(CONVW):
                        em.op("act", lambda e, w_=w_, c8=c8: e.mul(out=diag[:, w_, :], in_=ident_b[:], mul=cw_t[:, c8, w_:w_ + 1]),
                              reads=["ident_b", "cw_t"], writes=[("diag", w_)])
                    for tq in range(2):
                        pa, pak = next_ps()
                        mm_group(pa[:], [(wbuf[:, k, 0:128], uT[:, k, tq * 512:(tq + 1) * 512]) for k in range(KC)],
                                 reads=uT_all + [wkey], writes=[pak])
                        pg, pgk = next_ps()
                        mm_group(pg[:], [(wbuf[:, k, 128:256], uT[:, k, tq * 512:(tq + 1) * 512]) for k in range(KC)],
                                 reads=uT_all + [wkey], writes=[pgk])
                        em.op("act", lambda e, pg=pg: e.activation(out=f5[0], in_=pg[:], func=AF.Sigmoid), reads=[pgk], writes=["f50"])
                        em.op("dve", lambda e, pa=pa, tq=tq: e.tensor_tensor(out=gluT[:, 30 + tq * 512:30 + (tq + 1) * 512], in0=pa[:], in1=f5[0],
                                                                            op=ALU.mult), reads=[pak, "f50"], writes=[("gluT", tq)])
                    if half == 0 and NH > 1:
                        em.op("pool", lambda e, c8=c8: e.tensor_copy(out=gtail[:, c8, :], in_=gluT[:, HALF:HALF + 30]),
                              reads=[("gluT", 1)], writes=[("gtail", c8)])
                    if half == NH - 1:
                        pq, pqk = next_pcd()
                        mm_group(pq[:, 0:256], [(uT[:, k, HALF - 128:HALF], wbuf[:, k, 0:256]) for k in range(KC)],
                                 reads=uT_all + [wkey], writes=[pqk])
                        em.op("act", lambda e, pq=pq: e.activation(out=f5[1][:, 0:128], in_=pq[:, 128:256], func=AF.Sigmoid),
                              reads=[pqk], writes=["f51"])
                        em.op("dve", lambda e, pq=pq: e.tensor_tensor(out=glu_tm, in0=pq[:, 0:128], in1=f5[1][:, 0:128], op=ALU.mult),
                              reads=[pqk, "f51"], writes=["glu_tm"])
                        em.dma("sp", lambda e, c8=c8, l=l: e.dma_start(out=convp[l, :, c8 * 128:(c8 + 1) * 128], in_=glu_tm[98:128, :]),
                               reads=["glu_tm"])
                    for tq in range(2):
                        pq, pqk = next_pcd()
                        mm_group(pq[:], [(diag[:, w_, :], gluT[:, tq * 512 + w_: tq * 512 + w_ + 512]) for w_ in range(CONVW)],
                                 reads=[("diag", w_) for w_ in range(CONVW)] + ["gluT_h", ("gluT", 0), ("gluT", 1)], writes=[pqk])
                        em.op("act", lambda e, pq=pq, c8=c8, tq=tq: e.activation(out=cT[:, c8, tq * 512:(tq + 1) * 512], in_=pq[:], func=AF.Identity,
                                                                                 bias=ccol_t[:, c8:c8 + 1]),
                              reads=[pqk, "ccol_t"], writes=[("cT", c8, tq)])
                for tq in range(2):
                    sl = slice(tq * 512, (tq + 1) * 512)
                    p1, p1k = next_ps()
                    mm_group(p1[:], [(ones_f[:], cT[:, c8, sl]) for c8 in range(8)],
                             reads=["ones_f"] + [("cT", c8, tq) for c8 in range(8)], writes=[p1k])
                    p2, p2k = next_pcd()
                    for c8 in range(8):
                        fb, fbk = f5[c8 % 2], f"f5{c8 % 2}"
                        em.op("act", lambda e, fb=fb, c8=c8, sl=sl: e.activation(out=fb, in_=cT[:, c8, sl], func=AF.Square),
                              reads=[("cT", c8, tq)], writes=[fbk])
                        em.op("pe", lambda e, fb=fb, c8=c8, p2=p2: e.matmul(p2[:], lhsT=ones_f[:], rhs=fb, start=(c8 == 0), stop=(c8 == 7)),
                              reads=[fbk, "ones_f"], writes=[p2k])
                    em.op("act", lambda e, p1=p1, tq=tq: e.mul(out=mean_t[:, tq, :], in_=p1[:], mul=1.0 / CC), reads=[p1k], writes=[("mean", tq)])
                    em.op("dve", lambda e, tq=tq: e.tensor_tensor(out=f5[2], in0=mean_t[:, tq, :], in1=mean_t[:, tq, :], op=ALU.mult),
                          reads=[("mean", tq)], writes=["f52"])
                    em.op("dve", lambda e, p2=p2, tq=tq: e.scalar_tensor_tensor(out=rstd_t[:, tq, :], in0=p2[:], scalar=1.0 / CC, in1=f5[2],
                                                                               op0=ALU.mult, op1=ALU.subtract),
                          reads=[p2k, "f52"], writes=[("rstd", tq)])
                    em.op("act", lambda e, tq=tq: e.activation(out=rstd_t[:, tq, :], in_=rstd_t[:, tq, :], func=AF.Sqrt, bias=eps_t[:, 0:1]),
                          reads=[("rstd", tq), "eps_t"], writes=[("rstd", tq)])
                    em.op("dve", lambda e, tq=tq: e.reciprocal(out=rstd_t[:, tq, :], in_=rstd_t[:, tq, :]), reads=[("rstd", tq)], writes=[("rstd", tq)])
                for cp in range(4):
                    wbuf, wkey = load_w(w_in[l][:, ZC_OFF + cp * 256:ZC_OFF + (cp + 1) * 256], 256)
                    if do_s:
                        sample_cols(wbuf, wkey, 256, SC_ZC + 2 * cp)
                    for s_ in range(2):
                        c8 = cp * 2 + s_
                        for tq in range(2):
                            sl = slice(tq * 512, (tq + 1) * 512)
                            pz, pzk = next_ps()
                            mm_group(pz[:], [(wbuf[:, k, s_ * 128:(s_ + 1) * 128], uT[:, k, sl]) for k in range(KC)],
                                     reads=uT_all + [wkey], writes=[pzk])
                            em.op("act", lambda e, pz=pz: e.activation(out=b5[0], in_=pz[:], func=AF.Silu), reads=[pzk], writes=["b50"])
                            em.op("dve", lambda e, c8=c8, tq=tq, sl=sl: e.tensor_tensor(out=f5[0], in0=cT[:, c8, sl], in1=mean_t[:, tq, :], op=ALU.subtract),
                                  reads=[("cT", c8, tq), ("mean", tq)], writes=["f50"])
                            em.op("dve", lambda e, tq=tq: e.tensor_tensor(out=f5[0], in0=f5[0], in1=rstd_t[:, tq, :], op=ALU.mult),
                                  reads=["f50", ("rstd", tq)], writes=["f50"])
                            em.op("act", lambda e, c8=c8: e.activation(out=b5[1], in_=f5[0], func=AF.Silu, bias=ccol_t[:, 16 + c8:17 + c8],
                                                                        scale=ccol_t[:, 8 + c8:9 + c8]), reads=["f50", "ccol_t"], writes=["b51"])
                            em.op("pool", lambda e, c8=c8, sl=sl: e.tensor_tensor(out=mixT[:, c8, sl], in0=b5[1], in1=b5[0], op=ALU.mult),
                                  reads=["b50", "b51"], writes=[("mixT", c8)])

                if do_s:
                    sample_layer(l)

                em.barrier()
                cv = Carver()
                ystage = [cv([128, 256], F32) for _ in range(2)]
                junk = cv([128, 256], F32)
                ssp = cv([128, 8, 8], F32)
                yrs = [cv([128, D], F32) for _ in range(2)]
                xt6s = [cv([128, D], F32) for _ in range(2)]
                gpc = cv([128, D], F32)
                rs6 = cv([128, 8], F32)
                em.dma("sp", lambda e, l=l: e.dma_start(out=gpc, in_=gpost[l].partition_broadcast(128)), writes=["gpc"])
                mix_all = [("mixT", k) for k in range(KC)]
                for cj in range(8):
                    wbuf, wkey = load_w(w_out[l][:, cj * 256:(cj + 1) * 256], 256)
                    if do_s:
                        for s_ in range(2):
                            mm_group(pD[:, s_:s_ + 1], [(wbuf[:, k, s_ * 128:(s_ + 1) * 128], mixTs[:, k:k + 1]) for k in range(KC)],
                                     reads=[wkey, "mixTs"], writes=["pD"])
                        em.op("dve", lambda e, cj=cj: e.tensor_copy(out=ysraw[:, 2 * cj:2 * cj + 2], in_=pD[:, 0:2]), reads=["pD"], writes=["ysraw"])
                    for t in range(8):
                        pp, pk = next_ps()
                        mm_group(pp[:, 0:256], [(mixT[:, k, t * 128:(t + 1) * 128], wbuf[:, k, 0:256]) for k in range(KC)],
                                 reads=mix_all + [wkey], writes=[pk])
                        sb_ = (cj * 8 + t) % 2
                        em.op("dve", lambda e, pp=pp, sb_=sb_: e.tensor_copy(out=ystage[sb_], in_=pp[:, 0:256]), reads=[pk], writes=[f"ys{sb_}"])
                        em.op("act", lambda e, sb_=sb_, t=t, cj=cj: e.activation(out=junk, in_=ystage[sb_], func=AF.Square,
                                                                                accum_out=ssp[:, t, cj:cj + 1]), reads=[f"ys{sb_}"], writes=["junk", ("ssp", t)])
                        em.dma("sp", lambda e, sb_=sb_, t=t, cj=cj: e.dma_start(out=yraw[t * 128:(t + 1) * 128, cj * 256:(cj + 1) * 256],
                                                                                in_=ystage[sb_]), reads=[f"ys{sb_}"], writes=[("yraw", t)])
                for t in range(8):
                    T = T0 + t
                    yr, yrk, xt6, xtk = yrs[t % 2], f"yr{t % 2}", xt6s[t % 2], f"xt6{t % 2}"
                    em.dma("sp", lambda e, t=t, yr=yr: e.dma_start(out=yr, in_=yraw[t * 128:(t + 1) * 128, :]), reads=[("yraw", t)], writes=[yrk])
                    em.dma("sp", lambda e, T=T, xt6=xt6: e.dma_start(out=xt6, in_=src[T * 128:(T + 1) * 128, :]), reads=[("resid", T)], writes=[xtk])
                    em.op("dve", lambda e, t=t: e.tensor_reduce(out=rs6[:, t:t + 1], in_=ssp[:, t, :], axis=AX.X, op=ALU.add),
                          reads=[("ssp", t)], writes=[("rs6", t)])
                    em.op("act", lambda e, t=t: e.activation(out=rs6[:, t:t + 1], in_=rs6[:, t:t + 1], func=AF.Sqrt, bias=eps_t[:, 0:1],
                                                             scale=1.0 / D), reads=[("rs6", t), "eps_t"], writes=[("rs6", t)])
                    em.op("dve", lambda e, t=t: e.reciprocal(out=rs6[:, t:t + 1], in_=rs6[:, t:t + 1]), reads=[("rs6", t)], writes=[("rs6", t)])
                    em.op("dve", lambda e, t=t, yr=yr: e.scalar_tensor_tensor(out=yr, in0=yr, scalar=rs6[:, t:t + 1], in1=gpc, op0=ALU.mult, op1=ALU.mult),
                          reads=[yrk, ("rs6", t), "gpc"], writes=[yrk])
                    em.op("dve", lambda e, yr=yr, xt6=xt6: e.tensor_tensor(out=yr, in0=yr, in1=xt6, op=ALU.add), reads=[yrk, xtk], writes=[yrk])
                    dst = yp if last else resid
                    em.dma("sp", lambda e, T=T, dst=dst, yr=yr: e.dma_start(out=dst[T * 128:(T + 1) * 128, :], in_=yr), reads=[yrk],
                           writes=[("resid", T)])
                if do_s:
                    sample_post(l, last)
        em.finish()
    return nc


def prep_shared(inp, DEPTH, SEQ, PAST):
    f = np.float32
    L = DEPTH
    sh = {}
    sh["w_in"] = np.ascontiguousarray(np.asarray(inp["w_in"], f)[:, :, win_perm()])
    sh["w_out"] = np.ascontiguousarray(np.asarray(inp["w_out"], f)[:, wout_perm(), :])
    npre = np.asarray(inp["norm_pre"], f)
    npost = np.asarray(inp["norm_post"], f)
    sh["gpre"] = npre
    sh["gpost"] = npost
    sh["gcols"] = np.ascontiguousarray(np.concatenate([npre.reshape(L, 16, 128).transpose(0, 2, 1),
                                                       npost.reshape(L, 16, 128).transpose(0, 2, 1)], axis=2))
    sh["cwT"] = np.ascontiguousarray(np.asarray(inp["conv_w"], f).reshape(L, CONVW, 8, 128).transpose(0, 3, 2, 1))
    cols = [np.asarray(inp[k], f).reshape(L, 8, 128).transpose(0, 2, 1) for k in ("conv_b", "conv_ln_g", "conv_ln_b")]
    sh["ccols"] = np.ascontiguousarray(np.concatenate(cols, axis=2))
    pos = np.asarray(inp["cmp_pos"], f)
    sh["pos_tm"] = np.ascontiguousarray(np.concatenate([pos, pos], axis=2).transpose(0, 2, 1, 3))
    p32 = pos.reshape(L, 2, 32, 2, 64)
    pdp = np.concatenate([p32] * 4, axis=2)
    sh["pos_dp"] = np.ascontiguousarray(pdp.transpose(0, 2, 3, 1, 4))
    sh["w1"] = np.asarray(inp["cmp_w1"], f)
    sh["w2"] = np.asarray(inp["cmp_w2"], f)
    b1 = np.asarray(inp["cmp_b1"], f)
    b2 = np.asarray(inp["cmp_b2"], f)
    sh["mcols"] = np.ascontiguousarray(np.stack([b1[:, 0], b1[:, 1], np.concatenate([b2[:, 0], b2[:, 0]], axis=1),
                                                 np.concatenate([b2[:, 1], b2[:, 1]], axis=1)], axis=2))
    sh["b2row"] = np.ascontiguousarray(np.concatenate([b2[:, 0], b2[:, 1]], axis=1).reshape(L, 1, 128))
    gb = np.asarray(inp["gate_b"], f)
    sh["gateb"] = gb.reshape(L, 1, 48)
    sh["gatebc"] = gb.reshape(L, 48, 1)
    for k, v in host_consts(SEQ, PAST).items():
        sh["c_" + k] = v
    return sh


def prep_core(inp, c, nb, DEPTH, NPOOL):
    f = np.float32
    m = {}
    m["xp"] = np.asarray(inp["x_prompt"], f)[c % nb]
    m["xs"] = np.asarray(inp["x_sample"], f)[c]
    m["pool"] = np.asarray(inp["cache_kv_pages"], f).reshape(DEPTH * NPOOL * 64, 2048)
    m["winc"] = np.asarray(inp["cache_win"], f)[:, c].reshape(DEPTH, 512, 512)
    sc = np.asarray(inp["state_conv"], f)[:, c]
    m["convN"] = np.ascontiguousarray(sc)
    m["convT"] = np.ascontiguousarray(sc.reshape(DEPTH, 30, 8, 128).transpose(0, 3, 2, 1))
    m["ptab"] = np.ascontiguousarray(np.asarray(inp["page_table"], np.int32)[c:c + 1])
    return m


_NC_CACHE = {}


def kernel(**inputs):
    xpr = np.asarray(inputs["x_prompt"])
    DEPTH = int(np.asarray(inputs["w_in"]).shape[0])
    B, SEQ = int(xpr.shape[0]), int(xpr.shape[1])
    DB = int(np.asarray(inputs["x_sample"]).shape[0])
    NPOOL = int(np.asarray(inputs["cache_kv_pages"]).shape[1])
    PAST = int(np.asarray(inputs["page_table"]).shape[1]) * 128
    key = (DEPTH, SEQ, PAST, NPOOL)
    if key not in _NC_CACHE:
        _NC_CACHE[key] = build(*key)
    nc = _NC_CACHE[key]
    ncores = 8
    assert DB == ncores
    shared = prep_shared(inputs, DEPTH, SEQ, PAST)
    in_maps = []
    for c in range(ncores):
        m = dict(shared)
        m.update(prep_core(inputs, c, B, DEPTH, NPOOL))
        in_maps.append(m)
    res = run_bass_kernel_spmd(nc, in_maps, core_ids=list(range(ncores)))
    R = res.results
    f = np.float32
    y_prompt = np.stack([R[b]["yp"] for b in range(B)]).astype(f)
    y_sample = np.stack([R[c]["ys"] for c in range(DB)]).astype(f)
    kv_prompt = np.stack([R[b]["kvp"] for b in range(B)], axis=1).reshape(DEPTH, B, SEQ, 4, NKV, DH).astype(f)
    kv_sample = np.stack([R[c]["kvs"] for c in range(DB)], axis=1).reshape(DEPTH, DB, 1, 4, NKV, DH).astype(f)
    win_prompt = np.stack([R[b]["winp"] for b in range(B)], axis=1).reshape(DEPTH, B, 512, 2, NKV, DH).astype(f)
    win_sample = np.stack([R[c]["wins"] for c in range(DB)], axis=1).reshape(DEPTH, DB, 512, 2, NKV, DH).astype(f)
    conv_prompt = np.stack([R[b]["convp"] for b in range(B)], axis=1).astype(f)
    conv_sample = np.stack([R[c]["convs"] for c in range(DB)], axis=1).astype(f)
    return (y_prompt, y_sample, kv_prompt, kv_sample, win_prompt, win_sample, conv_prompt, conv_sample)
```

```python
import math
import types
from contextlib import ExitStack
import numpy as np
import concourse.bass as bass
import concourse.mybir as mybir
from concourse.bass_utils import run_bass_kernel_spmd

F32 = mybir.dt.float32
BF16 = mybir.dt.bfloat16
I32 = mybir.dt.int32
ALU = mybir.AluOpType
AF = mybir.ActivationFunctionType
AX = mybir.AxisListType

D = 2048
KC = 16
H = 16
DH = 64
NKV = 4
CC = 1024
NIN = 6704
HALF = 1024
WINDOW = 512
TOPN = 16
CONVW = 31
EPS = 1e-6
BIG = 1e9
MASK_NEG = 30000.0
ROPE_DIM = 16
ROPE_THETA = 500000.0
NDMA_SEM = 8

KV_OFF = 0
GL_OFF = 1536
Q_OFF = 1584
Z_OFF = Q_OFF + 1024
AG_OFF = Z_OFF + 1024
ZC_OFF = AG_OFF + 2048
SC_KV = 0
SC_GL = 12
SC_Q = 13
SC_Z = 21
SC_AG = 29
SC_ZC = 45
NSC = 53


def win_perm():
    perm = np.zeros(NIN, dtype=np.int64)
    perm[0:1536] = 5120 + np.arange(1536)
    perm[1536:1584] = 6656 + np.arange(48)
    for gp in range(2):
        for i in range(4):
            for e in range(2):
                h = (2 * gp + e) * 4 + i
                dst = gp * 512 + i * 128 + e * 64
                perm[Q_OFF + dst:Q_OFF + dst + 64] = 3072 + h * 64 + np.arange(64)
                perm[Z_OFF + dst:Z_OFF + dst + 64] = 4096 + h * 64 + np.arange(64)
    for c8 in range(8):
        perm[AG_OFF + c8 * 256:AG_OFF + c8 * 256 + 128] = c8 * 128 + np.arange(128)
        perm[AG_OFF + c8 * 256 + 128:AG_OFF + c8 * 256 + 256] = 1024 + c8 * 128 + np.arange(128)
    perm[ZC_OFF:ZC_OFF + 1024] = 2048 + np.arange(1024)
    return perm


def wout_perm():
    perm = np.zeros(D, dtype=np.int64)
    perm[0:1024] = np.arange(1024)
    for gp in range(2):
        for i in range(4):
            for e in range(2):
                h = (2 * gp + e) * 4 + i
                dst = 1024 + (gp * 4 + i) * 128 + e * 64
                perm[dst:dst + 64] = 1024 + h * 64 + np.arange(64)
    return perm


def _freeze(fn):
    if fn.__closure__ is None:
        return fn
    cells = []
    for c in fn.__closure__:
        try:
            cells.append(types.CellType(c.cell_contents))
        except ValueError:
            cells.append(c)
    g = types.FunctionType(fn.__code__, fn.__globals__, fn.__name__, fn.__defaults__, tuple(cells))
    g.__kwdefaults__ = fn.__kwdefaults__
    return g


class Em:
    COMPUTE = ("pe", "act", "dve", "pool")

    def __init__(self, nc, stack):
        self.nc = nc
        self.eng_names = ("pe", "act", "dve", "pool", "sp")
        self.prog = {e: [] for e in self.eng_names}
        self.tick = {e: 0 for e in self.COMPUTE}
        self.sem = {e: stack.enter_context(nc.semaphore("tk_" + e)) for e in self.COMPUTE}
        self.dq = ("sp", "pool")
        self.dsem = {q: [stack.enter_context(nc.semaphore(f"d_{q}{i}")) for i in range(NDMA_SEM)]
                     for q in self.dq}
        self.dcount = {q: 0 for q in self.dq}
        self.seen = {e: {} for e in self.eng_names}
        self.buf = {}
        self.n_inst = 0

    def _need(self, eng, tok, waits):
        if tok is None:
            return
        if tok[0] == "c":
            key, val = ("c", tok[1]), tok[2]
        else:
            key, val = ("d", tok[1], tok[2]), tok[3]
        if self.seen[eng].get(key, 0) >= val:
            return
        waits[key] = max(waits.get(key, 0), val)

    PSUM_KEYS = frozenset(("pA", "pB", "pC", "pD", "pO0", "pO1", "pO2", "pT"))

    def _split(self, reads, writes):
        r2 = [k for k in reads if k not in self.PSUM_KEYS]
        w2 = list(writes) + [k for k in reads if k in self.PSUM_KEYS]
        return r2, w2

    def _deps(self, eng, reads, writes):
        waits = {}
        for k in reads:
            st = self.buf.get(k)
            if st is not None:
                self._need(eng, st["w"], waits)
        for k in writes:
            st = self.buf.get(k)
            if st is not None:
                self._need(eng, st["w"], waits)
                for t in st["r"]:
                    self._need(eng, t, waits)
        for key, val in waits.items():
            self.seen[eng][key] = val
        return waits

    def _commit(self, tok, reads, writes):
        src = tok[:2] if tok[0] == "c" else tok[:3]
        for k in reads:
            st = self.buf.setdefault(k, {"w": None, "r": []})
            st["r"] = [t for t in st["r"] if (t[:2] if t[0] == "c" else t[:3]) != src] + [tok]
        for k in writes:
            self.buf[k] = {"w": tok, "r": []}

    def op(self, eng, fns, reads=(), writes=()):
        if callable(fns):
            fns = [fns]
        fns = [_freeze(f_) for f_ in fns]
        reads, writes = self._split(reads, writes)
        waits = self._deps(eng, reads, writes)
        self.tick[eng] += 1
        t = self.tick[eng]
        tok = ("c", eng, t)
        self.prog[eng].append((waits, fns, tok))
        self._commit(tok, reads, writes)
        self.n_inst += len(fns)

    def dma(self, q, fn, reads=(), writes=()):
        reads, writes = self._split(reads, writes)
        waits = self._deps(q, reads, writes)
        i = self.dcount[q]
        self.dcount[q] += 1
        j = i % NDMA_SEM
        val = 16 * (i // NDMA_SEM + 1)
        if i >= NDMA_SEM:
            key = ("d", q, j)
            prev = val - 16
            if self.seen[q].get(key, 0) < prev:
                waits[key] = max(waits.get(key, 0), prev)
                self.seen[q][key] = prev
        tok = ("d", q, j, val)
        self.prog[q].append((waits, [_freeze(fn)], tok))
        self._commit(tok, reads, writes)
        self.n_inst += 1
        return tok

    def _all_waits(self):
        waits = {}
        for q in self.dq:
            n = self.dcount[q]
            for j in range(min(n, NDMA_SEM)):
                waits[("d", q, j)] = 16 * ((n - 1 - j) // NDMA_SEM + 1)
        for e in self.COMPUTE:
            if self.tick[e] > 0:
                waits[("c", e)] = self.tick[e]
        return waits

    def barrier(self):
        allw = self._all_waits()
        for e in self.eng_names:
            w = {k: v for k, v in allw.items() if self.seen[e].get(k, 0) < v}
            for k, v in w.items():
                self.seen[e][k] = v
            if w:
                self.prog[e].append((w, [], None))
        self.buf = {}

    def _sem_of(self, key):
        if key[0] == "c":
            return self.sem[key[1]]
        return self.dsem[key[1]][key[2]]

    def finish(self, final_eng="sp"):
        self.prog[final_eng].append((self._all_waits(), [], None))
        nc = self.nc
        em = self

        def replay(name, engine):
            for waits, fns, tok in em.prog[name]:
                for key, val in waits.items():
                    engine.wait_ge(em._sem_of(key), val)
                n = len(fns)
                for i, fn in enumerate(fns):
                    ins = fn(engine)
                    if i == n - 1 and tok is not None:
                        if tok[0] == "c":
                            ins.then_inc(em.sem[tok[1]], 1)
                        else:
                            ins.then_inc(em.dsem[tok[1]][tok[2]], 16)

        with nc.Block() as block:
            @block.tensor
            def _(e):
                replay("pe", e)

            @block.scalar
            def _(e):
                replay("act", e)

            @block.vector
            def _(e):
                replay("dve", e)

            @block.gpsimd
            def _(e):
                replay("pool", e)

            @block.sync
            def _(e):
                replay("sp", e)


def rope_inv():
    half = ROPE_DIM // 2
    return (ROPE_THETA ** (-np.arange(half, dtype=np.float32) / half)).astype(np.float32)


def host_consts(SEQ, PAST):
    NT = SEQ // 128
    NBLK_S = PAST // 64
    c = {}
    a = np.arange(128)[:, None]
    b = np.arange(512)[None, :]
    masks = np.zeros((128, 8, 512), np.float32)
    for m in range(4):
        masks[:, m, :] = (b - a >= 128 * m)
    for m in range(4, 8):
        Dd = m - 3
        masks[:, m, :] = (b - a < 512 - 128 * Dd)
    c["masks"] = (masks - 1.0) * MASK_NEG
    cm = np.zeros((16, 2, 512), np.float32)
    i16 = np.arange(16)[:, None]
    for r in range(2):
        cm[:, r, :] = (64 * i16 + 63 <= 512 * r + b)
    c["cmaskc"] = cm
    inv = rope_inv()
    pos = np.arange(SEQ, dtype=np.float32)
    ang = pos[:, None] * inv[None, :]
    cosv, sinv = np.cos(ang).astype(np.float32), np.sin(ang).astype(np.float32)
    c["cos_tm"] = np.ascontiguousarray(cosv.reshape(NT, 128, 8).transpose(1, 0, 2))
    c["sin_tm"] = np.ascontiguousarray(sinv.reshape(NT, 128, 8).transpose(1, 0, 2))
    Cf = np.ones((128, SEQ), np.float32)
    Sf = np.zeros((128, SEQ), np.float32)
    for e in range(2):
        for i in range(8):
            Cf[e * 64 + i] = cosv[:, i]
            Cf[e * 64 + 8 + i] = cosv[:, i]
            Sf[e * 64 + i] = -sinv[:, i]
            Sf[e * 64 + 8 + i] = sinv[:, i]
    c["rope_c"] = Cf
    c["rope_s"] = Sf
    angs = np.float32(PAST) * inv
    cs, ss = np.cos(angs).astype(np.float32), np.sin(angs).astype(np.float32)
    Cs = np.ones((128, 1), np.float32)
    Ss = np.zeros((128, 1), np.float32)
    for e in range(2):
        for i in range(8):
            Cs[e * 64 + i] = cs[i]
            Cs[e * 64 + 8 + i] = cs[i]
            Ss[e * 64 + i] = -ss[i]
            Ss[e * 64 + 8 + i] = ss[i]
    c["rope_cs"] = np.concatenate([Cs, Ss], axis=1)
    perm = np.zeros((128, 128), np.float32)
    for m in range(128):
        dd = m % 64
        if dd < 8:
            perm[m + 8, m] = 1.0
        elif dd < 16:
            perm[m - 8, m] = 1.0
    c["perm"] = perm
    blk = np.zeros((128, 128), np.float32)
    blk[0:64, 0:64] = 1.0
    blk[64:128, 64:128] = 1.0
    c["blkones"] = blk
    NBP = SEQ // 64
    ex = np.zeros((128, NT, 128), np.float32)
    for kt in range(NT):
        for key in range(128):
            n = 2 * kt + key // 64
            if n < 32:
                ex[n, kt, key] = 1.0
    c["expand"] = ex
    cm_tm = np.zeros((128, 8, 32), np.float32)
    keep = np.zeros((128, 8, 32), np.float32)
    add = np.zeros((128, 8, 32), np.float32)
    n = np.arange(32)[None, :]
    for t in range(8):
        tp = ((NT - 8 + t) * 128 + np.arange(128))[:, None]
        cur = tp // 64
        cm_tm[:, t, :] = ((n + 1) * 64 - 1 <= tp)
        forced = (n == 0) | (n == cur) | (n == cur - 1)
        valid = n <= cur
        keep[:, t, :] = valid & ~forced
        add[:, t, :] = np.where(valid, np.where(forced, BIG, 0.0), -BIG)
    c["cm_tm"], c["keep_tm"], c["add_tm"] = cm_tm, keep, add
    NS = NBLK_S + 8
    ks_ = np.zeros((4, NS), np.float32)
    as_ = np.full((4, NS), -BIG, np.float32)
    for nn in range(NBLK_S + 1):
        forced = nn in (0, NBLK_S, NBLK_S - 1)
        ks_[:, nn] = 0.0 if forced else 1.0
        as_[:, nn] = BIG if forced else 0.0
    c["keep_s"], c["add_s"] = ks_, as_
    ind4 = np.zeros((4, 128), np.float32)
    for p in range(128):
        plo, r = p // 64, p % 64
        ind4[plo * 2 + r // 32, p] = 1.0
    c["ind4"] = ind4
    c["iota64"] = (np.arange(128) % 64).astype(np.float32).reshape(128, 1)
    return c


CONST_SHAPES = None


def build(DEPTH, SEQ, PAST, NPOOL):
    NT = SEQ // 128
    NH = SEQ // HALF
    NPAGES = PAST // 128
    NDP = NPAGES // 2
    NBLK_S = PAST // 64
    NBT = (NBLK_S + 127) // 128
    NS = NBLK_S + 8
    NBP = SEQ // 64
    assert NDP % 8 == 0 and SEQ % HALF == 0 and NBP <= 32
    nc = bass.Bass("TRN2", target_bir_lowering=False)
    st = ExitStack()
    consts = host_consts(SEQ, PAST)
    with st:
        def din(name, shape, dt=F32):
            return nc.dram_tensor(name, list(shape), dt, kind="ExternalInput").ap()

        def dout(name, shape, dt=F32):
            return nc.dram_tensor(name, list(shape), dt, kind="ExternalOutput").ap()

        def dtmp(name, shape, dt=F32):
            return nc.dram_tensor(name, list(shape), dt).ap()

        xp = din("xp", [SEQ, D])
        xs = din("xs", [1, D])
        pool = din("pool", [DEPTH * NPOOL * 64, 2048])
        winc = din("winc", [DEPTH, 512, 512])
        convT = din("convT", [DEPTH, 128, 8, 30])
        convN = din("convN", [DEPTH, 30, 1024])
        ptab = din("ptab", [1, NPAGES], I32)
        w_in = din("w_in", [DEPTH, D, NIN])
        w_out = din("w_out", [DEPTH, D, D])
        gpre = din("gpre", [DEPTH, D])
        gpost = din("gpost", [DEPTH, D])
        gcols = din("gcols", [DEPTH, 128, 32])
        cwT = din("cwT", [DEPTH, 128, 8, CONVW])
        ccols = din("ccols", [DEPTH, 128, 24])
        pos_tm_d = din("pos_tm", [DEPTH, 128, 2, 64])
        pos_dp_d = din("pos_dp", [DEPTH, 128, 2, 2, 64])
        w1 = din("w1", [DEPTH, 2, 4096, 128])
        w2 = din("w2", [DEPTH, 2, 128, 64])
        mcols = din("mcols", [DEPTH, 128, 4])
        b2row = din("b2row", [DEPTH, 1, 128])
        gateb = din("gateb", [DEPTH, 1, 48])
        gatebc = din("gatebc", [DEPTH, 48, 1])
        cd = {k: din("c_" + k, v.shape) for k, v in consts.items()}

        yp = dout("yp", [SEQ, D])
        ys = dout("ys", [1, D])
        kvp = dout("kvp", [DEPTH, SEQ, 1024])
        kvs = dout("kvs", [DEPTH, 1, 1024])
        winp = dout("winp", [DEPTH, 512, 512])
        wins = dout("wins", [DEPTH, 512, 512])
        convp = dout("convp", [DEPTH, 30, 1024])
        convs = dout("convs", [DEPTH, 30, 1024])
        resid = dtmp("resid", [SEQ, D])
        yraw = dtmp("yraw", [HALF, D])
        selscr = dtmp("selscr", [4, NBLK_S])

        em = Em(nc, st)

        def sb(name, shape, dt):
            return st.enter_context(nc.sbuf_tensor(name, list(shape), dt))

        def ps(name, shape, dt=F32):
            return st.enter_context(nc.psum_tensor(name, list(shape), dt))

        uT = sb("uT", [128, KC, HALF], BF16)
        mixT = sb("mixT", [128, KC, HALF], BF16)
        wb = [sb(f"wb{i}", [128, KC, 256], BF16) for i in range(2)]
        ksT = sb("ksT", [128, 2, SEQ], BF16)
        kwT = sb("kwT", [128, 2, SEQ], BF16)
        vsA = sb("vsA", [128, NT, 4, 65], BF16)
        vwA = sb("vwA", [128, NT, 4, 65], BF16)
        masks = sb("masks", [128, 8, 512], BF16)
        cmaskc = sb("cmaskc", [16, 2, 512], BF16)
        kcmpT = sb("kcmpT", [128, 2, 32], BF16)
        vcmp = [sb(f"vcmp{i}", [16, 4, 65], BF16) for i in range(2)]
        ident_b = sb("ident_b", [128, 128], BF16)
        ident_f = sb("ident_f", [128, 128], F32)
        ones_f = sb("ones_f", [128, 128], F32)
        ones_b = sb("ones_b", [128, 128], BF16)
        perm_b = sb("perm_b", [128, 128], BF16)
        perm_f = sb("perm_f", [128, 128], F32)
        blk_f = sb("blk_f", [128, 128], F32)
        expand = sb("expand", [128, NT, 128], BF16)
        cos_tm = sb("cos_tm", [128, NT, 8], F32)
        sin_tm = sb("sin_tm", [128, NT, 8], F32)
        cm_tm = sb("cm_tm", [128, 8, 32], F32)
        keep_tm = sb("keep_tm", [128, 8, 32], F32)
        add_tm = sb("add_tm", [128, 8, 32], F32)
        keep_s = sb("keep_s", [4, NS], F32)
        add_s = sb("add_s", [4, NS], F32)
        ind4 = sb("ind4", [4, 128], F32)
        iota64 = sb("iota64", [128, 1], F32)
        rope_cs = sb("rope_cs", [128, 2], F32)
        gates = sb("gates", [128, 8, 48], F32)
        gtail = sb("gtail", [128, 8, 30], BF16)
        gcol_t = sb("gcol_t", [128, 32], F32)
        cw_t = sb("cw_t", [128, 8, CONVW], F32)
        ccol_t = sb("ccol_t", [128, 24], F32)
        mcol_t = sb("mcol_t", [128, 4], F32)
        w2k_lo = sb("w2k_lo", [128, 128], BF16)
        w2k_hi = sb("w2k_hi", [128, 128], BF16)
        w2kn = sb("w2kn", [128, 64], BF16)
        w2v = sb("w2v", [128, 64], BF16)
        b2bc = sb("b2bc", [32, 128], F32)
        gateb_bc = sb("gateb_bc", [128, 48], F32)
        gateb_c = sb("gateb_c", [48, 1], F32)
        xsT = sb("xsT", [128, KC], F32)
        uTs = sb("uTs", [128, KC], BF16)
        projs = sb("projs", [128, NSC], F32)
        mixTs = sb("mixTs", [128, KC], BF16)
        ysraw = sb("ysraw", [128, KC], F32)
        idxb = sb("idxb", [128, NDP], F32)
        idxq = [sb(f"idxq{q_}", [128, NDP], I32) for q_ in range(4)]
        small = sb("small", [128, 64], F32)
        eps_t = sb("eps_t", [128, 1], F32)
        arena = sb("arena", [128, 32768], BF16)

        pA = ps("pA", [128, 512])
        pB = ps("pB", [128, 512])
        pC = ps("pC", [128, 512])
        pD = ps("pD", [128, 512])
        pO = [ps(f"pO{i}", [128, 512]) for i in range(3)]
        pT = ps("pT", [128, 1024], BF16)

        def carve(off, shape, dt):
            esz = 2 if dt == BF16 else 4
            n = int(np.prod(shape[1:]))
            nb = n * esz
            assert off % 4 == 0 and off + nb <= 65536, (off, nb)
            v = arena[0:shape[0], off // 2: off // 2 + nb // 2]
            if dt != BF16:
                v = v.bitcast(dt)
            if len(shape) == 3:
                v = v.rearrange("p (a b) -> p a b", b=shape[2])
            elif len(shape) == 4:
                v = v.rearrange("p (a b c) -> p a b c", b=shape[2], c=shape[3])
            elif len(shape) == 5:
                v = v.rearrange("p (a b c d) -> p a b c d", b=shape[2], c=shape[3], d=shape[4])
            return v, off + ((nb + 31) // 32) * 32

        class Carver:
            def __init__(self):
                self.off = 0

            def __call__(self, shape, dt):
                v, self.off = carve(self.off, shape, dt)
                return v

        def load_f32(tile_ap, src, key):
            em.dma("sp", lambda e: e.dma_start(out=tile_ap, in_=src), writes=[key])

        def load_cast(tile_ap, src, key):
            em.dma("pool", lambda e: e.dma_start(out=tile_ap, in_=src), writes=[key])

        load_cast(masks[:], cd["masks"], "masks")
        load_cast(cmaskc[:], cd["cmaskc"], "cmaskc")
        load_cast(expand[:], cd["expand"], "expand")
        load_cast(perm_b[:], cd["perm"], "perm_b")
        load_f32(perm_f[:], cd["perm"], "perm_f")
        load_f32(blk_f[:], cd["blkones"], "blk_f")
        load_f32(cos_tm[:], cd["cos_tm"], "cos_tm")
        load_f32(sin_tm[:], cd["sin_tm"], "sin_tm")
        load_f32(cm_tm[:], cd["cm_tm"], "cm_tm")
        load_f32(keep_tm[:], cd["keep_tm"], "keep_tm")
        load_f32(add_tm[:], cd["add_tm"], "add_tm")
        load_f32(keep_s[:], cd["keep_s"], "keep_s")
        load_f32(add_s[:], cd["add_s"], "add_s")
        load_f32(ind4[:], cd["ind4"], "ind4")
        load_f32(iota64[:], cd["iota64"], "iota64")
        load_f32(rope_cs[:], cd["rope_cs"], "rope_cs")
        em.op("pool", lambda e: e.memset(ident_f[:], 0.0), writes=["ident_f"])
        em.op("pool", lambda e: e.affine_select(out=ident_f[:], in_=ident_f[:], pattern=[[-1, 128]],
                                                 compare_op=ALU.not_equal, fill=1.0, base=0,
                                                 channel_multiplier=1),
              reads=["ident_f"], writes=["ident_f"])
        em.op("dve", lambda e: e.tensor_copy(out=ident_b[:], in_=ident_f[:]), reads=["ident_f"], writes=["ident_b"])
        em.op("dve", lambda e: e.memset(ones_f[:], 1.0), writes=["ones_f"])
        em.op("dve", lambda e: e.memset(eps_t[:], EPS), writes=["eps_t"])
        em.op("dve", lambda e: e.memset(ones_b[:], 1.0), writes=["ones_b"])
        em.op("pool", lambda e: e.memset(vsA[:], 1.0), writes=["vsA"])
        em.op("pool", lambda e: e.memset(vwA[:], 1.0), writes=["vwA"])
        for i in range(2):
            em.op("pool", lambda e, i=i: e.memset(vcmp[i][:], 1.0), writes=[f"vcmp{i}"])
        em.op("pool", lambda e: e.memset(w2k_lo[:], 0.0), writes=["w2k_lo"])
        em.op("pool", lambda e: e.memset(w2k_hi[:], 0.0), writes=["w2k_hi"])
        em.dma("sp", lambda e: e.dma_start(out=xsT[:], in_=xs.rearrange("o (k p) -> p (o k)", p=128),
                                           allow_slow_non_contiguous=True), writes=["xsT"])
        with_c = Carver()
        ptb_i = with_c([128, NPAGES], I32)
        ptb_f = with_c([128, NPAGES], F32)
        em.dma("sp", lambda e: e.dma_start(out=ptb_i, in_=ptab[0].partition_broadcast(128)), writes=["ptb_i"])
        em.op("dve", lambda e: e.tensor_copy(out=ptb_f, in_=ptb_i), reads=["ptb_i"], writes=["ptb_f"])
        pv = ptb_f.rearrange("p (u two) -> p u two", two=2)
        em.op("dve", lambda e: e.tensor_scalar(out=idxb[0:64, :], in0=pv[0:64, :, 0], scalar1=64.0,
                                                scalar2=iota64[0:64, 0:1], op0=ALU.mult, op1=ALU.add),
              reads=["ptb_f", "iota64"], writes=["idxb_lo"])
        em.op("dve", lambda e: e.tensor_scalar(out=idxb[64:128, :], in0=pv[64:128, :, 1], scalar1=64.0,
                                                scalar2=iota64[64:128, 0:1], op0=ALU.mult, op1=ALU.add),
              reads=["ptb_f", "iota64"], writes=["idxb_hi"])
        em.barrier()

        wcount = [0]

        pre_w = {}

        def prefetch_w(tag, src_ap, ncols):
            pre_w[tag] = load_w(src_ap, ncols)

        def load_w(src_ap, ncols, tag=None):
            if tag is not None and tag in pre_w:
                return pre_w.pop(tag)
            i = wcount[0] % 2
            wcount[0] += 1
            key = f"wb{i}"
            em.dma("pool", lambda e: e.dma_start(out=wb[i][:, :, 0:ncols],
                                                  in_=src_ap.rearrange("(k p) n -> p k n", p=128)),
                   writes=[key])
            return wb[i], key

        psw = [0]

        def next_ps():
            i = psw[0] % 2
            psw[0] += 1
            return (pA, "pA") if i == 0 else (pB, "pB")

        pcd = [0]

        def next_pcd():
            i = pcd[0] % 2
            pcd[0] += 1
            return (pC, "pC") if i == 0 else (pD, "pD")

        alt = [0]

        def evac_eng():
            alt[0] += 1
            return "act" if alt[0] % 2 == 0 else "dve"

        def copy_op(eng, out, in_, reads, writes):
            if eng == "act":
                em.op("act", lambda e: e.copy(out=out, in_=in_), reads=reads, writes=writes)
            else:
                em.op(eng, lambda e: e.tensor_copy(out=out, in_=in_), reads=reads, writes=writes)

        def mm_group(out_ap, pairs, reads, writes):
            n = len(pairs)
            fns = []
            for i, (l_, r_) in enumerate(pairs):
                fns.append(lambda e, l_=l_, r_=r_, i=i: e.matmul(out_ap, lhsT=l_, rhs=r_,
                                                                start=(i == 0), stop=(i == n - 1)))
            em.op("pe", fns, reads=reads, writes=writes)

        def sample_cols(wbuf, wkey, ncols, chunk0):
            nsub = (ncols + 127) // 128
            for s_ in range(nsub):
                m = min(128, ncols - s_ * 128)
                mm_group(pD[0:m, s_:s_ + 1],
                         [(wbuf[:, k, s_ * 128:s_ * 128 + m], uTs[:, k:k + 1]) for k in range(KC)],
                         reads=[wkey, "uTs"], writes=["pD"])
            m_all = min(128, ncols)
            em.op("dve", lambda e: e.tensor_copy(out=projs[0:m_all, chunk0:chunk0 + nsub], in_=pD[0:m_all, 0:nsub]),
                  reads=["pD"], writes=["projs"])

        def mlp_feed(Xp_ap, xpk, XT_ap, slot):
            em.op("pe", [lambda e, ck=ck: e.transpose(
                out=pT[:, ck * 128:(ck + 1) * 128],
                in_=Xp_ap[:, ck // 4, ck % 4, :],
                identity=ident_b[:]) for ck in range(8)], reads=[xpk, "ident_b"], writes=["pT"])
            copy_op(evac_eng(), XT_ap[:, :, slot, :], pT[:].rearrange("p (a b) -> p a b", b=128), ["pT"], [("XT", slot)])

        def mlp_stage1(XT_ap, nslots, hid_ap, W1_ap):
            nb = nslots * 4
            for ck in range(8):
                rhs_all = XT_ap[:, ck, 0:nslots, :].rearrange("p s (n j) -> p (s n) j", j=32)
                mm_group(pC[:, ck * 32:ck * 32 + nb],
                         [(W1_ap[:, ck // 4, jh, :], rhs_all[:, :, jh]) for jh in range(32)],
                         reads=["W1s"] + [("XT", s_) for s_ in range(nslots)], writes=["pC"])
            for c_ in range(2):
                em.op("act", lambda e, c_=c_: e.activation(
                    out=hid_ap[:, c_ * 4:(c_ + 1) * 4, 0:nb],
                    in_=pC[:, c_ * 128:(c_ + 1) * 128].rearrange("p (k n) -> p k n", n=32)[:, :, 0:nb],
                    func=AF.Silu, bias=mcol_t[:, c_:c_ + 1]), reads=["pC", "mcol_t"], writes=["hid"])
            return nb

        pool4 = pool.rearrange("n (q x) -> (n q) x", q=4)

        def col_op(out, in0, in1, op, reads, writes, eng="dve"):
            em.op(eng, lambda e: e.tensor_tensor(out=out, in0=in0, in1=in1, op=op), reads=reads, writes=writes)

        def sample_layer(l):
            em.barrier()
            cv = Carver()
            cmp_s = cv([128, NBT, 8, 64], F32)
            cmpV = cv([128, NBT, 256], BF16)
            D32 = [cv([128, 2, 512], F32) for _ in range(3)]
            Xs = [cv([128, 2, 4, 128], BF16) for _ in range(2)]
            pos_dp = cv([128, 2, 2, 64], F32)
            XT8 = cv([128, 8, 8, 128], BF16)
            W1s = cv([128, 2, 32, 128], BF16)
            hid = cv([128, 8, 32], BF16)
            stg = cv([32, 512], F32)
            load_f32(pos_dp, pos_dp_d[l], "pos_dp")
            em.dma("pool", lambda e: e.dma_start(out=W1s, in_=w1[l].rearrange("c (jh p) e -> p c jh e", p=128)),
                   writes=["W1s"])
            for u in range(NDP):
                b_ = u % 3
                for jl in range(2):
                    em.dma("pool", lambda e, u=u, b_=b_, jl=jl: e.indirect_dma_start(
                        out=D32[b_][:, jl, :], out_offset=None, in_=pool4,
                        in_offset=bass.IndirectOffsetOnAxis(ap=idxq[2 * jl][:, u:u + 1], axis=0)),
                           reads=[("idxq", 2 * jl)], writes=[f"D32{b_}"])
                em.op("dve", lambda e, b_=b_, u=u: e.tensor_tensor(
                    out=Xs[u % 2].rearrange("p c k (j d) -> p j c k d", d=64),
                    in0=D32[b_].rearrange("p j (c k d) -> p j c k d", c=2, d=64),
                    in1=pos_dp.unsqueeze(3).to_broadcast([128, 2, 2, 4, 64]), op=ALU.add),
                      reads=[f"D32{b_}", "pos_dp"], writes=[f"Xs{u % 2}"])
                mlp_feed(Xs[u % 2], f"Xs{u % 2}", XT8, u % 8)
                if u % 8 == 7:
                    bt = u // 8
                    nb = mlp_stage1(XT8, 8, hid, W1s)
                    pp, pk = next_ps()
                    for ck in range(8):
                        mm_group(pp[0:32, ck * 64:(ck + 1) * 64], [(hid[:, ck, 0:32], (w2kn if ck < 4 else w2v)[:])],
                                 reads=["hid", "w2kn", "w2v"], writes=[pk])
                    em.op("dve", lambda e, pp=pp: e.tensor_tensor(
                        out=stg.rearrange("p (c k d) -> p c k d", c=2, d=64),
                        in0=pp[0:32, :].rearrange("p (c k d) -> p c k d", c=2, d=64),
                        in1=b2bc[0:32, :].rearrange("p (c d) -> p c d", d=64).unsqueeze(2).to_broadcast([32, 2, 4, 64]),
                        op=ALU.add), reads=[pk, "b2bc"], writes=["stg"])
                    tl, po = (bt * 32) // 128, (bt * 32) % 128
                    em.dma("sp", lambda e, tl=tl, po=po: e.dma_start(
                        out=cmp_s[po:po + 32, tl, :, :].rearrange("p a b -> p (a b)"), in_=stg), reads=["stg"], writes=[("cmp_s", tl)])
            for tl in range(NBT):
                P = min(128, NBLK_S - tl * 128)
                em.op("dve", lambda e, tl=tl, P=P: e.tensor_copy(out=cmpV[0:P, tl, :].rearrange("p (k d) -> p k d", d=64),
                                                                 in_=cmp_s[0:P, tl, 4:8, :]),
                      reads=[("cmp_s", tl)], writes=[("cmpV", tl)])
            em.barrier()
            cv = Carver()
            cmp_s = cv([128, NBT, 8, 64], F32)
            cmpV = cv([128, NBT, 256], BF16)
            qbc_r = cv([128, 1024], F32)
            qbc_c = cv([128, 1024], F32)
            prod = cv([128, 2048], F32)
            D32b = [cv([128, 2, 512], F32) for _ in range(3)]
            KVb = [cv([128, 2, 512], BF16) for _ in range(2)]
            KT4 = cv([128, 4, 128], BF16)
            Qblk = cv([128, 2, 8], BF16)
            wtile = cv([128, 4, 512], F32)
            wV = cv([128, 4, 256], BF16)
            selexp = cv([128, NDP, 4], F32)
            sE = cv([128, 2, 16], F32)
            sEb = cv([128, 2, 16], BF16)
            Ec_all = cv([128, NBT, 16], F32)
            Qd = cv([128, 128], F32)
            cols = cv([128, 64], F32)
            apT = cv([128, 8, CONVW], F32)
            tmp31 = cv([128, 8, CONVW], F32)
            rin = cv([128, 12], F32)
            rr = cv([128, 12], F32)
            kvrow = cv([128, 8], F32)
            wrow = cv([128, 4], F32)
            srow = cv([4, NS], F32)
            srow2 = cv([4, NS], F32)
            selrow = cv([4, NS], F32)
            m16 = cv([4, 16], F32)
            selR = cv([4, NDP, 4], F32)
            imp_n = cv([128, NBT, 4], F32)
            denc = cv([128, 3, 16], F32)
            numc = cv([128, 3, 8], F32)
            dcol = cv([128, 3, 8], F32)
            gcol = cv([128, 8, 3], F32)
            gbc = cv([128, 48], F32)
            Gd = cv([48, 48], F32)
            gsc = cv([48, 1], F32)
            ocol = cv([128, 8], F32)
            agv = projs[:, SC_AG:SC_AG + 16].rearrange("p (c two) -> p c two", two=2)
            em.op("act", lambda e: e.activation(out=cols[:, 0:8], in_=agv[:, :, 1], func=AF.Sigmoid), reads=["projs"], writes=["c_sg"])
            col_op(cols[:, 8:16], agv[:, :, 0], cols[:, 0:8], ALU.mult, ["projs", "c_sg"], ["c_glu"])
            em.dma("sp", lambda e: e.dma_start(out=apT[:, :, 0:30], in_=convT[l]), writes=["apT_h"])
            em.op("dve", lambda e: e.tensor_copy(out=apT[:, :, 30], in_=cols[:, 8:16]), reads=["c_glu"], writes=["apT_n"])
            col_op(tmp31, apT, cw_t[:], ALU.mult, ["apT_h", "apT_n", "cw_t"], ["tmp31"])
            em.op("dve", lambda e: e.tensor_reduce(out=cols[:, 16:24], in_=tmp31, axis=AX.X, op=ALU.add), reads=["tmp31"], writes=["c_cs"])
            col_op(cols[:, 16:24], cols[:, 16:24], ccol_t[:, 0:8], ALU.add, ["c_cs", "ccol_t"], ["c_cs"])
            em.dma("sp", lambda e: e.dma_start(out=convs[l, 0:29, :], in_=convN[l, 1:30, :]))
            em.dma("sp", lambda e: e.dma_start(out=convs[l, 29, :].rearrange("(c p) -> p c", p=128), in_=cols[:, 8:16],
                                               allow_slow_non_contiguous=True), reads=["c_glu"])
            em.op("dve", lambda e: e.tensor_reduce(out=cols[:, 24:25], in_=cols[:, 16:24], axis=AX.X, op=ALU.add), reads=["c_cs"], writes=["c_p1"])
            col_op(cols[:, 32:40], cols[:, 16:24], cols[:, 16:24], ALU.mult, ["c_cs"], ["c_sq"])
            em.op("dve", lambda e: e.tensor_reduce(out=cols[:, 25:26], in_=cols[:, 32:40], axis=AX.X, op=ALU.add), reads=["c_sq"], writes=["c_p2"])
            mm_group(pD[:, 0:2], [(ones_f[:], cols[:, 24:26])], reads=["ones_f", "c_p1", "c_p2"], writes=["pD"])
            em.op("dve", lambda e: e.tensor_scalar(out=cols[:, 26:28], in0=pD[:, 0:2], scalar1=1.0 / CC, scalar2=None, op0=ALU.mult),
                  reads=["pD"], writes=["c_mv"])
            col_op(cols[:, 28:29], cols[:, 26:27], cols[:, 26:27], ALU.mult, ["c_mv"], ["c_m2"])
            col_op(cols[:, 29:30], cols[:, 27:28], cols[:, 28:29], ALU.subtract, ["c_mv", "c_m2"], ["c_var"])
            em.op("act", lambda e: e.activation(out=cols[:, 29:30], in_=cols[:, 29:30], func=AF.Sqrt, bias=eps_t[:, 0:1]),
                  reads=["c_var", "eps_t"], writes=["c_var"])
            em.op("dve", lambda e: e.reciprocal(out=cols[:, 29:30], in_=cols[:, 29:30]), reads=["c_var"], writes=["c_var"])
            em.op("dve", lambda e: e.tensor_scalar(out=cols[:, 32:40], in0=cols[:, 16:24], scalar1=cols[:, 26:27], scalar2=cols[:, 29:30],
                                                    op0=ALU.subtract, op1=ALU.mult), reads=["c_cs", "c_mv", "c_var", "c_sq"], writes=["c_t"])
            col_op(cols[:, 32:40], cols[:, 32:40], ccol_t[:, 8:16], ALU.mult, ["c_t", "ccol_t"], ["c_t"])
            col_op(cols[:, 32:40], cols[:, 32:40], ccol_t[:, 16:24], ALU.add, ["c_t"], ["c_t"])
            em.op("act", lambda e: e.activation(out=cols[:, 32:40], in_=cols[:, 32:40], func=AF.Silu), reads=["c_t"], writes=["c_t"])
            em.op("act", lambda e: e.activation(out=cols[:, 40:48], in_=projs[:, SC_ZC:SC_ZC + 8], func=AF.Silu), reads=["projs"], writes=["c_zs"])
            col_op(mixTs[:, 0:8], cols[:, 32:40], cols[:, 40:48], ALU.mult, ["c_t", "c_zs"], ["mixTs_c"])
            em.op("dve", lambda e: e.tensor_copy(out=rin[:, 0:8], in_=projs[:, SC_Q:SC_Q + 8]), reads=["projs"], writes=["rin_a"])
            em.op("dve", lambda e: e.tensor_copy(out=rin[:, 8:10], in_=projs[:, SC_KV + 4:SC_KV + 6]), reads=["projs"], writes=["rin_b"])
            em.op("dve", lambda e: e.tensor_copy(out=rin[:, 10:12], in_=projs[:, SC_KV + 8:SC_KV + 10]), reads=["projs"], writes=["rin_c"])
            mm_group(pD[:, 8:20], [(perm_f[:], rin)], reads=["perm_f", "rin_a", "rin_b", "rin_c"], writes=["pD"])
            em.op("dve", lambda e: e.tensor_scalar(out=rr, in0=rin, scalar1=rope_cs[:, 0:1], scalar2=None, op0=ALU.mult),
                  reads=["rin_a", "rin_b", "rin_c", "rope_cs"], writes=["rr"])
            em.op("dve", lambda e: e.scalar_tensor_tensor(out=rr, in0=pD[:, 8:20], scalar=rope_cs[:, 1:2], in1=rr, op0=ALU.mult, op1=ALU.add),
                  reads=["pD", "rr", "rope_cs"], writes=["rr"])
            qr, ksr, kwr = rr[:, 0:8], rr[:, 8:10], rr[:, 10:12]
            em.op("dve", lambda e: e.tensor_copy(out=kvrow[:, 0:4], in_=projs[:, SC_KV:SC_KV + 4]), reads=["projs"], writes=["kvrow_a"])
            em.op("dve", lambda e: e.tensor_copy(out=kvrow[:, 4:6], in_=ksr), reads=["rr"], writes=["kvrow_b"])
            em.op("dve", lambda e: e.tensor_copy(out=kvrow[:, 6:8], in_=projs[:, SC_KV + 6:SC_KV + 8]), reads=["projs"], writes=["kvrow_c"])
            em.dma("sp", lambda e: e.dma_start(out=kvs[l, 0, :].rearrange("(c p) -> p c", p=128), in_=kvrow, allow_slow_non_contiguous=True),
                   reads=["kvrow_a", "kvrow_b", "kvrow_c"])
            em.op("dve", lambda e: e.tensor_copy(out=wrow[:, 0:2], in_=kwr), reads=["rr"], writes=["wrow_a"])
            em.op("dve", lambda e: e.tensor_copy(out=wrow[:, 2:4], in_=projs[:, SC_KV + 10:SC_KV + 12]), reads=["projs"], writes=["wrow_b"])
            em.dma("sp", lambda e: e.dma_start(out=wins[l, 511, :].rearrange("(c p) -> p c", p=128), in_=wrow, allow_slow_non_contiguous=True),
                   reads=["wrow_a", "wrow_b"], writes=["wins_row"])
            em.dma("sp", lambda e: e.dma_start(out=wins[l, 0:511, :], in_=winc[l, 1:512, :]))
            em.dma("sp", lambda e: e.dma_start(out=wtile, in_=winc[l].rearrange("(t p) x -> p t x", p=128)), writes=["wtile"])
            em.dma("sp", lambda e: e.dma_start(out=wtile[0:1, 0, :], in_=wins[l, 511:512, :]), reads=["wins_row", "wtile"], writes=["wtile"])
            em.op("act", lambda e: e.copy(out=wV, in_=wtile[:, :, 256:512]), reads=["wtile"], writes=["wV"])
            for which, (qsrc, qk, qdst, qdk) in enumerate(((qr, "rr", qbc_r, "qbc_r"), (projs[:, SC_Q:SC_Q + 8], "projs", qbc_c, "qbc_c"))):
                for gp_ in range(2):
                    pp, pk = next_ps()
                    for i_ in range(4):
                        j_ = gp_ * 4 + i_
                        em.op("dve", lambda e, j_=j_, qsrc=qsrc: e.tensor_scalar(out=Qd, in0=ident_f[:], scalar1=qsrc[:, j_:j_ + 1], scalar2=None,
                                                                               op0=ALU.mult), reads=["ident_f", qk], writes=["Qd"])
                        mm_group(pp[:, i_ * 128:(i_ + 1) * 128], [(ones_f[:], Qd)], reads=["ones_f", "Qd"], writes=[pk])
                    em.op("dve", lambda e, pp=pp, gp_=gp_, qdst=qdst: e.tensor_copy(
                        out=qdst.rearrange("p (g e i d) -> p g i e d", g=2, e=2, i=4)[:, gp_],
                        in_=pp[:].rearrange("p (i e d) -> p i e d", e=2, d=64)), reads=[pk], writes=[(qdk, gp_)])

            def s_branch(Kap, Vap, P, nrow, qbc, qbk, kkeys, vkeys, br, first, last, mask=None, mkeys=(), keepE=None, front=None):
                pv_ = prod[0:P, 0:nrow * 1024].rearrange("p (r k i d) -> p r k i d", k=4, i=4, d=64)
                qv_ = qbc[0:P, :].rearrange("p (k i d) -> p k i d", k=4, i=4)
                if front is not None:
                    front()
                for r_ in (range(nrow) if front is None else ()):
                    em.op("dve", lambda e, r_=r_: e.tensor_tensor(out=pv_[:, r_], in0=Kap[:, r_].unsqueeze(2).to_broadcast([P, 4, 4, 64]),
                                                                  in1=qv_, op=ALU.mult),
                          reads=list(kkeys) + [(qbk, 0), (qbk, 1)], writes=[("prod", r_)])
                if front is None:
                    em.op("dve", lambda e: e.tensor_reduce(out=sE[0:P, 0:nrow, :], in_=prod[0:P, 0:nrow * 1024].rearrange("p (r h d) -> p r h d", h=16, d=64),
                                                           axis=AX.X, op=ALU.add), reads=[("prod", r_) for r_ in range(nrow)], writes=["sE"])
                    em.op("act", lambda e: e.activation(out=sE[0:P, 0:nrow, :], in_=sE[0:P, 0:nrow, :], func=AF.Exp, scale=0.125),
                          reads=["sE"], writes=["sE"])
                if mask is not None:
                    em.op("dve", lambda e: e.tensor_tensor(out=sE[0:P, 0:nrow, :].rearrange("p r (k i) -> p r k i", i=4),
                                                           in0=sE[0:P, 0:nrow, :].rearrange("p r (k i) -> p r k i", i=4),
                                                           in1=mask, op=ALU.mult), reads=["sE"] + list(mkeys), writes=["sE"])
                if keepE is not None:
                    em.op("dve", lambda e: e.tensor_copy(out=keepE, in_=sE[0:P, 0, :]), reads=["sE"], writes=["Ec_all"])
                em.op("dve", lambda e: e.tensor_copy(out=sEb[0:P, 0:nrow, :], in_=sE[0:P, 0:nrow, :]), reads=["sE"], writes=["sEb"])
                fns = []
                for r_ in range(nrow):
                    st_ = first and r_ == 0
                    fns.append(lambda e, r_=r_, st_=st_: e.matmul(pO[br][:, 16:32], lhsT=ones_b[0:P, :], rhs=sEb[0:P, r_, :], start=st_, stop=False,
                                                                  skip_group_check=True))
                    ev = sEb[0:P, r_, :].rearrange("p (k i) -> p k i", i=4)
                    for gp_ in range(2):
                        for i_ in range(4):
                            j_ = gp_ * 4 + i_
                            fns.append(lambda e, r_=r_, gp_=gp_, i_=i_, j_=j_, ev=ev: e.matmul(
                                pO[br][:, 2 * j_:2 * j_ + 2], lhsT=Vap[:, r_, gp_ * 128:(gp_ + 1) * 128], rhs=ev[:, 2 * gp_:2 * gp_ + 2, i_],
                                start=False, stop=False, skip_group_check=True))
                em.op("pe", fns, reads=["sEb", "ones_b"] + list(vkeys), writes=[f"pO{br}"])

            for tl in range(NBT):
                P = min(128, NBLK_S - tl * 128)
                s_branch(cmp_s[0:P, tl:tl + 1, 0:4, :], cmpV[0:P, tl:tl + 1, :], P, 1, qbc_c, "qbc_c", [("cmp_s", tl)], [("cmpV", tl)], 0,
                         tl == 0, tl == NBT - 1, keepE=Ec_all[0:P, tl, :])
            em.op("dve", lambda e: e.tensor_scalar(out=denc[:, 0, :], in0=pO[0][:, 16:32], scalar1=1e-30, scalar2=None, op0=ALU.max),
                  reads=["pO0"], writes=[("denc", 0)])
            em.op("dve", lambda e: e.reciprocal(out=cols[:, 48:64], in_=denc[:, 0, :]), reads=[("denc", 0)], writes=["c_rd"])
            Pl = min(128, NBLK_S)
            em.op("dve", lambda e: e.tensor_tensor(out=Ec_all[0:Pl], in0=Ec_all[0:Pl], in1=cols[0:Pl, 48:64].unsqueeze(1).to_broadcast([Pl, NBT, 16]),
                                                   op=ALU.mult), reads=["Ec_all", "c_rd"], writes=["Ec_all"])
            em.op("dve", lambda e: e.tensor_reduce(out=imp_n[0:Pl], in_=Ec_all[0:Pl].rearrange("p t (k i) -> p t k i", i=4), axis=AX.X, op=ALU.add),
                  reads=["Ec_all"], writes=["imp_n"])
            pp, pk = next_ps()
            for tl in range(NBT):
                P = min(128, NBLK_S - tl * 128)
                em.op("pe", lambda e, tl=tl, P=P, pp=pp: e.transpose(out=pp[0:4, tl * 128:tl * 128 + P], in_=imp_n[0:P, tl, :], identity=ident_f[0:P, 0:P]),
                      reads=["imp_n", "ident_f"], writes=[pk])
            em.op("dve", lambda e: e.memset(srow, 0.0), writes=["srow"])
            em.op("dve", lambda e, pp=pp: e.tensor_copy(out=srow[:, 0:NBLK_S], in_=pp[0:4, 0:NBLK_S]), reads=[pk, "srow"], writes=["srow"])
            col_op(srow, srow, keep_s[:], ALU.mult, ["srow", "keep_s"], ["srow"])
            col_op(srow, srow, add_s[:], ALU.add, ["srow", "add_s"], ["srow"])
            em.op("dve", lambda e: e.max(out=m16[:, 0:8], in_=srow), reads=["srow"], writes=["m16a"])
            em.op("dve", lambda e: e.match_replace(out=srow2, in_to_replace=m16[:, 0:8], in_values=srow, imm_value=-3e9),
                  reads=["srow", "m16a"], writes=["srow2"])
            em.op("dve", lambda e: e.max(out=m16[:, 8:16], in_=srow2), reads=["srow2"], writes=["m16b"])
            em.op("dve", lambda e: e.tensor_scalar(out=selrow, in0=srow, scalar1=m16[:, 15:16], scalar2=None, op0=ALU.is_ge),
                  reads=["srow", "m16b"], writes=["selrow"])
            em.dma("sp", lambda e: e.dma_start(out=selscr, in_=selrow[:, 0:NBLK_S]), reads=["selrow"], writes=["selscr"])
            for k_ in range(4):
                em.dma("sp", lambda e, k_=k_: e.dma_start(out=selR[:, :, k_], in_=selscr[k_].rearrange("(u n) -> n u", n=4),
                                                          allow_slow_non_contiguous=True),
                       reads=["selscr"], writes=[("selR", k_)])
            pp, pk = next_ps()
            mm_group(pp[:, 0:NDP * 4], [(ind4[:], selR.rearrange("p u k -> p (u k)"))], reads=["ind4"] + [("selR", k_) for k_ in range(4)], writes=[pk])
            em.op("dve", lambda e, pp=pp: e.tensor_copy(out=selexp.rearrange("p u k -> p (u k)"), in_=pp[:, 0:NDP * 4]), reads=[pk], writes=["selexp"])
            for tl in range(4):
                s_branch(wtile[:, tl:tl + 1, 0:256].rearrange("p r (k d) -> p r k d", d=64), wV[:, tl:tl + 1, :], 128, 1, qbc_r, "qbc_r",
                         ["wtile"], ["wV"], 2, tl == 0, tl == 3)
            em.op("dve", lambda e: e.memset(Qblk, 0.0), writes=["Qblk"])
            em.op("dve", lambda e: e.tensor_copy(out=Qblk[0:64, :, 0:4], in_=qr[0:64, :].rearrange("p (g i) -> p g i", i=4)),
                  reads=["rr", "Qblk"], writes=["Qblk"])
            em.op("dve", lambda e: e.tensor_copy(out=Qblk[64:128, :, 4:8], in_=qr[64:128, :].rearrange("p (g i) -> p g i", i=4)),
                  reads=["rr", "Qblk"], writes=["Qblk"])
            for u in range(NDP):
                b_ = u % 3
                v_ = u % 2
                for jl in range(2):
                    em.dma("pool", lambda e, u=u, b_=b_, jl=jl: e.indirect_dma_start(
                        out=D32b[b_][:, jl, :], out_offset=None, in_=pool4,
                        in_offset=bass.IndirectOffsetOnAxis(ap=idxq[2 * jl + 1][:, u:u + 1], axis=0)),
                           reads=[("idxq", 2 * jl + 1)], writes=[f"D32b{b_}"])
                em.op("act", lambda e, b_=b_, v_=v_: e.copy(out=KVb[v_], in_=D32b[b_]), reads=[f"D32b{b_}"], writes=[f"KVb{v_}"])

                def front(v_=v_):
                    em.op("pe", [lambda e, jl=jl, g_=g_: e.transpose(out=pT[:, (jl * 2 + g_) * 128:(jl * 2 + g_ + 1) * 128],
                                                                    in_=KVb[v_][:, jl, g_ * 128:(g_ + 1) * 128], identity=ident_b[:])
                                 for jl in range(2) for g_ in range(2)], reads=[f"KVb{v_}", "ident_b"], writes=["pT"])
                    em.op("dve", lambda e: e.tensor_copy(out=KT4, in_=pT[:, 0:512].rearrange("p (a b) -> p a b", b=128)), reads=["pT"], writes=["KT4"])
                    pq, pqk = next_pcd()
                    for jl in range(2):
                        for g_ in range(2):
                            mm_group(pq[:, jl * 16 + g_ * 8: jl * 16 + g_ * 8 + 8], [(KT4[:, jl * 2 + g_, :], Qblk[:, g_, :])],
                                     reads=["KT4", "Qblk"], writes=[pqk])
                    em.op("act", lambda e, pq=pq: e.activation(out=sE[:, 0:2, :], in_=pq[:, 0:32].rearrange("p (r h) -> p r h", h=16),
                                                               func=AF.Exp, scale=0.125), reads=[pqk], writes=["sE"])
                s_branch(None, KVb[v_][:, :, 256:512], 128, 2, qbc_r, "qbc_r",
                         [f"KVb{v_}"], [f"KVb{v_}"], 1, u == 0, u == NDP - 1,
                         mask=selexp[:, u, :].unsqueeze(1).unsqueeze(3).to_broadcast([128, 2, 4, 4]), mkeys=["selexp"], front=front)
            em.op("dve", lambda e: e.tensor_tensor(out=cols[:, 0:8].rearrange("p (g i) -> p g i", i=4), in0=qr.rearrange("p (g i) -> p g i", i=4),
                                                   in1=ksr.unsqueeze(2).to_broadcast([128, 2, 4]), op=ALU.mult), reads=["rr"], writes=["c_sn"])
            mm_group(pD[:, 32:40], [(blk_f[:], cols[:, 0:8])], reads=["blk_f", "c_sn"], writes=["pD"])
            em.op("act", lambda e: e.activation(out=cols[:, 8:16], in_=pD[:, 32:40], func=AF.Exp, scale=0.125), reads=["pD"], writes=["c_en"])
            for br in range(3):
                n3 = pO[br][:, 0:16].rearrange("p (j two) -> p j two", two=2)
                em.op("dve", lambda e, br=br, n3=n3: e.tensor_copy(out=numc[0:64, br, :], in_=n3[0:64, :, 0]), reads=[f"pO{br}"], writes=[("numc", br, 0)])
                em.op("dve", lambda e, br=br, n3=n3: e.tensor_copy(out=numc[64:128, br, :], in_=n3[64:128, :, 1]), reads=[f"pO{br}"], writes=[("numc", br, 1)])
                dv = pO[br][:, 16:32].rearrange("p (g e i) -> p g e i", g=2, e=2)
                em.op("dve", lambda e, br=br, dv=dv: e.tensor_copy(out=dcol[0:64, br, :].rearrange("p (g i) -> p g i", i=4), in_=dv[0:64, :, 0, :]),
                      reads=[f"pO{br}"], writes=[("dcol", br, 0)])
                em.op("dve", lambda e, br=br, dv=dv: e.tensor_copy(out=dcol[64:128, br, :].rearrange("p (g i) -> p g i", i=4), in_=dv[64:128, :, 1, :]),
                      reads=[f"pO{br}"], writes=[("dcol", br, 1)])
            nk = [("numc", br, e_) for br in range(3) for e_ in range(2)]
            dk = [("dcol", br, e_) for br in range(3) for e_ in range(2)]
            em.op("dve", lambda e: e.tensor_tensor(out=cols[:, 16:24].rearrange("p (g i) -> p g i", i=4), in0=cols[:, 8:16].rearrange("p (g i) -> p g i", i=4),
                                                   in1=projs[:, SC_KV + 6:SC_KV + 8].unsqueeze(2).to_broadcast([128, 2, 4]), op=ALU.mult),
                  reads=["c_en", "projs"], writes=["c_ev"])
            col_op(numc[:, 1, :], numc[:, 1, :], cols[:, 16:24], ALU.add, nk + ["c_ev"], nk)
            col_op(dcol[:, 1, :], dcol[:, 1, :], cols[:, 8:16], ALU.add, dk + ["c_en"], dk)
            em.op("dve", lambda e: e.tensor_scalar(out=dcol, in0=dcol, scalar1=1e-30, scalar2=None, op0=ALU.max), reads=dk, writes=dk)
            em.op("dve", lambda e: e.reciprocal(out=dcol, in_=dcol), reads=dk, writes=dk)
            col_op(numc, numc, dcol, ALU.mult, nk + dk, nk)
            col_op(gsc, projs[0:48, SC_GL:SC_GL + 1], gateb_c[:], ALU.add, ["projs", "gateb_c"], ["gsc"])
            em.op("act", lambda e: e.activation(out=gsc, in_=gsc, func=AF.Sigmoid), reads=["gsc"], writes=["gsc"])
            em.op("dve", lambda e: e.tensor_scalar(out=Gd, in0=ident_f[0:48, 0:48], scalar1=gsc[:, 0:1], scalar2=None, op0=ALU.mult),
                  reads=["ident_f", "gsc"], writes=["Gd"])
            pp, pk = next_ps()
            mm_group(pp[:, 0:48], [(ones_f[0:48, :], Gd)], reads=["ones_f", "Gd"], writes=[pk])
            em.op("dve", lambda e, pp=pp: e.tensor_copy(out=gbc, in_=pp[:, 0:48]), reads=[pk], writes=["gbc"])
            gv = gbc.rearrange("p (g e i b) -> p g e i b", g=2, e=2, i=4)
            for e_ in range(2):
                rs_ = slice(e_ * 64, (e_ + 1) * 64)
                em.op("dve", lambda e, e_=e_, rs_=rs_: e.tensor_copy(out=gcol[rs_].rearrange("p (g i) b -> p g i b", i=4), in_=gv[rs_, :, e_, :, :]),
                      reads=["gbc"], writes=[("gcol", e_)])
            gk = [("gcol", 0), ("gcol", 1)]
            em.op("dve", lambda e: e.tensor_tensor(out=numc, in0=numc, in1=gcol.rearrange("p j b -> p b j"), op=ALU.mult), reads=nk + gk, writes=nk)
            col_op(ocol, numc[:, 0, :], numc[:, 1, :], ALU.add, nk, ["ocol"])
            col_op(ocol, ocol, numc[:, 2, :], ALU.add, nk + ["ocol"], ["ocol"])
            em.op("act", lambda e: e.activation(out=cols[:, 24:32], in_=projs[:, SC_Z:SC_Z + 8], func=AF.Silu), reads=["projs"], writes=["c_za"])
            col_op(mixTs[:, 8:16], ocol, cols[:, 24:32], ALU.mult, ["ocol", "c_za"], ["mixTs_a"])
            em.op("dve", lambda e: e.memset(small[:, 32:33], 0.0), reads=["mixTs_a", "mixTs_c"], writes=["mixTs"])

        def sample_post(l, last):
            sq = small[:, 0:16]
            em.op("dve", lambda e: e.tensor_tensor(out=sq, in0=ysraw[:], in1=ysraw[:], op=ALU.mult), reads=["ysraw"], writes=["sm_sq"])
            em.op("dve", lambda e: e.tensor_reduce(out=small[:, 16:17], in_=sq, axis=AX.X, op=ALU.add), reads=["sm_sq"], writes=["sm_part"])
            mm_group(pD[:, 0:1], [(ones_f[:], small[:, 16:17])], reads=["ones_f", "sm_part"], writes=["pD"])
            em.op("act", lambda e: e.activation(out=small[:, 17:18], in_=pD[:, 0:1], func=AF.Sqrt, bias=eps_t[:, 0:1], scale=1.0 / D),
                  reads=["pD", "eps_t"], writes=["sm_rs"])
            em.op("dve", lambda e: e.reciprocal(out=small[:, 17:18], in_=small[:, 17:18]), reads=["sm_rs"], writes=["sm_rs"])
            em.op("dve", lambda e: e.scalar_tensor_tensor(out=sq, in0=ysraw[:], scalar=small[:, 17:18], in1=gcol_t[:, 16:32], op0=ALU.mult, op1=ALU.mult),
                  reads=["ysraw", "sm_rs", "gcol_t", "sm_sq"], writes=["sm_sq"])
            em.op("dve", lambda e: e.tensor_tensor(out=xsT[:], in0=xsT[:], in1=sq, op=ALU.add), reads=["xsT", "sm_sq"], writes=["xsT"])
            if last:
                em.dma("sp", lambda e: e.dma_start(out=ys.rearrange("o (k p) -> p (o k)", p=128), in_=xsT[:], allow_slow_non_contiguous=True),
                       reads=["xsT"])

        for l in range(DEPTH):
            last = (l == DEPTH - 1)
            load_f32(gcol_t[:], gcols[l], "gcol_t")
            load_f32(cw_t[:], cwT[l], "cw_t")
            load_f32(ccol_t[:], ccols[l], "ccol_t")
            load_f32(mcol_t[:], mcols[l], "mcol_t")
            load_cast(w2k_lo[:, 0:64], w2[l, 0], "w2k_lo")
            load_cast(w2k_hi[:, 64:128], w2[l, 0], "w2k_hi")
            load_cast(w2kn[:], w2[l, 0], "w2kn")
            load_cast(w2v[:], w2[l, 1], "w2v")
            em.dma("sp", lambda e, l=l: e.dma_start(out=b2bc[:], in_=b2row[l, 0].partition_broadcast(32)), writes=["b2bc"])
            em.dma("sp", lambda e, l=l: e.dma_start(out=gateb_bc[:], in_=gateb[l, 0].partition_broadcast(128)), writes=["gateb_bc"])
            load_f32(gateb_c[:], gatebc[l], "gateb_c")
            for q_ in range(4):
                em.op("dve", lambda e, l=l, q_=q_: e.tensor_scalar(out=idxq[q_][:], in0=idxb[:], scalar1=4.0,
                                                                   scalar2=float(l * NPOOL * 64 * 4 + q_), op0=ALU.mult, op1=ALU.add),
                      reads=["idxb_lo", "idxb_hi"], writes=[("idxq", q_)])

            sq = small[:, 0:16]
            em.op("dve", lambda e: e.tensor_tensor(out=sq, in0=xsT[:], in1=xsT[:], op=ALU.mult),
                  reads=["xsT"], writes=["sm_sq"])
            em.op("dve", lambda e: e.tensor_reduce(out=small[:, 16:17], in_=sq, axis=AX.X, op=ALU.add),
                  reads=["sm_sq"], writes=["sm_part"])
            mm_group(pD[:, 0:1], [(ones_f[:], small[:, 16:17])], reads=["ones_f", "sm_part"], writes=["pD"])
            em.op("act", lambda e: e.activation(out=small[:, 17:18], in_=pD[:, 0:1], func=AF.Sqrt, bias=eps_t[:, 0:1], scale=1.0 / D),
                  reads=["pD", "eps_t"], writes=["sm_rs"])
            em.op("dve", lambda e: e.reciprocal(out=small[:, 17:18], in_=small[:, 17:18]), reads=["sm_rs"], writes=["sm_rs"])
            em.op("dve", lambda e: e.scalar_tensor_tensor(out=uTs[:], in0=xsT[:], scalar=small[:, 17:18],
                                                           in1=gcol_t[:, 0:16], op0=ALU.mult, op1=ALU.mult),
                  reads=["xsT", "sm_rs", "gcol_t"], writes=["uTs"])

            for half in range(NH):
                T0 = half * 8
                do_s = (half == 0)
                em.barrier()
                cv = Carver()
                xt = [cv([128, D], F32) for _ in range(2)]
                ub = [cv([128, D], BF16) for _ in range(2)]
                gbc = cv([128, D], F32)
                rsb = cv([128, 16], F32)
                em.dma("sp", lambda e, l=l: e.dma_start(out=gbc, in_=gpre[l].partition_broadcast(128)),
                       writes=["gbc"])
                src = xp if l == 0 else resid
                for t in range(8):
                    T = T0 + t
                    b_ = t % 2
                    em.dma("sp", lambda e, T=T, b_=b_: e.dma_start(out=xt[b_], in_=src[T * 128:(T + 1) * 128, :]),
                           reads=[("resid", T)], writes=[f"xt{b_}"])
                    em.op("act", lambda e, b_=b_, t=t: e.activation(out=ub[b_], in_=xt[b_], func=AF.Square,
                                                                    accum_out=rsb[:, t:t + 1]),
                          reads=[f"xt{b_}"], writes=[f"ub{b_}", ("rsb", t)])
                    em.op("act", lambda e, t=t: e.activation(out=rsb[:, t:t + 1], in_=rsb[:, t:t + 1], func=AF.Sqrt,
                                                             bias=eps_t[:, 0:1], scale=1.0 / D),
                          reads=[("rsb", t), "eps_t"], writes=[("rsb", t)])
                    em.op("dve", lambda e, t=t: e.reciprocal(out=rsb[:, t:t + 1], in_=rsb[:, t:t + 1]),
                          reads=[("rsb", t)], writes=[("rsb", t)])
                    em.op("dve", lambda e, b_=b_, t=t: e.scalar_tensor_tensor(out=ub[b_], in0=xt[b_], scalar=rsb[:, t:t + 1],
                                                                              in1=gbc, op0=ALU.mult, op1=ALU.mult),
                          reads=[f"xt{b_}", ("rsb", t), "gbc", f"ub{b_}"], writes=[f"ub{b_}"])
                    for kh in range(2):
                        pk = f"pT{kh}"
                        em.op("pe", [lambda e, b_=b_, k=kh * 8 + kk, kk=kk: e.transpose(
                            out=pT[:, kk * 128:(kk + 1) * 128],
                            in_=ub[b_][:, k * 128:(k + 1) * 128], identity=ident_b[:]) for kk in range(8)],
                              reads=[f"ub{b_}", "ident_b"], writes=["pT"])
                        copy_op(evac_eng(), uT[:, kh * 8:(kh + 1) * 8, t * 128:(t + 1) * 128],
                                pT[:].rearrange("p (a b) -> p a b", b=128), ["pT"], [("uT", t)])

                prefetch_w(("kv", l, half), w_in[l][:, KV_OFF:KV_OFF + 256], 256)
                em.barrier()
                cv = Carver()
                stage = [cv([128, 256], F32) for _ in range(2)]
                kb = [cv([128, 256], BF16) for _ in range(2)]
                rtmp = cv([128, 4, 4, 8], F32)
                Xb_all = cv([128, 8, 2, 256], BF16)
                Xp = [cv([128, 2, 4, 128], BF16) for _ in range(2)]
                XT = cv([128, 8, 4, 128], BF16)
                W1s = cv([128, 2, 32, 128], BF16)
                hid = cv([128, 8, 32], BF16)
                pos_t = cv([128, 2, 64], F32)
                vst = cv([32, 256], F32)
                load_f32(pos_t, pos_tm_d[l], "pos_t")
                em.dma("pool", lambda e, l=l: e.dma_start(out=W1s, in_=w1[l].rearrange("c (jh p) e -> p c jh e", p=128)),
                       writes=["W1s"])
                uT_all = [("uT", t) for t in range(8)]
                for j in range(7):
                    ncols = 256 if j < 6 else 48
                    off = KV_OFF + j * 256 if j < 6 else GL_OFF
                    wbuf, wkey = load_w(w_in[l][:, off:off + ncols], ncols, tag=(("kv", l, half) if j == 0 else None))
                    if do_s:
                        sample_cols(wbuf, wkey, ncols, SC_KV + 2 * j if j < 6 else SC_GL)
                    for t in range(8):
                        T = T0 + t
                        pp, pk = next_ps()
                        mm_group(pp[:, 0:ncols], [(uT[:, k, t * 128:(t + 1) * 128], wbuf[:, k, 0:ncols]) for k in range(KC)],
                                 reads=[("uT", t), wkey], writes=[pk])
                        if j == 6:
                            em.op("dve", lambda e, pp=pp, t=t: e.tensor_tensor(out=gates[:, t, :], in0=pp[:, 0:48], in1=gateb_bc[:],
                                                                              op=ALU.add), reads=[pk, "gateb_bc"], writes=[("gates", t)])
                            em.op("act", lambda e, t=t: e.activation(out=gates[:, t, :], in_=gates[:, t, :], func=AF.Sigmoid),
                                  reads=[("gates", t)], writes=[("gates", t)])
                            continue
                        sb_ = (j * 8 + t) % 2
                        sg, sgk = stage[sb_], f"stage{sb_}"
                        em.op("act", lambda e, sg=sg, pp=pp: e.copy(out=sg, in_=pp[:, 0:256]), reads=[pk], writes=[sgk])
                        if j in (2, 4):
                            s4 = sg.rearrange("p (k d) -> p k d", d=64)
                            x1, x2 = s4[:, :, 0:8], s4[:, :, 8:16]
                            cb_ = cos_tm[:, T, :].unsqueeze(1).to_broadcast([128, 4, 8])
                            sn_ = sin_tm[:, T, :].unsqueeze(1).to_broadcast([128, 4, 8])
                            rk = "rtmp"
                            em.op("dve", lambda e, x1=x1, cb_=cb_: e.tensor_tensor(out=rtmp[:, 0], in0=x1, in1=cb_, op=ALU.mult),
                                  reads=[sgk, "cos_tm"], writes=[rk])
                            em.op("dve", lambda e, x2=x2, sn_=sn_: e.tensor_tensor(out=rtmp[:, 1], in0=x2, in1=sn_, op=ALU.mult),
                                  reads=[sgk, "sin_tm", rk], writes=[rk])
                            em.op("dve", lambda e, x2=x2, cb_=cb_: e.tensor_tensor(out=rtmp[:, 2], in0=x2, in1=cb_, op=ALU.mult),
                                  reads=[sgk, rk], writes=[rk])
                            em.op("dve", lambda e, x1=x1, sn_=sn_: e.tensor_tensor(out=rtmp[:, 3], in0=x1, in1=sn_, op=ALU.mult),
                                  reads=[sgk, rk], writes=[rk])
                            em.op("dve", lambda e, x1=x1: e.tensor_tensor(out=x1, in0=rtmp[:, 0], in1=rtmp[:, 1], op=ALU.subtract),
                                  reads=[rk, sgk], writes=[sgk])
                            em.op("dve", lambda e, x2=x2: e.tensor_tensor(out=x2, in0=rtmp[:, 2], in1=rtmp[:, 3], op=ALU.add),
                                  reads=[rk, sgk], writes=[sgk])
                        if j < 4:
                            em.dma("sp", lambda e, sg=sg, T=T, j=j, l=l: e.dma_start(
                                out=kvp[l, T * 128:(T + 1) * 128, j * 256:(j + 1) * 256], in_=sg), reads=[sgk])
                        elif T >= NT - 4:
                            em.dma("sp", lambda e, sg=sg, T=T, j=j, l=l: e.dma_start(
                                out=winp[l, (T - (NT - 4)) * 128:(T - (NT - 4) + 1) * 128, (j - 4) * 256:(j - 3) * 256], in_=sg),
                                   reads=[sgk])
                        if j in (0, 1):
                            em.op("dve", lambda e, sg=sg, t=t, j=j: e.tensor_tensor(
                                out=Xb_all[:, t, j, :].rearrange("p (k d) -> p k d", d=64),
                                in0=sg.rearrange("p (k d) -> p k d", d=64),
                                in1=pos_t[:, j, :].unsqueeze(1).to_broadcast([128, 4, 64]), op=ALU.add),
                                  reads=[sgk, "pos_t"], writes=[("Xb", t)])
                        elif j in (2, 4):
                            kbb, kbk = kb[sb_], f"kb{sb_}"
                            em.op("dve", lambda e, kbb=kbb, sg=sg: e.tensor_copy(out=kbb, in_=sg), reads=[sgk], writes=[kbk])
                            em.op("pe", [lambda e, kbb=kbb, g_=g_: e.transpose(out=pT[:, g_ * 128:(g_ + 1) * 128],
                                                                              in_=kbb[:, g_ * 128:(g_ + 1) * 128], identity=ident_b[:])
                                         for g_ in range(2)], reads=[kbk, "ident_b"], writes=["pT"])
                            dst = ksT if j == 2 else kwT
                            dk = ("ksT", T) if j == 2 else ("kwT", T)
                            copy_op("act", dst[:, :, T * 128:(T + 1) * 128], pT[:, 0:256].rearrange("p (a b) -> p a b", b=128),
                                    ["pT"], [dk])
                        else:
                            dst = vsA if j == 3 else vwA
                            dk = ("vsA", T) if j == 3 else ("vwA", T)
                            em.op("dve", lambda e, dst=dst, sg=sg, T=T: e.tensor_copy(
                                out=dst[:, T, :, 0:64], in_=sg.rearrange("p (k d) -> p k d", d=64)),
                                  reads=[sgk], writes=[dk])

                for v in range(4):
                    xpp, xpk = Xp[v % 2], f"Xp{v % 2}"
                    for tp in range(2):
                        for jl in range(2):
                            em.dma("sp", lambda e, xpp=xpp, tp=tp, jl=jl, v=v: e.dma_start(
                                out=xpp[tp * 64:(tp + 1) * 64, :, :, jl * 64:(jl + 1) * 64],
                                in_=Xb_all[jl:128:2, 2 * v + tp, :, :].rearrange("p c (k d) -> p c k d", d=64)),
                                   reads=[("Xb", 2 * v + tp)], writes=[xpk])
                    mlp_feed(xpp, xpk, XT, v)
                nb = mlp_stage1(XT, 4, hid, W1s)
                for gp in range(2):
                    pp, pk = next_ps()
                    mm_group(pp[:, 0:nb], [(w2k_lo[:], hid[:, 2 * gp, 0:nb]), (w2k_hi[:], hid[:, 2 * gp + 1, 0:nb])],
                             reads=["hid", "w2k_lo", "w2k_hi"], writes=[pk])
                    em.op("act", lambda e, pp=pp, gp=gp: e.activation(out=kcmpT[:, gp, half * 16:half * 16 + nb], in_=pp[:, 0:nb],
                                                                     func=AF.Identity, bias=mcol_t[:, 2:3]),
                          reads=[pk, "mcol_t"], writes=[("kcmpT", half)])
                pp, pk = next_ps()
                for kv_ in range(4):
                    mm_group(pp[0:nb, kv_ * 64:(kv_ + 1) * 64], [(hid[:, 4 + kv_, 0:nb], w2v[:])], reads=["hid", "w2v"], writes=[pk])
                em.op("dve", lambda e, pp=pp: e.tensor_tensor(
                    out=vcmp[half][0:nb, :, 0:64], in0=pp[0:nb, 0:256].rearrange("p (k d) -> p k d", d=64),
                    in1=b2bc[0:nb, 64:128].unsqueeze(1).to_broadcast([nb, 4, 64]), op=ALU.add),
                      reads=[pk, "b2bc"], writes=[f"vcmp{half}"])

                Qc0 = half * 2
                for gp in range(2):
                    prefetch_w(("q", l, half, gp), w_in[l][:, Q_OFF + gp * 512:Q_OFF + gp * 512 + 256], 256)
                    em.barrier()
                    cv = Carver()
                    QT = cv([128, 4, HALF], BF16)
                    QrP = [cv([128, 4, HALF], BF16) for _ in range(2)]
                    zT = cv([128, 4, HALF], BF16)
                    Ct = cv([128, HALF], F32)
                    osb = Ct[:, 0:780].rearrange("p (b c) -> p b c", c=260)
                    St = cv([128, HALF], F32)
                    rt = [cv([128, 512], F32) for _ in range(2)]
                    Et = [cv([128, 512], BF16) for _ in range(3)]
                    E16 = [cv([16, 512], BF16) for _ in range(2)]
                    o_tm = cv([128, 8, 512], BF16)
                    selT = [cv([128, HALF], BF16) for _ in range(2)]
                    ec = cv([128, 4, 32], F32)
                    tk = cv([128, 128], F32)
                    selb = cv([128, 32], BF16)
                    d3 = cv([128, 4, 3], F32)
                    w3 = cv([128, 4, 3], F32)
                    ot = [rt[b_][:, 0:256].rearrange("p (s d) -> p s d", d=64) for b_ in range(2)]
                    em.op("pool", lambda e: e.memset(QrP[0][64:128], 0.0), writes=[("QrT", i_, t_) for i_ in range(4) for t_ in range(2)])
                    em.op("pool", lambda e: e.memset(QrP[1][0:64], 0.0), writes=[("QrT", i_, t_) for i_ in range(4) for t_ in range(2)])
                    if half >= 1:
                        for e2 in range(2):
                            em.op("pool", lambda e, e2=e2: e.memset(selT[e2], 0.0), writes=[("selT", e2, 0), ("selT", e2, 1)])
                    load_f32(Ct, cd["rope_c"][:, half * HALF:(half + 1) * HALF], "Ct")
                    load_f32(St, cd["rope_s"][:, half * HALF:(half + 1) * HALF], "St")
                    for which in range(2):
                        for wl in range(2):
                            off = (Q_OFF if which == 0 else Z_OFF) + gp * 512 + wl * 256
                            wbuf, wkey = load_w(w_in[l][:, off:off + 256], 256, tag=(("q", l, half, gp) if (which == 0 and wl == 0) else None))
                            if do_s:
                                sample_cols(wbuf, wkey, 256, (SC_Q if which == 0 else SC_Z) + gp * 4 + wl * 2)
                            for s_ in range(2):
                                i = wl * 2 + s_
                                for tq in range(2):
                                    pp, pk = next_ps()
                                    mm_group(pp[:], [(wbuf[:, k, s_ * 128:(s_ + 1) * 128], uT[:, k, tq * 512:(tq + 1) * 512])
                                                     for k in range(KC)], reads=uT_all + [wkey], writes=[pk])
                                    sl = slice(tq * 512, (tq + 1) * 512)
                                    if which == 1:
                                        em.op("act", lambda e, pp=pp, i=i, sl=sl: e.activation(out=zT[:, i, sl], in_=pp[:], func=AF.Silu),
                                              reads=[pk], writes=[("zT", i, tq)])
                                        continue
                                    em.op("act", lambda e, pp=pp, i=i, sl=sl: e.copy(out=QT[:, i, sl], in_=pp[:]),
                                          reads=[pk], writes=[("QT", i, tq)])
                                    pq, pqk = next_pcd()
                                    mm_group(pq[:], [(perm_b[:], QT[:, i, sl])], reads=["perm_b", ("QT", i, tq)], writes=[pqk])
                                    em.op("dve", lambda e, i=i, sl=sl: e.tensor_tensor(out=rt[0], in0=QT[:, i, sl], in1=Ct[:, sl], op=ALU.mult),
                                          reads=[("QT", i, tq), "Ct"], writes=["rt0"])
                                    em.op("dve", lambda e, pq=pq, sl=sl: e.tensor_tensor(out=rt[1], in0=pq[:], in1=St[:, sl], op=ALU.mult),
                                          reads=[pqk, "St"], writes=["rt1"])
                                    for e2 in range(2):
                                        hs = slice(e2 * 64, (e2 + 1) * 64)
                                        em.op("dve", lambda e, i=i, sl=sl, e2=e2, hs=hs: e.tensor_tensor(out=QrP[e2][hs, i, sl], in0=rt[0][hs, :], in1=rt[1][hs, :], op=ALU.add),
                                              reads=["rt0", "rt1"], writes=[("QrT", i, tq)])
                    QT_all = [("QT", i, tq) for i in range(4) for tq in range(2)]
                    def prelude_tile(e_, t):
                        base = e_ * 64
                        T = T0 + t
                        pp, pk = next_ps()
                        for i in range(4):
                            mm_group(pp[:, i * 32:i * 32 + NBP],
                                     [(QT[base:base + 64, i, t * 128:(t + 1) * 128], kcmpT[base:base + 64, gp, 0:NBP])],
                                     reads=[("QT", i, t // 4), ("kcmpT", 0), ("kcmpT", 1)], writes=[pk])
                        em.op("act", lambda e, pp=pp: e.activation(out=ec[:, :, 0:NBP],
                                                                   in_=pp[:, 0:128].rearrange("p (h n) -> p h n", n=32)[:, :, 0:NBP],
                                                                   func=AF.Exp, scale=0.125), reads=[pk], writes=["ec"])
                        em.op("dve", lambda e, T=T: e.tensor_tensor(out=ec[:, :, 0:NBP], in0=ec[:, :, 0:NBP],
                                                                    in1=cm_tm[:, T - (NT - 8), 0:NBP].unsqueeze(1).to_broadcast([128, 4, NBP]),
                                                                    op=ALU.mult), reads=["ec", "cm_tm"], writes=["ec"])
                        em.op("dve", lambda e: e.tensor_reduce(out=tk[:, 0:4], in_=ec[:, :, 0:NBP], axis=AX.X, op=ALU.add),
                              reads=["ec"], writes=["tk_den"])
                        em.op("dve", lambda e: e.tensor_scalar(out=tk[:, 0:4], in0=tk[:, 0:4], scalar1=1e-30, scalar2=None, op0=ALU.max),
                              reads=["tk_den"], writes=["tk_den"])
                        em.op("dve", lambda e: e.reciprocal(out=tk[:, 4:8], in_=tk[:, 0:4]), reads=["tk_den"], writes=["tk_r"])
                        em.op("dve", lambda e: e.tensor_tensor(out=ec[:, :, 0:NBP], in0=ec[:, :, 0:NBP],
                                                               in1=tk[:, 4:8].unsqueeze(2).to_broadcast([128, 4, NBP]), op=ALU.mult),
                              reads=["ec", "tk_r"], writes=["ec"])
                        em.op("dve", lambda e: e.tensor_reduce(out=tk[:, 8:8 + NBP], in_=ec[:, :, 0:NBP].rearrange("p h n -> p n h"),
                                                               axis=AX.X, op=ALU.add), reads=["ec"], writes=["tk_imp"])
                        em.op("dve", lambda e, T=T: e.tensor_tensor(out=tk[:, 8:8 + NBP], in0=tk[:, 8:8 + NBP], in1=keep_tm[:, T - (NT - 8), 0:NBP],
                                                                    op=ALU.mult), reads=["tk_imp", "keep_tm"], writes=["tk_imp"])
                        em.op("dve", lambda e, T=T: e.tensor_tensor(out=tk[:, 8:8 + NBP], in0=tk[:, 8:8 + NBP], in1=add_tm[:, T - (NT - 8), 0:NBP],
                                                                    op=ALU.add), reads=["tk_imp", "add_tm"], writes=["tk_imp"])
                        em.op("dve", lambda e: e.max(out=tk[:, 48:56], in_=tk[:, 8:8 + NBP]), reads=["tk_imp"], writes=["tk_m1"])
                        em.op("dve", lambda e: e.match_replace(out=tk[:, 64:64 + NBP], in_to_replace=tk[:, 48:56],
                                                               in_values=tk[:, 8:8 + NBP], imm_value=-3e9),
                              reads=["tk_imp", "tk_m1"], writes=["tk_s2"])
                        em.op("dve", lambda e: e.max(out=tk[:, 56:64], in_=tk[:, 64:64 + NBP]), reads=["tk_s2"], writes=["tk_m2"])
                        em.op("dve", lambda e: e.tensor_scalar(out=tk[:, 64:64 + NBP], in0=tk[:, 8:8 + NBP], scalar1=tk[:, 63:64], scalar2=None,
                                                               op0=ALU.is_ge), reads=["tk_imp", "tk_m2", "tk_s2"], writes=["tk_s2"])
                        em.op("dve", lambda e: e.tensor_scalar(out=selb[:, 0:NBP], in0=tk[:, 64:64 + NBP], scalar1=-1.0, scalar2=MASK_NEG,
                                                               op0=ALU.add, op1=ALU.mult), reads=["tk_s2"], writes=["selb"])
                        em.op("pe", lambda e: e.transpose(out=pT[0:NBP, 0:128], in_=selb[:, 0:NBP], identity=ident_b[:]),
                              reads=["selb", "ident_b"], writes=["pT"])
                        copy_op("act", selT[e_][0:NBP, t * 128:(t + 1) * 128], pT[0:NBP, 0:128], ["pT"], [("selT", e_, t // 4)])
                    prelude_todo = []
                    if half >= 1:
                        for t in range(8):
                            prelude_tile(0, t)
                        prelude_todo = [(1, t) for t in range(8)]
                    LOOK = 2
                    jobs = []
                    for e_ in range(2):
                        for i in range(4):
                            for tq in range(2):
                                Qc = Qc0 + tq
                                grp = dict(e_=e_, i=i, tq=tq, Qc=Qc)
                                sets = [(0, tq)] if half == 0 else [(0, None), (1, tq)]
                                for si, (set_, r_) in enumerate(sets):
                                    jobs.append(dict(g=grp, br=0, first=(si == 0), set_=set_, r_=r_))
                                for br in (1, 2):
                                    kts = list(range(0, 4 * Qc + 4)) if br == 1 else list(range(max(0, 4 * Qc - 4), 4 * Qc + 4))
                                    for ki, kt in enumerate(kts):
                                        jobs.append(dict(g=grp, br=br, first=(ki == 0), kt=kt))
                                jobs[-1]["last"] = True
                    ecnt = [0, 0]

                    def emitA(jb):
                        g = jb["g"]
                        e_, i, tq, Qc = g["e_"], g["i"], g["tq"], g["Qc"]
                        base = e_ * 64
                        sl = slice(tq * 512, (tq + 1) * 512)
                        pq, pqk = next_pcd()
                        if jb["br"] == 0:
                            set_, r_ = jb["set_"], jb["r_"]
                            mm_group(pq[0:16, :], [(kcmpT[base:base + 64, gp, set_ * 16:(set_ + 1) * 16], QT[base:base + 64, i, sl])],
                                     reads=[("kcmpT", set_), ("QT", i, tq)], writes=[pqk])
                            E, Ek = E16[ecnt[0] % 2], f"E16{ecnt[0] % 2}"
                            ecnt[0] += 1
                            em.op("act", lambda e: e.activation(out=E, in_=pq[0:16, :], func=AF.Exp, scale=0.125), reads=[pqk], writes=[Ek])
                            if r_ is not None:
                                em.op("dve", lambda e: e.tensor_tensor(out=E, in0=E, in1=cmaskc[:, r_, :], op=ALU.mult),
                                      reads=[Ek, "cmaskc"], writes=[Ek])
                        else:
                            br, kt = jb["br"], jb["kt"]
                            KT, kname = (ksT, "ksT") if br == 1 else (kwT, "kwT")
                            Dd = 4 * Qc - kt
                            midx = -Dd if Dd <= 0 else (3 + Dd if br == 2 else None)
                            pairs = [(KT[:, gp, kt * 128:(kt + 1) * 128], QrP[e_][:, i, sl])]
                            rds = [(kname, kt), ("QrT", i, tq)]
                            if midx is not None:
                                pairs.append((ident_b[:], masks[:, midx, :]))
                                rds += ["ident_b", "masks"]
                            if br == 1 and Qc >= 2:
                                pairs.append((expand[:, kt, :], selT[e_][:, sl]))
                                rds += ["expand", ("selT", e_, tq)]
                            mm_group(pq[:], pairs, reads=rds, writes=[pqk])
                            E, Ek = Et[ecnt[1] % 3], f"Et{ecnt[1] % 3}"
                            ecnt[1] += 1
                            em.op("act", lambda e: e.activation(out=E, in_=pq[:], func=AF.Exp, scale=0.125), reads=[pqk], writes=[Ek])
                        jb["E"], jb["Ek"] = E, Ek

                    def emitB(jb):
                        g = jb["g"]
                        e_, i, tq = g["e_"], g["i"], g["tq"]
                        kvh = 2 * gp + e_
                        h = kvh * 4 + i
                        base = e_ * 64
                        E, Ek, br, first = jb["E"], jb["Ek"], jb["br"], jb["first"]
                        if br == 0:
                            set_ = jb["set_"]
                            em.op("pe", [lambda e, s_=s_: e.matmul(
                                pO[0][:, s_ * 65:(s_ + 1) * 65], lhsT=E[:, s_ * 128:(s_ + 1) * 128], rhs=vcmp[set_][:, kvh, :],
                                start=(first and s_ == 0), stop=False, skip_group_check=True) for s_ in range(4)],
                                  reads=[Ek, f"vcmp{set_}"], writes=["pO0"])
                        else:
                            kt = jb["kt"]
                            Vv, vname = (vsA, "vsA") if br == 1 else (vwA, "vwA")
                            em.op("pe", [lambda e, s_=s_: e.matmul(
                                pO[br][:, s_ * 65:(s_ + 1) * 65], lhsT=E[:, s_ * 128:(s_ + 1) * 128], rhs=Vv[:, kt, kvh, :],
                                start=(first and s_ == 0), stop=False, skip_group_check=True) for s_ in range(4)],
                                  reads=[Ek, (vname, kt)], writes=[f"pO{br}"])
                        if not jb.get("last"):
                            return
                        for br_ in range(3):
                            ceng = "dve" if br_ == 2 else "act"
                            copy_op(ceng, osb[:, br_, :], pO[br_][:, 0:260], [f"pO{br_}"], [("osb", br_), "Ct"])
                        for br_ in range(3):
                            em.op("dve", lambda e, br_=br_: e.tensor_scalar(
                                out=d3[:, :, br_], in0=osb[:, br_, :].rearrange("p (s c) -> p s c", c=65)[:, :, 64],
                                scalar1=1e-30, scalar2=None, op0=ALU.max), reads=[("osb", br_)], writes=[("d3", br_)])
                        d3k = [("d3", b_) for b_ in range(3)]
                        em.op("dve", lambda e: e.reciprocal(out=w3[:], in_=d3[:]), reads=d3k, writes=["w3"])
                        em.op("dve", lambda e: e.tensor_tensor(out=w3[:], in0=w3[:], in1=gates[:, tq * 4:(tq + 1) * 4, h * 3:(h + 1) * 3], op=ALU.mult),
                              reads=["w3"] + [("gates", tq * 4 + s_) for s_ in range(4)], writes=["w3"])

                        def oview(br_):
                            return osb[:, br_, :].rearrange("p (s c) -> p s c", c=65)[:, :, 0:64]

                        def wv(br_):
                            return w3[:, :, br_:br_ + 1].to_broadcast([128, 4, 64])
                        em.op("dve", lambda e: e.tensor_tensor(out=ot[0], in0=oview(0), in1=wv(0), op=ALU.mult),
                              reads=[("osb", 0), "w3"], writes=["rt0"])
                        em.op("dve", lambda e: e.tensor_tensor(out=ot[1], in0=oview(1), in1=wv(1), op=ALU.mult),
                              reads=[("osb", 1), "w3"], writes=["rt1"])
                        em.op("dve", lambda e: e.tensor_tensor(out=ot[0], in0=ot[0], in1=ot[1], op=ALU.add),
                              reads=["rt0", "rt1"], writes=["rt0"])
                        em.op("dve", lambda e: e.tensor_tensor(out=ot[1], in0=oview(2), in1=wv(2), op=ALU.mult),
                              reads=[("osb", 2), "w3", "rt0"], writes=["rt1"])
                        em.op("dve", lambda e: e.tensor_tensor(
                            out=o_tm[:, tq * 4:(tq + 1) * 4, i * 128 + base:i * 128 + base + 64], in0=ot[0], in1=ot[1], op=ALU.add),
                              reads=["rt0", "rt1"], writes=[("o_tm", tq, i, e_)])

                    pend = []
                    for ji, jb in enumerate(jobs):
                        if prelude_todo and ji % 4 == 3:
                            prelude_tile(*prelude_todo.pop(0))
                        emitA(jb)
                        pend.append(jb)
                        if len(pend) > LOOK:
                            emitB(pend.pop(0))
                    while pend:
                        emitB(pend.pop(0))
                    for i in range(4):
                        em.op("pe", [lambda e, t=t, i=i: e.transpose(out=pT[:, t * 128:(t + 1) * 128], in_=o_tm[:, t, i * 128:(i + 1) * 128],
                                                                      identity=ident_b[:]) for t in range(8)],
                              reads=[("o_tm", tq, i, e_) for tq in range(2) for e_ in range(2)] + ["ident_b"], writes=["pT"])
                        em.op("dve", lambda e, i=i, gp=gp: e.tensor_tensor(out=mixT[:, 8 + gp * 4 + i, :], in0=pT[:], in1=zT[:, i, :], op=ALU.mult),
                              reads=["pT", ("zT", i, 0), ("zT", i, 1)], writes=[("mixT", 8 + gp * 4 + i)])

                prefetch_w(("ag", l, half), w_in[l][:, AG_OFF:AG_OFF + 256], 256)
                em.barrier()
                cv = Carver()
                cT = cv([128, 8, HALF], F32)
                gluT = cv([128, 30 + HALF], BF16)
                diag = cv([128, CONVW, 128], BF16)
                f5 = [cv([128, 512], F32) for _ in range(3)]
                b5 = [cv([128, 512], BF16) for _ in range(2)]
                mean_t = cv([128, 2, 512], F32)
                rstd_t = cv([128, 2, 512], F32)
                glu_tm = cv([128, 128], F32)
                for c8 in range(8):
                    wbuf, wkey = load_w(w_in[l][:, AG_OFF + c8 * 256:AG_OFF + (c8 + 1) * 256], 256, tag=(("ag", l, half) if c8 == 0 else None))
                    if do_s:
                        sample_cols(wbuf, wkey, 256, SC_AG + 2 * c8)
                    if half == 0:
                        em.op("pool", lambda e: e.memset(gluT[:, 0:30], 0.0), writes=["gluT_h"])
                    else:
                        em.op("pool", lambda e, c8=c8: e.tensor_copy(out=gluT[:, 0:30], in_=gtail[:, c8, :]),
                              reads=[("gtail", c8)], writes=["gluT_h"])
                    for w_ in range(CONVW):
                        em.op("act", lambda e, w_=w_, c8=c8: e.mul(out=diag[:, w_, :], in_=ident_b[:], mul=cw_t[:, c8, w_:w_ + 1]),
                              reads=["ident_b", "cw_t"], writes=[("diag", w_)])
                    for tq in range(2):
                        pa, pak = next_ps()
                        mm_group(pa[:], [(wbuf[:, k, 0:128], uT[:, k, tq * 512:(tq + 1) * 512]) for k in range(KC)],
                                 reads=uT_all + [wkey], writes=[pak])
                        pg, pgk = next_ps()
                        mm_group(pg[:], [(wbuf[:, k, 128:256], uT[:, k, tq * 512:(tq + 1) * 512]) for k in range(KC)],
                                 reads=uT_all + [wkey], writes=[pgk])
                        em.op("act", lambda e, pg=pg: e.activation(out=f5[0], in_=pg[:], func=AF.Sigmoid), reads=[pgk], writes=["f50"])
                        em.op("dve", lambda e, pa=pa, tq=tq: e.tensor_tensor(out=gluT[:, 30 + tq * 512:30 + (tq + 1) * 512], in0=pa[:], in1=f5[0],
                                                                            op=ALU.mult), reads=[pak, "f50"], writes=[("gluT", tq)])
                    if half == 0 and NH > 1:
                        em.op("pool", lambda e, c8=c8: e.tensor_copy(out=gtail[:, c8, :], in_=gluT[:, HALF:HALF + 30]),
                              reads=[("gluT", 1)], writes=[("gtail", c8)])
                    if half == NH - 1:
                        pq, pqk = next_pcd()
                        mm_group(pq[:, 0:256], [(uT[:, k, HALF - 128:HALF], wbuf[:, k, 0:256]) for k in range(KC)],
                                 reads=uT_all + [wkey], writes=[pqk])
                        em.op("act", lambda e, pq=pq: e.activation(out=f5[1][:, 0:128], in_=pq[:, 128:256], func=AF.Sigmoid),
                              reads=[pqk], writes=["f51"])
                        em.op("dve", lambda e, pq=pq: e.tensor_tensor(out=glu_tm, in0=pq[:, 0:128], in1=f5[1][:, 0:128], op=ALU.mult),
                              reads=[pqk, "f51"], writes=["glu_tm"])
                        em.dma("sp", lambda e, c8=c8, l=l: e.dma_start(out=convp[l, :, c8 * 128:(c8 + 1) * 128], in_=glu_tm[98:128, :]),
                               reads=["glu_tm"])
                    for tq in range(2):
                        pq, pqk = next_pcd()
                        mm_group(pq[:], [(diag[:, w_, :], gluT[:, tq * 512 + w_: tq * 512 + w_ + 512]) for w_ in range(CONVW)],
                                 reads=[("diag", w_) for w_ in range(CONVW)] + ["gluT_h", ("gluT", 0), ("gluT", 1)], writes=[pqk])
                        em.op("act", lambda e, pq=pq, c8=c8, tq=tq: e.activation(out=cT[:, c8, tq * 512:(tq + 1) * 512], in_=pq[:], func=AF.Identity,
                                                                                 bias=ccol_t[:, c8:c8 + 1]),
                              reads=[pqk, "ccol_t"], writes=[("cT", c8, tq)])
                for tq in range(2):
                    sl = slice(tq * 512, (tq + 1) * 512)
                    p1, p1k = next_ps()
                    mm_group(p1[:], [(ones_f[:], cT[:, c8, sl]) for c8 in range(8)],
                             reads=["ones_f"] + [("cT", c8, tq) for c8 in range(8)], writes=[p1k])
                    p2, p2k = next_pcd()
                    for c8 in range(8):
                        fb, fbk = f5[c8 % 2], f"f5{c8 % 2}"
                        em.op("act", lambda e, fb=fb, c8=c8, sl=sl: e.activation(out=fb, in_=cT[:, c8, sl], func=AF.Square),
                              reads=[("cT", c8, tq)], writes=[fbk])
                        em.op("pe", lambda e, fb=fb, c8=c8, p2=p2: e.matmul(p2[:], lhsT=ones_f[:], rhs=fb, start=(c8 == 0), stop=(c8 == 7)),
                              reads=[fbk, "ones_f"], writes=[p2k])
                    em.op("act", lambda e, p1=p1, tq=tq: e.mul(out=mean_t[:, tq, :], in_=p1[:], mul=1.0 / CC), reads=[p1k], writes=[("mean", tq)])
                    em.op("dve", lambda e, tq=tq: e.tensor_tensor(out=f5[2], in0=mean_t[:, tq, :], in1=mean_t[:, tq, :], op=ALU.mult),
                          reads=[("mean", tq)], writes=["f52"])
                    em.op("dve", lambda e, p2=p2, tq=tq: e.scalar_tensor_tensor(out=rstd_t[:, tq, :], in0=p2[:], scalar=1.0 / CC, in1=f5[2],
                                                                               op0=ALU.mult, op1=ALU.subtract),
                          reads=[p2k, "f52"], writes=[("rstd", tq)])
                    em.op("act", lambda e, tq=tq: e.activation(out=rstd_t[:, tq, :], in_=rstd_t[:, tq, :], func=AF.Sqrt, bias=eps_t[:, 0:1]),
                          reads=[("rstd", tq), "eps_t"], writes=[("rstd", tq)])
                    em.op("dve", lambda e, tq=tq: e.reciprocal(out=rstd_t[:, tq, :], in_=rstd_t[:, tq, :]), reads=[("rstd", tq)], writes=[("rstd", tq)])
                for cp in range(4):
                    wbuf, wkey = load_w(w_in[l][:, ZC_OFF + cp * 256:ZC_OFF + (cp + 1) * 256], 256)
                    if do_s:
                        sample_cols(wbuf, wkey, 256, SC_ZC + 2 * cp)
                    for s_ in range(2):
                        c8 = cp * 2 + s_
                        for tq in range(2):
                            sl = slice(tq * 512, (tq + 1) * 512)
                            pz, pzk = next_ps()
                            mm_group(pz[:], [(wbuf[:, k, s_ * 128:(s_ + 1) * 128], uT[:, k, sl]) for k in range(KC)],
                                     reads=uT_all + [wkey], writes=[pzk])
                            em.op("act", lambda e, pz=pz: e.activation(out=b5[0], in_=pz[:], func=AF.Silu), reads=[pzk], writes=["b50"])
                            em.op("dve", lambda e, c8=c8, tq=tq, sl=sl: e.tensor_tensor(out=f5[0], in0=cT[:, c8, sl], in1=mean_t[:, tq, :], op=ALU.subtract),
                                  reads=[("cT", c8, tq), ("mean", tq)], writes=["f50"])
                            em.op("dve", lambda e, tq=tq: e.tensor_tensor(out=f5[0], in0=f5[0], in1=rstd_t[:, tq, :], op=ALU.mult),
                                  reads=["f50", ("rstd", tq)], writes=["f50"])
                            em.op("act", lambda e, c8=c8: e.activation(out=b5[1], in_=f5[0], func=AF.Silu, bias=ccol_t[:, 16 + c8:17 + c8],
                                                                        scale=ccol_t[:, 8 + c8:9 + c8]), reads=["f50", "ccol_t"], writes=["b51"])
                            em.op("pool", lambda e, c8=c8, sl=sl: e.tensor_tensor(out=mixT[:, c8, sl], in0=b5[1], in1=b5[0], op=ALU.mult),
                                  reads=["b50", "b51"], writes=[("mixT", c8)])

                prefetch_w(("wo", l, half), w_out[l][:, 0:256], 256)
                if do_s:
                    sample_layer(l)

                em.barrier()
                cv = Carver()
                ystage = [cv([128, 256], F32) for _ in range(2)]
                junk = cv([128, 256], F32)
                ssp = cv([128, 8, 8], F32)
                yrs = [cv([128, D], F32) for _ in range(2)]
                xt6s = [cv([128, D], F32) for _ in range(2)]
                gpc = cv([128, D], F32)
                rs6 = cv([128, 8], F32)
                em.dma("sp", lambda e, l=l: e.dma_start(out=gpc, in_=gpost[l].partition_broadcast(128)), writes=["gpc"])
                mix_all = [("mixT", k) for k in range(KC)]
                for cj in range(8):
                    wbuf, wkey = load_w(w_out[l][:, cj * 256:(cj + 1) * 256], 256, tag=(("wo", l, half) if cj == 0 else None))
                    if do_s:
                        for s_ in range(2):
                            mm_group(pD[:, s_:s_ + 1], [(wbuf[:, k, s_ * 128:(s_ + 1) * 128], mixTs[:, k:k + 1]) for k in range(KC)],
                                     reads=[wkey, "mixTs"], writes=["pD"])
                        em.op("dve", lambda e, cj=cj: e.tensor_copy(out=ysraw[:, 2 * cj:2 * cj + 2], in_=pD[:, 0:2]), reads=["pD"], writes=["ysraw"])
                    for t in range(8):
                        pp, pk = next_ps()
                        mm_group(pp[:, 0:256], [(mixT[:, k, t * 128:(t + 1) * 128], wbuf[:, k, 0:256]) for k in range(KC)],
                                 reads=mix_all + [wkey], writes=[pk])
                        sb_ = (cj * 8 + t) % 2
                        em.op("dve", lambda e, pp=pp, sb_=sb_: e.tensor_copy(out=ystage[sb_], in_=pp[:, 0:256]), reads=[pk], writes=[f"ys{sb_}"])
                        em.op("act", lambda e, sb_=sb_, t=t, cj=cj: e.activation(out=junk, in_=ystage[sb_], func=AF.Square,
                                                                                accum_out=ssp[:, t, cj:cj + 1]), reads=[f"ys{sb_}"], writes=["junk", ("ssp", t)])
                        em.dma("sp", lambda e, sb_=sb_, t=t, cj=cj: e.dma_start(out=yraw[t * 128:(t + 1) * 128, cj * 256:(cj + 1) * 256],
                                                                                in_=ystage[sb_]), reads=[f"ys{sb_}"], writes=[("yraw", t)])
                for t in range(8):
                    T = T0 + t
                    yr, yrk, xt6, xtk = yrs[t % 2], f"yr{t % 2}", xt6s[t % 2], f"xt6{t % 2}"
                    em.dma("sp", lambda e, t=t, yr=yr: e.dma_start(out=yr, in_=yraw[t * 128:(t + 1) * 128, :]), reads=[("yraw", t)], writes=[yrk])
                    em.dma("sp", lambda e, T=T, xt6=xt6: e.dma_start(out=xt6, in_=src[T * 128:(T + 1) * 128, :]), reads=[("resid", T)], writes=[xtk])
                    em.op("dve", lambda e, t=t: e.tensor_reduce(out=rs6[:, t:t + 1], in_=ssp[:, t, :], axis=AX.X, op=ALU.add),
                          reads=[("ssp", t)], writes=[("rs6", t)])
                    em.op("act", lambda e, t=t: e.activation(out=rs6[:, t:t + 1], in_=rs6[:, t:t + 1], func=AF.Sqrt, bias=eps_t[:, 0:1],
                                                             scale=1.0 / D), reads=[("rs6", t), "eps_t"], writes=[("rs6", t)])
                    em.op("dve", lambda e, t=t: e.reciprocal(out=rs6[:, t:t + 1], in_=rs6[:, t:t + 1]), reads=[("rs6", t)], writes=[("rs6", t)])
                    em.op("dve", lambda e, t=t, yr=yr: e.scalar_tensor_tensor(out=yr, in0=yr, scalar=rs6[:, t:t + 1], in1=gpc, op0=ALU.mult, op1=ALU.mult),
                          reads=[yrk, ("rs6", t), "gpc"], writes=[yrk])
                    em.op("dve", lambda e, yr=yr, xt6=xt6: e.tensor_tensor(out=yr, in0=yr, in1=xt6, op=ALU.add), reads=[yrk, xtk], writes=[yrk])
                    dst = yp if last else resid
                    em.dma("sp", lambda e, T=T, dst=dst, yr=yr: e.dma_start(out=dst[T * 128:(T + 1) * 128, :], in_=yr), reads=[yrk],
                           writes=[("resid", T)])
                if do_s:
                    sample_post(l, last)
        em.finish()
    return nc


def prep_shared(inp, DEPTH, SEQ, PAST):
    f = np.float32
    L = DEPTH
    sh = {}
    sh["w_in"] = np.ascontiguousarray(np.asarray(inp["w_in"], f)[:, :, win_perm()])
    sh["w_out"] = np.ascontiguousarray(np.asarray(inp["w_out"], f)[:, wout_perm(), :])
    npre = np.asarray(inp["norm_pre"], f)
    npost = np.asarray(inp["norm_post"], f)
    sh["gpre"] = npre
    sh["gpost"] = npost
    sh["gcols"] = np.ascontiguousarray(np.concatenate([npre.reshape(L, 16, 128).transpose(0, 2, 1),
                                                       npost.reshape(L, 16, 128).transpose(0, 2, 1)], axis=2))
    sh["cwT"] = np.ascontiguousarray(np.asarray(inp["conv_w"], f).reshape(L, CONVW, 8, 128).transpose(0, 3, 2, 1))
    cols = [np.asarray(inp[k], f).reshape(L, 8, 128).transpose(0, 2, 1) for k in ("conv_b", "conv_ln_g", "conv_ln_b")]
    sh["ccols"] = np.ascontiguousarray(np.concatenate(cols, axis=2))
    pos = np.asarray(inp["cmp_pos"], f)
    sh["pos_tm"] = np.ascontiguousarray(np.concatenate([pos, pos], axis=2).transpose(0, 2, 1, 3))
    p32 = pos.reshape(L, 2, 32, 2, 64)
    pdp = np.concatenate([p32] * 4, axis=2)
    sh["pos_dp"] = np.ascontiguousarray(pdp.transpose(0, 2, 3, 1, 4))
    sh["w1"] = np.asarray(inp["cmp_w1"], f)
    sh["w2"] = np.asarray(inp["cmp_w2"], f)
    b1 = np.asarray(inp["cmp_b1"], f)
    b2 = np.asarray(inp["cmp_b2"], f)
    sh["mcols"] = np.ascontiguousarray(np.stack([b1[:, 0], b1[:, 1], np.concatenate([b2[:, 0], b2[:, 0]], axis=1),
                                                 np.concatenate([b2[:, 1], b2[:, 1]], axis=1)], axis=2))
    sh["b2row"] = np.ascontiguousarray(np.concatenate([b2[:, 0], b2[:, 1]], axis=1).reshape(L, 1, 128))
    gb = np.asarray(inp["gate_b"], f)
    sh["gateb"] = gb.reshape(L, 1, 48)
    sh["gatebc"] = gb.reshape(L, 48, 1)
    for k, v in host_consts(SEQ, PAST).items():
        sh["c_" + k] = v
    return sh


def prep_core(inp, c, nb, DEPTH, NPOOL):
    f = np.float32
    m = {}
    m["xp"] = np.asarray(inp["x_prompt"], f)[c % nb]
    m["xs"] = np.asarray(inp["x_sample"], f)[c]
    m["pool"] = np.asarray(inp["cache_kv_pages"], f).reshape(DEPTH * NPOOL * 64, 2048)
    m["winc"] = np.asarray(inp["cache_win"], f)[:, c].reshape(DEPTH, 512, 512)
    sc = np.asarray(inp["state_conv"], f)[:, c]
    m["convN"] = np.ascontiguousarray(sc)
    m["convT"] = np.ascontiguousarray(sc.reshape(DEPTH, 30, 8, 128).transpose(0, 3, 2, 1))
    m["ptab"] = np.ascontiguousarray(np.asarray(inp["page_table"], np.int32)[c:c + 1])
    return m


_NC_CACHE = {}


def kernel(**inputs):
    xpr = np.asarray(inputs["x_prompt"])
    DEPTH = int(np.asarray(inputs["w_in"]).shape[0])
    B, SEQ = int(xpr.shape[0]), int(xpr.shape[1])
    DB = int(np.asarray(inputs["x_sample"]).shape[0])
    NPOOL = int(np.asarray(inputs["cache_kv_pages"]).shape[1])
    PAST = int(np.asarray(inputs["page_table"]).shape[1]) * 128
    key = (DEPTH, SEQ, PAST, NPOOL)
    if key not in _NC_CACHE:
        _NC_CACHE[key] = build(*key)
    nc = _NC_CACHE[key]
    ncores = 8
    assert DB == ncores
    shared = prep_shared(inputs, DEPTH, SEQ, PAST)
    in_maps = []
    for c in range(ncores):
        m = dict(shared)
        m.update(prep_core(inputs, c, B, DEPTH, NPOOL))
        in_maps.append(m)
    res = run_bass_kernel_spmd(nc, in_maps, core_ids=list(range(ncores)))
    R = res.results
    f = np.float32
    y_prompt = np.stack([R[b]["yp"] for b in range(B)]).astype(f)
    y_sample = np.stack([R[c]["ys"] for c in range(DB)]).astype(f)
    kv_prompt = np.stack([R[b]["kvp"] for b in range(B)], axis=1).reshape(DEPTH, B, SEQ, 4, NKV, DH).astype(f)
    kv_sample = np.stack([R[c]["kvs"] for c in range(DB)], axis=1).reshape(DEPTH, DB, 1, 4, NKV, DH).astype(f)
    win_prompt = np.stack([R[b]["winp"] for b in range(B)], axis=1).reshape(DEPTH, B, 512, 2, NKV, DH).astype(f)
    win_sample = np.stack([R[c]["wins"] for c in range(DB)], axis=1).reshape(DEPTH, DB, 512, 2, NKV, DH).astype(f)
    conv_prompt = np.stack([R[b]["convp"] for b in range(B)], axis=1).astype(f)
    conv_sample = np.stack([R[c]["convs"] for c in range(DB)], axis=1).astype(f)
    return (y_prompt, y_sample, kv_prompt, kv_sample, win_prompt, win_sample, conv_prompt, conv_sample)
```
